# Optimizing a Trainium2 kernel written in Bass

```python
import jax, jax.numpy as jnp
from jax import lax
import numpy as np

D_MODEL = 1024
BATCH = 2
SEQ = 8192
DEPTH = 2

GRID_W = 64
CTX_LEN = 256
EXPAND = 2
D_INNER = EXPAND * D_MODEL
HG_HEAD_DIM = 128
HG_HEADS = D_INNER // HG_HEAD_DIM
HG_STREAMS = 5
FT_GROUPS = 8
FT_GROUP_DIM = D_INNER // FT_GROUPS
CHUNK = 64
N_MIXERS = 2
N_HGRN = (DEPTH + 1) // 2
N_FOURIER = DEPTH // 2
EPS = 1e-6

kernel_name = "hgrn2_fnet_interleaved_prefix_dit"


def rms_norm(x, g):
    xf = x.astype(jnp.float32)
    var = jnp.mean(xf * xf, axis=-1, keepdims=True)
    return (xf * lax.rsqrt(var + EPS)).astype(x.dtype) * g


def adaln(cvec, w, b):
    m = jax.nn.silu(cvec) @ w + b
    return jnp.split(m, 3, axis=-1)


def chunk_scan(q, k, v, g, s0):
    B, H, N, Dk = q.shape
    Dv = v.shape[-1]
    nc = N // CHUNK

    def to_chunks(a):
        a = a.astype(jnp.float32).reshape(B, H, nc, CHUNK, a.shape[-1])
        return jnp.moveaxis(a, 2, 0)

    causal = jnp.tril(jnp.ones((CHUNK, CHUNK), dtype=bool))[:, :, None]

    def step(S, inp):
        qc, kc, vc, gc = inp
        b = jnp.cumsum(gc, axis=-2)
        diff = b[..., :, None, :] - b[..., None, :, :]
        decay = jnp.exp(jnp.where(causal, diff, -jnp.inf))
        scores = jnp.einsum('bhtc,bhtsc,bhsc->bhts', qc, decay, kc)
        o = (jnp.einsum('bhts,bhsv->bhtv', scores, vc)
             + jnp.einsum('bhtc,bhcv->bhtv', qc * jnp.exp(b), S))
        b_last = b[..., -1:, :]
        S_new = (jnp.exp(b_last)[..., 0, :, None] * S
                 + jnp.einsum('bhsc,bhsv->bhcv', kc * jnp.exp(b_last - b), vc))
        return S_new, o

    S_fin, o = lax.scan(step, s0, (to_chunks(q), to_chunks(k), to_chunks(v), to_chunks(g)))
    o = jnp.moveaxis(o, 0, 2).reshape(B, H, N, Dv)
    return o, S_fin


def hgrn_lower_bound(logits, j):
    p = jax.nn.softmax(logits.astype(jnp.float32), axis=0)
    return jnp.cumsum(p, axis=0)[j]


def hgrn_project(h, w_in, lb):
    B, N, _ = h.shape
    p = (h @ w_in).reshape(B, N, HG_STREAMS, HG_HEADS, HG_HEAD_DIM)
    p = jnp.transpose(p, (2, 0, 3, 1, 4))
    q = jax.nn.silu(p[0])
    lbh = lb.reshape(2, 1, HG_HEADS, 1, HG_HEAD_DIM)
    f = lbh + (1.0 - lbh) * jax.nn.sigmoid(p[1:3].astype(jnp.float32))
    return q, 1.0 - f, jnp.log(f), p[3], p[4]


def hgrn_readout(o, z, norm_g, w_out):
    y = rms_norm(o, norm_g).astype(z.dtype) * jax.nn.silu(z)
    B, H, N, d = y.shape
    return jnp.transpose(y, (0, 2, 1, 3)).reshape(B, N, H * d) @ w_out


def hgrn_mixer(h_ctx, h_lat, w_in, lb, norm_g, w_out, ctx_out):
    qc, kc, gc, vc, zc = hgrn_project(h_ctx, w_in, lb)
    ql, kl, gl, vl, zl = hgrn_project(h_lat, w_in, lb)
    B = h_lat.shape[0]
    s0 = jnp.zeros((B, HG_HEADS, HG_HEAD_DIM, HG_HEAD_DIM), jnp.float32)
    flip = lambda a: jnp.flip(a, axis=2)
    o_cf, s_cf = chunk_scan(qc, kc[0], vc, gc[0], s0)
    o_cb, s_cb = chunk_scan(flip(qc), flip(kc[1]), flip(vc), flip(gc[1]), s0)
    o_lf, _ = chunk_scan(ql, kl[0], vl, gl[0], s_cf)
    o_lb, _ = chunk_scan(flip(ql), flip(kl[1]), flip(vl), flip(gl[1]), s_cb)
    y_lat = hgrn_readout(o_lf + flip(o_lb), zl, norm_g, w_out)
    y_ctx = hgrn_readout(o_cf + flip(o_cb), zc, norm_g, w_out) if ctx_out else None
    return y_lat, y_ctx


def fourier_mixer(h, w_in, w_out):
    B, N, _ = h.shape
    u, z = jnp.split(h @ w_in, 2, axis=-1)
    ug = u.astype(jnp.float32).reshape(B, N, FT_GROUPS, FT_GROUP_DIM)
    y = jnp.fft.fft2(ug, axes=(1, 3), norm='ortho').real.reshape(B, N, D_INNER)
    return (y.astype(h.dtype) * jax.nn.silu(z)) @ w_out


def setup_inputs(seed: int = 0) -> dict:
    key = jax.random.key(seed)
    ks = jax.random.split(key, 14)
    D, E = D_MODEL, D_INNER
    nrm = jax.random.normal
    return {
        'x': nrm(ks[0], (BATCH, SEQ, D), jnp.float32),
        'c': nrm(ks[1], (BATCH, D), jnp.float32),
        'ctx': nrm(ks[2], (BATCH, CTX_LEN, D), jnp.float32),
        'c_ctx': nrm(ks[3], (D,), jnp.float32),
        'ada_w': nrm(ks[4], (DEPTH, D, 3 * D), jnp.float32) * (0.5 * D ** -0.5),
        'ada_b': nrm(ks[5], (DEPTH, 3 * D), jnp.float32) * 0.01,
        'norm_g': 1.0 + 0.05 * nrm(ks[6], (DEPTH, D), jnp.float32),
        'hg_w_in': nrm(ks[7], (N_HGRN, D, HG_STREAMS * E), jnp.float32) * D ** -0.5,
        'hg_lb_logits': 0.5 * nrm(ks[8], (N_HGRN + 1, 2, E), jnp.float32),
        'hg_norm_g': 1.0 + 0.05 * nrm(ks[9], (N_HGRN, HG_HEAD_DIM), jnp.float32),
        'hg_w_out': nrm(ks[10], (N_HGRN, E, D), jnp.float32) * E ** -0.5,
        'ft_w_in': nrm(ks[11], (N_FOURIER, D, 2 * E), jnp.float32) * D ** -0.5,
        'ft_w_out': nrm(ks[12], (N_FOURIER, E, D), jnp.float32) * E ** -0.5,
        'final_g': 1.0 + 0.05 * nrm(ks[13], (D,), jnp.float32),
    }


def reference(x, c, ctx, c_ctx, ada_w, ada_b, norm_g, hg_w_in, hg_lb_logits, hg_norm_g,
              hg_w_out, ft_w_in, ft_w_out, final_g):
    x_lat, x_ctx = x, ctx
    for i in range(DEPTH):
        ctx_out = i < DEPTH - 1
        j = i // N_MIXERS
        sh_l, sc_l, gt_l = adaln(c, ada_w[i], ada_b[i])
        h_lat = rms_norm(x_lat, norm_g[i]) * (1.0 + sc_l[:, None]) + sh_l[:, None]
        if i % N_MIXERS == 0:
            sh_c, sc_c, gt_c = adaln(c_ctx, ada_w[i], ada_b[i])
            h_ctx = rms_norm(x_ctx, norm_g[i]) * (1.0 + sc_c) + sh_c
            lb = hgrn_lower_bound(hg_lb_logits, j)
            y_lat, y_ctx = hgrn_mixer(h_ctx, h_lat, hg_w_in[j], lb, hg_norm_g[j], hg_w_out[j], ctx_out)
        else:
            y_lat = fourier_mixer(h_lat, ft_w_in[j], ft_w_out[j])
            if ctx_out:
                sh_c, sc_c, gt_c = adaln(c_ctx, ada_w[i], ada_b[i])
                h_ctx = rms_norm(x_ctx, norm_g[i]) * (1.0 + sc_c) + sh_c
                y_ctx = fourier_mixer(h_ctx, ft_w_in[j], ft_w_out[j])
        x_lat = x_lat + gt_l[:, None] * y_lat
        if ctx_out:
            x_ctx = x_ctx + gt_c * y_ctx
    return rms_norm(x_lat, final_g)
```

```python
from contextlib import ExitStack
import os
import numpy as np
import ml_dtypes
import concourse.bass as bass
import concourse.mybir as mybir
from concourse.bass_utils import run_bass_kernel_spmd

F32 = mybir.dt.float32
BF16 = mybir.dt.bfloat16
AF = mybir.ActivationFunctionType
ALU = mybir.AluOpType
NPBF = ml_dtypes.bfloat16

D = 1024
E = 2048
SEQ = 8192
CTX = 256
NCORES = 8
EPS = 1e-6

SAME_ENGINE_SYNC = True
N_DMA_SEMS = 24


class Reg:
    __slots__ = ("name", "w", "r")

    def __init__(self, name=""):
        self.name = name
        self.w = None
        self.r = []


class Prog:
    ENGS = ("pe", "act", "dve", "pool", "sp")

    def __init__(self, nc):
        self.nc = nc
        self.q = {e: [] for e in self.ENGS}
        self.cnt = {e: 0 for e in self.ENGS}
        self.seen = {e: {} for e in self.ENGS}
        self.pend = {e: ([], []) for e in self.ENGS}
        self.dma_cnt = {}
        self.dma_key = {}
        self.stack = ExitStack()
        self.n_ops = 0

    def sb(self, name, shape, dt):
        return self.stack.enter_context(self.nc.sbuf_tensor("sb_" + name, list(shape), dt))

    def ps(self, name, shape, dt):
        return self.stack.enter_context(self.nc.psum_tensor("ps_" + name, list(shape), dt))

    def _waits(self, eng, reads, writes):
        need = {}

        def add(tok):
            if tok is None:
                return
            s, v = tok
            if need.get(s, 0) < v:
                need[s] = v

        for r in reads:
            add(r.w)
        for w in writes:
            add(w.w)
            for t in w.r:
                add(t)
        waits = []
        for s, v in need.items():
            if s == eng and not SAME_ENGINE_SYNC:
                continue
            if self.seen[eng].get(s, 0) >= v:
                continue
            self.seen[eng][s] = v
            waits.append((s, v))
        return waits

    def op(self, eng, fn, reads=(), writes=(), inc=True):
        reads = list(reads)
        writes = list(writes)
        waits = self._waits(eng, reads, writes)
        pr, pw = self.pend[eng]
        pr.extend(reads)
        pw.extend(writes)
        tok = None
        if inc:
            self.cnt[eng] += 1
            tok = (eng, self.cnt[eng])
            for r in pr:
                r.r.append(tok)
            for w in pw:
                w.w = tok
                w.r = []
            self.pend[eng] = ([], [])
        self.q[eng].append((waits, fn, tok, 1))
        self.n_ops += 1

    def dma(self, eng, out, in_, reads=(), writes=(), **kw):
        reads = list(reads)
        writes = list(writes)
        waits = self._waits(eng, reads, writes)
        key = self.dma_key.get(id(writes[0]))
        if key is None:
            key = "dma%d" % len(self.dma_key)
            self.dma_key[id(writes[0])] = key
            self.dma_cnt[key] = 0
        self.dma_cnt[key] += 16
        tok = (key, self.dma_cnt[key])
        for r in reads:
            r.r.append(tok)
        for w in writes:
            w.w = tok
            w.r = []
        self.q[eng].append((waits, lambda e: e.dma_start(out=out, in_=in_, **kw), tok, 16))
        self.n_ops += 1

    def wait(self, eng, regs):
        waits = self._waits(eng, list(regs), list(regs))
        self.q[eng].append((waits, None, None, 0))

    def emit(self):
        nc = self.nc
        names = ["pe", "act", "dve", "pool"] + list(self.dma_cnt.keys())
        sems = {n: self.stack.enter_context(nc.semaphore("s_" + n)) for n in names}
        block = self.stack.enter_context(nc.Block())
        attr = {"pe": "tensor", "act": "scalar", "dve": "vector", "pool": "gpsimd", "sp": "sync"}
        for eng in self.ENGS:
            q = self.q[eng]

            def body(e, q=q):
                for waits, fn, tok, amt in q:
                    for s, v in waits:
                        e.wait_ge(sems[s], v)
                    if fn is None:
                        continue
                    ins = fn(e)
                    if tok is not None:
                        ins.then_inc(sems[tok[0]], amt)

            getattr(block, attr[eng])(body)

    def close(self):
        self.stack.close()


def build_tok(ntok, nctx, has_outproj, has_normmod, has_final):
    nc = bass.Bass("TRN2", target_bir_lowering=False)
    P = Prog(nc)
    NV = 2 if nctx else 1
    ntot = ntok + nctx
    dr = {}

    def din(name, shape, dt):
        dr[name] = nc.dram_tensor(name, list(shape), dt, kind="ExternalInput").ap()
        return dr[name]

    def dout(name, shape, dt):
        dr[name] = nc.dram_tensor(name, list(shape), dt, kind="ExternalOutput").ap()
        return dr[name]

    x_d = din("x", [ntok, D], F32)
    cv_d = din("cvec", [128, 8, NV], F32)
    id_d = din("ident", [128, 128], BF16)
    if has_outproj:
        y_d = din("y", [ntok, E], BF16)
        w_d = din("w_out", [E, D], F32)
        awg_d = din("aw_g", [128, 8, D], F32)
        wsc_d = din("wsc", [128, 1], F32)
        abg_d = din("ab_g", [1, D], F32)
    if has_normmod:
        awm_d = din("aw_m", [128, 8, 2 * D], F32)
        abm_d = din("ab_m", [128, 16], F32)
        ng_d = din("ng", [128, 8], F32)
        hT_d = dout("hT", [8, 128, ntot], BF16)
    if has_final:
        fg_d = din("fg", [1, D], F32)
    if nctx:
        xc_d = din("xc", [nctx, D], F32)
    if has_outproj or has_final:
        xo_d = dout("xo", [ntok, D], F32)

    ident = P.sb("ident_sb", [128, 128], BF16)
    cvec = P.sb("cvec_sb", [128, 8, NV], F32)
    scv = P.sb("scv_sb", [128, 8, NV], F32)
    zeros = P.sb("zeros_sb", [128, 128], F32)
    ones1 = P.sb("ones1_sb", [1, 128], F32)
    epst = P.sb("eps_sb", [128, 1], F32)
    stage = P.sb("stage_sb", [128, 8, D], F32)
    r_ident, r_cvec, r_scv, r_zeros, r_ones, r_eps = (Reg() for _ in range(6))
    r_stage = [Reg() for _ in range(8)]

    ps = [P.ps("psb%d" % i, [128, 512], F32) for i in range(4)]
    r_ps = [Reg() for _ in range(4)]
    pst = [P.ps("pst%d" % i, [128, 1024], BF16) for i in range(2)]
    r_pst = [Reg() for _ in range(2)]

    P.dma("sp", ident[:], id_d, writes=[r_ident])
    P.dma("sp", cvec[:], cv_d, writes=[r_cvec])
    P.op("dve", lambda e: e.memset(zeros[:], 0.0), writes=[r_zeros])
    P.op("dve", lambda e: e.memset(ones1[:], 1.0), writes=[r_ones])
    P.op("dve", lambda e: e.memset(epst[:], EPS), writes=[r_eps])
    P.op("act", lambda e: e.activation(out=scv[:], in_=cvec[:], func=AF.Silu), reads=[r_cvec], writes=[r_scv])

    if has_outproj:
        gtb = P.sb("gtb_sb", [128, D], F32)
        r_gtb = Reg()
        scb = P.sb("scb_sb", [128, 8, 128], F32)
        r_scb = Reg()
        abg = P.sb("abg_sb", [1, D], F32)
        r_abg = Reg()
        P.dma("sp", abg[:], abg_d, writes=[r_abg])
        for kc in range(8):
            P.op("act", lambda e, kc=kc: e.activation(out=scb[:, kc, :], in_=zeros[:], func=AF.Identity,
                                                       bias=scv[:, kc, 0:1], scale=1.0),
                 reads=[r_zeros, r_scv], writes=[r_scb], inc=(kc == 7))
        for kc in range(8):
            P.dma("sp", stage[:, kc, :], awg_d[:, kc, :], writes=[r_stage[kc]])
        for hf in range(2):
            for kc in range(8):
                P.op("pe", lambda e, kc=kc, hf=hf: e.matmul(ps[hf][:], lhsT=scb[:, kc, :],
                                                            rhs=stage[:, kc, hf * 512:(hf + 1) * 512],
                                                            start=(kc == 0), stop=False),
                     reads=[r_scb, r_stage[kc]], writes=[r_ps[hf]], inc=False)
            P.op("pe", lambda e, hf=hf: e.matmul(ps[hf][:], lhsT=ones1[:], rhs=abg[:, hf * 512:(hf + 1) * 512],
                                                 start=False, stop=True),
                 reads=[r_ones, r_abg], writes=[r_ps[hf]])
            P.op("dve", lambda e, hf=hf: e.tensor_copy(out=gtb[:, hf * 512:(hf + 1) * 512], in_=ps[hf][:]),
                 reads=[r_ps[hf]], writes=[r_gtb])

    if has_normmod:
        ngc = P.sb("ngc_sb", [128, 8], F32)
        abm = P.sb("abm_sb", [128, 16], F32)
        mcol = P.sb("mcol_sb", [128, 16, NV], F32)
        acol = P.sb("acol_sb", [128, 8, NV], F32)
        r_ngc, r_abm, r_mcol, r_acol = Reg(), Reg(), Reg(), Reg()
        P.dma("sp", ngc[:], ng_d, writes=[r_ngc])
        P.dma("sp", abm[:], abm_d, writes=[r_abm])
        pcol = ps[2]
        for half in range(2):
            for kc in range(8):
                P.dma("sp", stage[:, kc, :], awm_d[:, kc, half * D:(half + 1) * D], writes=[r_stage[kc]])
            for fc in range(8):
                for kc in range(8):
                    P.op("pe", lambda e, kc=kc, fc=fc, half=half: e.matmul(
                        pcol[:, (half * 8 + fc) * NV:(half * 8 + fc + 1) * NV],
                        lhsT=stage[:, kc, fc * 128:(fc + 1) * 128], rhs=scv[:, kc, :],
                        start=(kc == 0), stop=(kc == 7)),
                         reads=[r_stage[kc], r_scv], writes=[r_ps[2]], inc=(kc == 7 and fc == 7))
        for v in range(NV):
            P.op("dve", lambda e, v=v: e.tensor_tensor(
                out=mcol[:, :, v], in0=pcol[:, 0:16 * NV].rearrange("p (f v) -> p f v", v=NV)[:, :, v],
                in1=abm[:], op=ALU.add), reads=[r_ps[2], r_abm], writes=[r_mcol])
            P.op("dve", lambda e, v=v: e.scalar_tensor_tensor(
                out=acol[:, :, v], in0=mcol[:, 8:16, v], scalar=1.0, in1=ngc[:], op0=ALU.add, op1=ALU.mult),
                 reads=[r_mcol, r_ngc], writes=[r_acol])

    if has_final:
        fgb = P.sb("fgb_sb", [128, D], F32)
        fgr = P.sb("fgr_sb", [1, D], F32)
        r_fgb, r_fgr = Reg(), Reg()
        P.dma("sp", fgr[:], fg_d, writes=[r_fgr])
        for hf in range(2):
            P.op("pe", lambda e, hf=hf: e.matmul(ps[hf][:], lhsT=ones1[:], rhs=fgr[:, hf * 512:(hf + 1) * 512],
                                                 start=True, stop=True),
                 reads=[r_ones, r_fgr], writes=[r_ps[hf]])
            P.op("dve", lambda e, hf=hf: e.tensor_copy(out=fgb[:, hf * 512:(hf + 1) * 512], in_=ps[hf][:]),
                 reads=[r_ps[hf]], writes=[r_fgb])

    if has_outproj:
        wbf = P.sb("wbf_sb", [128, 16, D], BF16)
        wsc = P.sb("wsc_sb", [128, 1], F32)
        r_wsc = Reg()
        P.dma("sp", wsc[:], wsc_d, writes=[r_wsc])
        r_wbf = [Reg() for _ in range(16)]
        for ec in range(16):
            s = ec % 8
            P.dma("sp", stage[:, s, :], w_d[ec * 128:(ec + 1) * 128, :], writes=[r_stage[s]])
            eng = "pool" if ec % 2 else "dve"
            P.op(eng, lambda e, ec=ec, s=s: e.tensor_scalar(out=wbf[:, ec, :], in0=stage[:, s, :], scalar1=wsc[:, 0:1],
                                                            scalar2=None, op0=ALU.mult),
                 reads=[r_stage[s], r_wsc], writes=[r_wbf[ec]])

    NB = 2
    xt = [P.sb("xt%d" % i, [128, D], F32) for i in range(NB)]
    r_xt = [Reg() for _ in range(NB)]
    xn = [P.sb("xn%d" % i, [128, D], F32) for i in range(NB)]
    r_xn = [Reg() for _ in range(NB)]
    sq = P.sb("sq_sb", [128, D], F32)
    r_sq = Reg()
    stat = [P.sb("stat%d" % i, [128, 4], F32) for i in range(NB)]
    r_stat = [Reg() for _ in range(NB)]
    if has_outproj:
        yt = [P.sb("yt%d" % i, [128, E], BF16) for i in range(NB)]
        r_yt = [Reg() for _ in range(NB)]
        yT = [P.sb("yT%d" % i, [128, 16, 128], BF16) for i in range(NB)]
        r_yT = [Reg() for _ in range(NB)]
    if has_normmod:
        xb = [P.sb("xb%d" % i, [128, D], BF16) for i in range(NB)]
        r_xb = [Reg() for _ in range(NB)]
        hTt = [P.sb("hTt%d" % i, [128, 8, 128], BF16) for i in range(NB)]
        r_hTt = [Reg() for _ in range(NB)]
    if has_final:
        ot = [P.sb("ot%d" % i, [128, D], F32) for i in range(NB)]
        r_ot = [Reg() for _ in range(NB)]
    r_out = Reg()

    tiles = [(False, t * 128, min(128, ntok - t * 128)) for t in range((ntok + 127) // 128)]
    tiles += [(True, t * 128, min(128, nctx - t * 128)) for t in range((nctx + 127) // 128)]
    for it, (is_ctx, t0, n) in enumerate(tiles):
        b = it % NB
        src = xc_d if is_ctx else x_d
        v = 1 if is_ctx else 0
        P.dma("sp", xt[b][:n, :], src[t0:t0 + n, :], writes=[r_xt[b]])
        cur, r_cur = xt[b], r_xt[b]
        if has_outproj:
            P.dma("sp", yt[b][:n, :], y_d[t0:t0 + n, :], writes=[r_yt[b]])
            for g in range(2):
                for j in range(8):
                    ec = g * 8 + j
                    P.op("pe", lambda e, b=b, g=g, j=j, ec=ec, n=n: e.transpose(
                        out=pst[g][:, j * 128:j * 128 + n], in_=yt[b][:n, ec * 128:(ec + 1) * 128],
                        identity=ident[:n, :n]),
                         reads=[r_yt[b], r_ident], writes=[r_pst[g]], inc=(j == 7))
                eng = "act" if g == 0 else "dve"
                if eng == "act":
                    P.op("act", lambda e, b=b, g=g, n=n: e.copy(
                        out=yT[b][:, g * 8:(g + 1) * 8, :n],
                        in_=pst[g][:, :].rearrange("p (j t) -> p j t", t=128)[:, :, :n]),
                         reads=[r_pst[g]], writes=[r_yT[b]])
                else:
                    P.op("dve", lambda e, b=b, g=g, n=n: e.tensor_copy(
                        out=yT[b][:, g * 8:(g + 1) * 8, :n],
                        in_=pst[g][:, :].rearrange("p (j t) -> p j t", t=128)[:, :, :n]),
                         reads=[r_pst[g]], writes=[r_yT[b]])
            for hf in range(2):
                for ec in range(16):
                    P.op("pe", lambda e, b=b, hf=hf, ec=ec, n=n: e.matmul(
                        ps[hf][:n, :], lhsT=yT[b][:, ec, :n], rhs=wbf[:, ec, hf * 512:(hf + 1) * 512],
                        start=(ec == 0), stop=(ec == 15)),
                         reads=[r_yT[b], r_wbf[ec]], writes=[r_ps[hf]], inc=(ec == 15))
                P.op("dve", lambda e, b=b, hf=hf, n=n: e.tensor_tensor(
                    out=xn[b][:n, hf * 512:(hf + 1) * 512], in0=ps[hf][:n, :],
                    in1=gtb[:n, hf * 512:(hf + 1) * 512], op=ALU.mult),
                     reads=[r_ps[hf], r_gtb], writes=[r_xn[b]])
            P.op("pool", lambda e, b=b, n=n: e.tensor_tensor(
                out=xn[b][:n, :], in0=xn[b][:n, :], in1=xt[b][:n, :], op=ALU.add),
                 reads=[r_xn[b], r_xt[b]], writes=[r_xn[b]])
            cur, r_cur = xn[b], r_xn[b]
            if has_normmod:
                P.dma("pool", xo_d[t0:t0 + n, :], xn[b][:n, :], reads=[r_xn[b]], writes=[r_out])
        P.op("act", lambda e, b=b, n=n, cur=cur: e.activation(
            out=sq[:n, :], in_=cur[:n, :], func=AF.Square, accum_out=stat[b][:n, 0:1]),
             reads=[r_cur], writes=[r_sq, r_stat[b]])
        P.op("act", lambda e, b=b, n=n: e.activation(
            out=stat[b][:n, 1:2], in_=stat[b][:n, 0:1], func=AF.Ln, bias=epst[:n, :], scale=1.0 / D),
             reads=[r_stat[b], r_eps], writes=[r_stat[b]])
        P.op("act", lambda e, b=b, n=n: e.activation(
            out=stat[b][:n, 2:3], in_=stat[b][:n, 1:2], func=AF.Exp, scale=-0.5),
             reads=[r_stat[b]], writes=[r_stat[b]])
        if has_normmod:
            P.op("dve", lambda e, b=b, n=n, cur=cur: e.tensor_scalar(
                out=xb[b][:n, :], in0=cur[:n, :], scalar1=stat[b][:n, 2:3], scalar2=None, op0=ALU.mult),
                 reads=[r_cur, r_stat[b]], writes=[r_xb[b]])
            for j in range(8):
                P.op("pe", lambda e, b=b, j=j, n=n: e.transpose(
                    out=pst[0][:, j * 128:j * 128 + n], in_=xb[b][:n, j * 128:(j + 1) * 128],
                    identity=ident[:n, :n]),
                     reads=[r_xb[b], r_ident], writes=[r_pst[0]], inc=(j == 7))
            for j in range(8):
                eng = "act" if j % 2 == 0 else "dve"
                if eng == "act":
                    P.op("act", lambda e, b=b, j=j, n=n, v=v: e.activation(
                        out=hTt[b][:, j, :n], in_=pst[0][:, j * 128:j * 128 + n], func=AF.Identity,
                        bias=mcol[:, j, v:v + 1], scale=acol[:, j, v:v + 1]),
                         reads=[r_pst[0], r_mcol, r_acol], writes=[r_hTt[b]])
                else:
                    P.op("dve", lambda e, b=b, j=j, n=n, v=v: e.tensor_scalar(
                        out=hTt[b][:, j, :n], in0=pst[0][:, j * 128:j * 128 + n],
                        scalar1=acol[:, j, v:v + 1], scalar2=mcol[:, j, v:v + 1], op0=ALU.mult, op1=ALU.add),
                         reads=[r_pst[0], r_mcol, r_acol], writes=[r_hTt[b]])
            c0 = (ntok + t0) if is_ctx else t0
            P.dma("pool", hT_d[:, :, c0:c0 + n].rearrange("k p t -> p k t"), hTt[b][:, :, :n],
                  reads=[r_hTt[b]], writes=[r_out])
        if has_final:
            P.op("dve", lambda e, b=b, n=n, cur=cur: e.scalar_tensor_tensor(
                out=ot[b][:n, :], in0=cur[:n, :], scalar=stat[b][:n, 2:3], in1=fgb[:n, :],
                op0=ALU.mult, op1=ALU.mult), reads=[r_cur, r_stat[b], r_fgb], writes=[r_ot[b]])
            P.dma("pool", xo_d[t0:t0 + n, :], ot[b][:n, :], reads=[r_ot[b]], writes=[r_out])
    P.wait("sp", [r_out])
    P.emit()
    return nc, P


def _col(v):
    v = np.asarray(v, np.float32)
    return np.ascontiguousarray(v.reshape(-1, 128).T)


def _run(nc, in_maps):
    res = run_bass_kernel_spmd(nc, in_maps, core_ids=list(range(NCORES)))
    return res.results


def build_hgrn(nheads=4, nlat=SEQ, nctx=CTX, upto=3):
    nc = bass.Bass("TRN2", target_bir_lowering=False)
    P = Prog(nc)
    NCH = nlat // 128
    NCC = nctx // 128
    NBLK = nlat // 512

    def din(name, shape, dt):
        return nc.dram_tensor(name, list(shape), dt, kind="ExternalInput").ap()

    hl_d = din("hl", [8, 128, nlat], BF16)
    hc_d = din("hc", [8, 128, nctx], BF16)
    w_d = din("w", [128, 8, nheads, 640], F32)
    lg_d = din("lg", [128, 2, 2, nheads], F32)
    id_d = din("ident", [128, 128], BF16)
    mk_d = din("masks", [128, 2, 128], F32)
    y_d = nc.dram_tensor("y", [nlat, nheads * 128], BF16, kind="ExternalOutput").ap()

    ident = P.sb("ident", [128, 128], BF16); r_ident = Reg()
    masks = P.sb("masks", [128, 2, 128], F32); r_masks = Reg()
    lg = P.sb("lg", [128, 2, 2, nheads], F32); r_lg = Reg()
    lb = P.sb("lb", [128, 2, nheads], F32)
    oml = P.sb("oml", [128, 2, nheads], F32)
    noml = P.sb("noml", [128, 2, nheads], F32)
    r_lb = Reg()
    ones = P.sb("ones", [128, 512], F32); r_ones = Reg()
    epst = P.sb("epst", [128, 1], F32); r_eps = Reg()
    wst = P.sb("wst", [128, 8, 640], F32); r_wst = Reg()
    wbf = P.sb("wbf", [128, 8, 640], BF16); r_wbf = Reg()
    hblk = [P.sb("hblk%d" % i, [128, 8, 512], BF16) for i in range(2)]
    r_hblk = [Reg() for _ in range(2)]
    QF = P.sb("QF", [128, nlat], BF16); KF = P.sb("KF", [128, nlat + nctx], BF16)
    QB = P.sb("QB", [128, nlat], BF16); KB = P.sb("KB", [128, nlat + nctx], BF16)
    V = P.sb("V", [128, NCH + NCC, 128], BF16)
    ZG = P.sb("ZG", [128, NCH, 128], BF16)
    SB = P.sb("SB", [128, NCH, 128], BF16)
    es = [P.sb("es%d" % d, [128, NCH + NCC, 3], F32) for d in range(2)]
    r_QF = [Reg() for _ in range(NBLK)]; r_QB = [Reg() for _ in range(NBLK)]
    r_KF = [Reg() for _ in range(NBLK + 1)]; r_KB = [Reg() for _ in range(NBLK + 1)]
    r_V = [Reg() for _ in range(NBLK + 1)]; r_ZG = [Reg() for _ in range(NBLK)]
    r_es = [[Reg() for _ in range(NBLK + 1)] for _ in range(2)]
    r_SB = [Reg() for _ in range(NCH)]
    qf = P.sb("qf", [128, 512], F32); r_qf = Reg()
    sg = [P.sb("sg%d" % d, [128, 512], F32) for d in range(2)]; r_sg = [Reg(), Reg()]
    gg = [P.sb("gg%d" % d, [128, 512], F32) for d in range(2)]; r_gg = [Reg(), Reg()]
    kk = [P.sb("kk%d" % d, [128, 512], F32) for d in range(2)]; r_kk = [Reg(), Reg()]
    Bc = [[P.sb("Bc%d_%d" % (d, i), [128, 513], F32) for i in range(2)] for d in range(2)]
    r_Bc = [[Reg(), Reg()] for _ in range(2)]
    Rt = [P.sb("Rt%d" % d, [128, 4, 2], F32) for d in range(2)]; r_Rt = [Reg(), Reg()]
    dd = [P.sb("dd%d" % d, [128, 4, 3], F32) for d in range(2)]; r_dd = [Reg(), Reg()]
    eq = [P.sb("eq%d" % d, [128, 512], F32) for d in range(2)]; r_eq = [Reg(), Reg()]
    ek = [P.sb("ek%d" % d, [128, 512], F32) for d in range(2)]; r_ek = [Reg(), Reg()]
    S = P.sb("S", [128, 128], F32); r_S = Reg()
    S1 = P.sb("S1", [128, 128], F32); r_S1 = Reg()
    SFb = [P.sb("SFb%d" % i, [128, 128], BF16) for i in range(2)]; r_SFb = [Reg(), Reg()]
    kT = [P.sb("kT%d" % i, [128, 128], BF16) for i in range(2)]; r_kT = [Reg(), Reg()]
    Am = [[P.sb("Am%d_%d" % (d, i), [128, 128], BF16) for i in range(2)] for d in range(2)]
    r_Am = [[Reg(), Reg()] for _ in range(2)]
    sq = P.sb("sq", [128, 128], F32); r_sq = Reg()
    st = [P.sb("st%d" % i, [128, 3, 4], F32) for i in range(2)]; r_st = [Reg(), Reg()]
    y4 = [P.sb("y4_%d" % i, [128, 4, 128], BF16) for i in range(2)]; r_y4 = [Reg(), Reg()]
    r_out = Reg()
    r_ser = Reg()

    pb = [P.ps("pb%d" % i, [128, 512], F32) for i in range(6)]; r_pb = [Reg() for _ in range(6)]
    pt = [P.ps("pt%d" % i, [128, 1024], BF16) for i in range(2)]; r_pt = [Reg(), Reg()]
    psc = pb[0:3]; r_psc = r_pb[0:3]
    ptm = pb[3][:, :].rearrange("p (j c) -> p j c", c=256); r_ptm = r_pb[3]
    ppo = [pb[4][:, :].rearrange("p (j c) -> p j c", c=128), pb[5][:, :].rearrange("p (j c) -> p j c", c=128)]
    r_ppo = r_pb[4:6]

    P.dma("sp", ident[:], id_d, writes=[r_ident])
    P.dma("sp", masks[:], mk_d, writes=[r_masks])
    P.dma("sp", lg[:], lg_d, writes=[r_lg])
    P.op("dve", lambda e: e.memset(ones[:], 1.0), writes=[r_ones])
    P.op("dve", lambda e: e.memset(epst[:], EPS), writes=[r_eps])
    P.op("dve", lambda e: e.tensor_tensor(out=lb[:], in0=lg[:, 0, :, :], in1=lg[:, 1, :, :], op=ALU.subtract),
         reads=[r_lg], writes=[r_lb])
    P.op("act", lambda e: e.activation(out=lb[:], in_=lb[:], func=AF.Sigmoid), reads=[r_lb], writes=[r_lb])
    P.op("dve", lambda e: e.tensor_scalar(out=oml[:], in0=lb[:], scalar1=-1.0, scalar2=1.0, op0=ALU.mult, op1=ALU.add),
         reads=[r_lb], writes=[r_lb])
    P.op("dve", lambda e: e.tensor_scalar(out=noml[:], in0=lb[:], scalar1=-1.0, scalar2=None, op0=ALU.add),
         reads=[r_lb], writes=[r_lb])

    def load_w(h):
        P.dma("sp", wst[:], w_d[:, :, h, :], writes=[r_wst])
        for kc in range(8):
            eng = ("dve", "pool")[kc % 2]
            P.op(eng, lambda e, kc=kc: e.tensor_copy(out=wbf[:, kc, :], in_=wst[:, kc, :]),
                 reads=[r_wst], writes=[r_wbf])

    blk_i = [0]

    def pass1_block(h, is_ctx, t0, n):
        nch = n // 128
        bb = blk_i[0] % 2
        blk_i[0] += 1
        src = hc_d if is_ctx else hl_d
        P.dma("sp", hblk[bb][:, :, :n], src[:, :, t0:t0 + n].rearrange("k p t -> p k t"), writes=[r_hblk[bb]])
        bi = NBLK if is_ctx else t0 // 512
        c0 = NCH if is_ctx else t0 // 128
        kcol0 = nlat if is_ctx else t0
        streams = (1, 2) if is_ctx else (0, 1, 2)
        for s in streams:
            for kc in range(8):
                P.op("pe", lambda e, s=s, kc=kc: e.matmul(psc[s][:, :n], lhsT=wbf[:, kc, s * 128:(s + 1) * 128],
                                                          rhs=hblk[bb][:, kc, :n], start=(kc == 0), stop=(kc == 7)),
                     reads=[r_wbf, r_hblk[bb]], writes=[r_psc[s]], inc=(kc == 7))
        LVL = int(os.environ.get('HG_LVL', '9'))
        if not is_ctx:
            P.op("act", lambda e: e.activation(out=qf[:, :n], in_=psc[0][:, :n], func=AF.Silu),
                 reads=[r_psc[0]], writes=[r_qf])
        for d in range(2):
            P.op("act", lambda e, d=d: e.activation(out=sg[d][:, :n], in_=psc[1 + d][:, :n], func=AF.Sigmoid),
                 reads=[r_psc[1 + d]], writes=[r_sg[d]])
        if LVL < 2:
            return
        for d in range(2):
            P.op("act", lambda e, d=d: e.activation(out=gg[d][:, :n], in_=sg[d][:, :n], func=AF.Ln,
                                                    bias=lb[:, d, h:h + 1], scale=oml[:, d, h:h + 1]),
                 reads=[r_sg[d], r_lb], writes=[r_gg[d]])
            P.op("dve", lambda e, d=d: e.tensor_scalar(out=kk[d][:, :n], in0=sg[d][:, :n],
                                                       scalar1=noml[:, d, h:h + 1], scalar2=oml[:, d, h:h + 1],
                                                       op0=ALU.mult, op1=ALU.add),
                 reads=[r_sg[d], r_lb], writes=[r_kk[d]])
        first = is_ctx or t0 == 0
        if LVL < 3:
            return
        for d in range(2):
            cur, prv = Bc[d][bb], Bc[d][1 - bb]
            if first:
                P.op("dve", lambda e, cur=cur: e.memset(cur[:, 0:1], 0.0), writes=[r_Bc[d][bb]])
                P.op("dve", lambda e, cur=cur, d=d: e.tensor_tensor_scan(
                    out=cur[:, 1:1 + n], data0=ones[:, :n], data1=gg[d][:, :n], initial=0.0,
                    op0=ALU.mult, op1=ALU.add), reads=[r_ones, r_gg[d]], writes=[r_Bc[d][bb]])
            else:
                P.op("dve", lambda e, cur=cur, prv=prv: e.tensor_copy(out=cur[:, 0:1], in_=prv[:, 512:513]),
                     reads=[r_Bc[d][1 - bb]], writes=[r_Bc[d][bb]])
                P.op("dve", lambda e, cur=cur, prv=prv, d=d: e.tensor_tensor_scan(
                    out=cur[:, 1:1 + n], data0=ones[:, :n], data1=gg[d][:, :n], initial=prv[:, 512:513],
                    op0=ALU.mult, op1=ALU.add), reads=[r_ones, r_gg[d], r_Bc[d][1 - bb]], writes=[r_Bc[d][bb]])
            if LVL < 4:
                continue
            lo = cur[:, 0:n].rearrange("p (c t) -> p c t", t=128)
            hi = cur[:, 1:1 + n].rearrange("p (c t) -> p c t", t=128)
            P.op("dve", lambda e, lo=lo, d=d: e.tensor_copy(out=Rt[d][:, :nch, 0], in_=lo[:, :, 64]),
                 reads=[r_Bc[d][bb]], writes=[r_Rt[d]])
            P.op("dve", lambda e, lo=lo, d=d: e.tensor_scalar(out=Rt[d][:, :nch, 1], in0=lo[:, :, 64], scalar1=-1.0,
                                                              scalar2=None, op0=ALU.mult),
                 reads=[r_Bc[d][bb]], writes=[r_Rt[d]])
            P.op("dve", lambda e, lo=lo, hi=hi, d=d: e.tensor_tensor(out=dd[d][:, :nch, 0], in0=hi[:, :, 127],
                                                                     in1=lo[:, :, 0], op=ALU.subtract),
                 reads=[r_Bc[d][bb]], writes=[r_dd[d]])
            P.op("dve", lambda e, lo=lo, hi=hi, d=d: e.tensor_tensor(out=dd[d][:, :nch, 1], in0=hi[:, :, 127],
                                                                     in1=lo[:, :, 64], op=ALU.subtract),
                 reads=[r_Bc[d][bb]], writes=[r_dd[d]])
            P.op("dve", lambda e, lo=lo, d=d: e.tensor_tensor(out=dd[d][:, :nch, 2], in0=lo[:, :, 64],
                                                              in1=lo[:, :, 0], op=ALU.subtract),
                 reads=[r_Bc[d][bb]], writes=[r_dd[d]])
            P.op("act", lambda e, d=d: e.activation(out=es[d][:, c0:c0 + nch, :], in_=dd[d][:, :nch, :], func=AF.Exp),
                 reads=[r_dd[d]], writes=[r_es[d][bi]])
            if LVL < 5:
                continue
            for c in range(nch):
                sl = slice(c * 128, (c + 1) * 128)
                if d == 0:
                    bsl = cur[:, 1 + c * 128:1 + (c + 1) * 128]
                    qs, qb_, ks, kb_ = 1.0, Rt[d][:, c, 1:2], -1.0, Rt[d][:, c, 0:1]
                else:
                    bsl = cur[:, c * 128:(c + 1) * 128]
                    qs, qb_, ks, kb_ = -1.0, Rt[d][:, c, 0:1], 1.0, Rt[d][:, c, 1:2]
                if not is_ctx:
                    P.op("act", lambda e, d=d, sl=sl, bsl=bsl, qs=qs, qb_=qb_: e.activation(
                        out=eq[d][:, sl], in_=bsl, func=AF.Exp, bias=qb_, scale=qs),
                         reads=[r_Bc[d][bb], r_Rt[d]], writes=[r_eq[d]])
                P.op("act", lambda e, d=d, sl=sl, bsl=bsl, ks=ks, kb_=kb_: e.activation(
                    out=ek[d][:, sl], in_=bsl, func=AF.Exp, bias=kb_, scale=ks),
                     reads=[r_Bc[d][bb], r_Rt[d]], writes=[r_ek[d]])
            Kd, r_Kd = (KF, r_KF) if d == 0 else (KB, r_KB)
            P.op("pool", lambda e, d=d, Kd=Kd: e.tensor_tensor(out=Kd[:, kcol0:kcol0 + n], in0=kk[d][:, :n],
                                                               in1=ek[d][:, :n], op=ALU.mult),
                 reads=[r_kk[d], r_ek[d]], writes=[r_Kd[bi]])
            if not is_ctx:
                Qd, r_Qd = (QF, r_QF) if d == 0 else (QB, r_QB)
                P.op("dve", lambda e, d=d, Qd=Qd: e.tensor_tensor(out=Qd[:, t0:t0 + n], in0=qf[:, :n],
                                                                  in1=eq[d][:, :n], op=ALU.mult),
                     reads=[r_qf, r_eq[d]], writes=[r_Qd[bi]])
        if LVL < 6:
            return
        ncol = 128 if is_ctx else 256
        for cp in range(nch // 2):
            for j in range(2):
                c = cp * 2 + j
                for kc in range(8):
                    P.op("pe", lambda e, j=j, c=c, kc=kc: e.matmul(
                        ptm[:, j, :ncol], lhsT=hblk[bb][:, kc, c * 128:(c + 1) * 128], rhs=wbf[:, kc, 384:384 + ncol],
                        start=(kc == 0), stop=(kc == 7)),
                         reads=[r_wbf, r_hblk[bb]], writes=[r_ptm], inc=(kc == 7 and j == 1))
            cc = c0 + cp * 2
            P.op("dve", lambda e, cc=cc: e.tensor_copy(out=V[:, cc:cc + 2, :], in_=ptm[:, :, 0:128]),
                 reads=[r_ptm], writes=[r_V[bi], r_ser])
            if not is_ctx:
                P.op("act", lambda e, cc=cc: e.activation(out=ZG[:, cc:cc + 2, :], in_=ptm[:, :, 128:256], func=AF.Silu),
                     reads=[r_ptm, r_ser], writes=[r_ZG[bi]])

    def state_step(d, ck, kcol, bi, first):
        Kd, r_Kd = (KF, r_KF) if d == 0 else (KB, r_KB)
        i1, i2 = (0, 1) if d == 0 else (0, 2)
        sl = state_step.i % 2
        state_step.i += 1
        P.op("pe", lambda e: e.transpose(out=pt[sl][:, 0:128], in_=Kd[:, kcol:kcol + 128], identity=ident[:]),
             reads=[r_Kd[bi], r_ident], writes=[r_pt[sl]])
        P.op("act", lambda e: e.copy(out=kT[sl][:], in_=pt[sl][:, 0:128]), reads=[r_pt[sl]], writes=[r_kT[sl]])
        P.op("pe", lambda e: e.matmul(pb[sl][:, 0:128], lhsT=kT[sl][:], rhs=V[:, ck, :], start=True, stop=True),
             reads=[r_kT[sl], r_V[bi]], writes=[r_pb[sl]])
        if first:
            P.op("dve", lambda e: e.tensor_scalar(out=S[:], in0=pb[sl][:, 0:128], scalar1=es[d][:, ck, i2:i2 + 1],
                                                  scalar2=None, op0=ALU.mult),
                 reads=[r_pb[sl], r_es[d][bi]], writes=[r_S])
        else:
            P.op("pool", lambda e: e.tensor_scalar(out=S1[:], in0=S[:], scalar1=es[d][:, ck, i1:i1 + 1],
                                                   scalar2=None, op0=ALU.mult),
                 reads=[r_S, r_es[d][bi]], writes=[r_S1])
            P.op("dve", lambda e: e.scalar_tensor_tensor(out=S[:], in0=pb[sl][:, 0:128], scalar=es[d][:, ck, i2:i2 + 1],
                                                         in1=S1[:], op0=ALU.mult, op1=ALU.add),
                 reads=[r_pb[sl], r_es[d][bi], r_S1], writes=[r_S])
    state_step.i = 0

    for h in range(nheads):
        load_w(h)
        pass1_block(h, True, 0, nctx)
        for blk in range(NBLK):
            pass1_block(h, False, blk * 512, 512)
        if upto < 2:
            continue
        for c in range(NCC - 1, -1, -1):
            state_step(1, NCH + c, nlat + c * 128, NBLK, first=(c == NCC - 1))
        for n in range(NCH - 1, -1, -1):
            bi = n // 4
            P.op("act", lambda e, n=n: e.activation(out=SB[:, n, :], in_=S[:], func=AF.Identity,
                                                    scale=es[1][:, n, 1:2]),
                 reads=[r_S, r_es[1][bi]], writes=[r_SB[n]])
            if n > 0:
                state_step(1, n, n * 128, bi, first=False)
        if upto < 3:
            continue
        for c in range(NCC):
            state_step(0, NCH + c, nlat + c * 128, NBLK, first=(c == 0))
        for n in range(NCH):
            bi = n // 4
            g, j = n // 4, n % 4
            gb = g % 2
            sl = n % 2
            csl = slice(n * 128, (n + 1) * 128)
            P.op("act", lambda e, n=n, sl=sl: e.activation(out=SFb[sl][:], in_=S[:], func=AF.Identity,
                                                           scale=es[0][:, n, 2:3]),
                 reads=[r_S, r_es[0][bi]], writes=[r_SFb[sl]])
            for d in range(2):
                Kd, r_Kd = (KF, r_KF) if d == 0 else (KB, r_KB)
                Qd, r_Qd = (QF, r_QF) if d == 0 else (QB, r_QB)
                c0_, c1_, c2_ = n * 128, n * 128 + 64, (n + 1) * 128
                if d == 0:
                    P.op("pe", lambda e, Kd=Kd, Qd=Qd, c0_=c0_, c1_=c1_, c2_=c2_: e.matmul(
                        pb[2][0:64, 0:128], lhsT=Kd[:, c0_:c1_], rhs=Qd[:, c0_:c2_], start=True, stop=True),
                         reads=[r_Kd[bi], r_Qd[bi]], writes=[r_pb[2]], inc=False)
                    P.op("pe", lambda e, Kd=Kd, Qd=Qd, c0_=c0_, c1_=c1_, c2_=c2_: e.matmul(
                        pb[2][64:128, 64:128], lhsT=Kd[:, c1_:c2_], rhs=Qd[:, c1_:c2_], start=True, stop=True),
                         reads=[r_Kd[bi], r_Qd[bi]], writes=[r_pb[2]])
                else:
                    P.op("pe", lambda e, Kd=Kd, Qd=Qd, c0_=c0_, c1_=c1_, c2_=c2_: e.matmul(
                        pb[3][64:128, 0:128], lhsT=Kd[:, c1_:c2_], rhs=Qd[:, c0_:c2_], start=True, stop=True),
                         reads=[r_Kd[bi], r_Qd[bi]], writes=[r_pb[3]], inc=False)
                    P.op("pe", lambda e, Kd=Kd, Qd=Qd, c0_=c0_, c1_=c1_, c2_=c2_: e.matmul(
                        pb[3][0:64, 0:64], lhsT=Kd[:, c0_:c1_], rhs=Qd[:, c0_:c1_], start=True, stop=True),
                         reads=[r_Kd[bi], r_Qd[bi]], writes=[r_pb[3]])
                P.op("dve", lambda e, d=d, sl=sl: e.tensor_tensor(out=Am[d][sl][:], in0=pb[2 + d][:, 0:128],
                                                                  in1=masks[:, d, :], op=ALU.mult),
                     reads=[r_pb[2 + d], r_masks], writes=[r_Am[d][sl]])
            P.op("pe", lambda e, sl=sl, gb=gb, j=j, n=n: e.matmul(ppo[gb][:, j, :], lhsT=Am[0][sl][:], rhs=V[:, n, :],
                                                                  start=True, stop=False),
                 reads=[r_Am[0][sl], r_V[bi]], writes=[r_ppo[gb]], inc=False)
            P.op("pe", lambda e, sl=sl, gb=gb, j=j, n=n: e.matmul(ppo[gb][:, j, :], lhsT=Am[1][sl][:], rhs=V[:, n, :],
                                                                  start=False, stop=False),
                 reads=[r_Am[1][sl]], writes=[r_ppo[gb]], inc=False)
            P.op("pe", lambda e, sl=sl, gb=gb, j=j, csl=csl: e.matmul(ppo[gb][:, j, :], lhsT=QF[:, csl], rhs=SFb[sl][:],
                                                                      start=False, stop=False),
                 reads=[r_QF[bi], r_SFb[sl]], writes=[r_ppo[gb]], inc=False)
            P.op("pe", lambda e, gb=gb, j=j, csl=csl, n=n: e.matmul(ppo[gb][:, j, :], lhsT=QB[:, csl], rhs=SB[:, n, :],
                                                                    start=False, stop=True),
                 reads=[r_QB[bi], r_SB[n]], writes=[r_ppo[gb]])
            if n < NCH - 1:
                state_step(0, n, n * 128, bi, first=False)
            P.op("act", lambda e, gb=gb, j=j: e.activation(out=sq[:], in_=ppo[gb][:, j, :], func=AF.Square,
                                                           accum_out=st[gb][:, 0, j:j + 1]),
                 reads=[r_ppo[gb]], writes=[r_sq, r_st[gb]])
            if j == 3:
                P.op("act", lambda e, gb=gb: e.activation(out=st[gb][:, 1, :], in_=st[gb][:, 0, :], func=AF.Ln,
                                                          bias=epst[:], scale=1.0 / 128),
                     reads=[r_st[gb], r_eps], writes=[r_st[gb]])
                P.op("act", lambda e, gb=gb: e.activation(out=st[gb][:, 2, :], in_=st[gb][:, 1, :], func=AF.Exp,
                                                          scale=-0.5),
                     reads=[r_st[gb]], writes=[r_st[gb]])
                for jj in range(4):
                    nn = g * 4 + jj
                    P.op("dve", lambda e, gb=gb, jj=jj, nn=nn: e.scalar_tensor_tensor(
                        out=y4[gb][:, jj, :], in0=ppo[gb][:, jj, :], scalar=st[gb][:, 2, jj:jj + 1],
                        in1=ZG[:, nn, :], op0=ALU.mult, op1=ALU.mult),
                         reads=[r_ppo[gb], r_st[gb], r_ZG[bi]], writes=[r_y4[gb]])
                P.dma("pool", y_d[g * 512:(g + 1) * 512, h * 128:(h + 1) * 128].rearrange("(j p) v -> p j v", p=128),
                      y4[gb][:], reads=[r_y4[gb]], writes=[r_out])
    P.wait("sp", [r_out])
    P.emit()
    return nc, P


def _consts():
    ident = np.eye(128, dtype=NPBF)
    s = np.arange(128)[:, None]
    t = np.arange(128)[None, :]
    masks = np.stack([(s <= t), (s >= t)], axis=1).astype(np.float32)
    return ident, np.ascontiguousarray(masks)


def hgrn_maps(inp, hl, hc):
    ident, masks = _consts()
    w_in = np.asarray(inp["hg_w_in"][0])
    lgt = np.asarray(inp["hg_lb_logits"])
    maps = []
    for core in range(NCORES):
        b, hg = core // 4, core % 4
        w5 = w_in.reshape(8, 128, 5, 16, 128)[:, :, :, hg * 4:(hg + 1) * 4, :]
        w = np.ascontiguousarray(w5.transpose(1, 0, 3, 2, 4).reshape(128, 8, 4, 640))
        lg = lgt.reshape(2, 2, 16, 128)[:, :, hg * 4:(hg + 1) * 4, :]
        lg = np.ascontiguousarray(lg.transpose(3, 0, 1, 2))
        hlT = np.ascontiguousarray(np.asarray(hl[b]).reshape(SEQ, 8, 128).transpose(1, 2, 0))
        hcT = np.ascontiguousarray(np.asarray(hc[b]).reshape(CTX, 8, 128).transpose(1, 2, 0))
        maps.append(dict(hl=hlT, hc=hcT, w=w, lg=lg, ident=ident, masks=masks))
    return maps


def build_fourier():
    nc = bass.Bass("TRN2", target_bir_lowering=False)
    P = Prog(nc)

    def din(name, shape, dt):
        return nc.dram_tensor(name, list(shape), dt, kind="ExternalInput").ap()

    hp_d = din("hp", [8, 128, 64, 128], BF16)
    wu_d = din("wu", [128, 8, 512], F32)
    wz_d = din("wz", [128, 8, 512], F32)
    cs_d = din("cs", [128, 2, 512], BF16)
    gt_d = din("gt", [64, 128, 512], BF16)
    fb_d = din("fb", [128, 2, 128], BF16)
    id_d = din("ident", [128, 128], BF16)
    y_d = nc.dram_tensor("yg", [SEQ, 512], BF16, kind="ExternalOutput").ap()

    ident = P.sb("ident", [128, 128], BF16); r_ident = Reg()
    cs = P.sb("cs", [128, 2, 512], BF16); r_cs = Reg()
    fb = P.sb("fb", [128, 2, 128], BF16); r_fb = Reg()
    stg = [P.sb("stg%d" % i, [128, 512], F32) for i in range(2)]; r_stg = [Reg(), Reg()]
    wubk = [P.sb("wubk%d" % i, [128, 512], BF16) for i in range(2)]; r_wubk = [Reg(), Reg()]
    WuT = P.sb("WuT", [128, 4, 1024], BF16); r_WuT = Reg()
    Wp = P.sb("Wp", [128, 8, 1024], BF16); r_Wp = Reg()
    wzb = P.sb("wzb", [128, 8, 512], BF16); r_wzb = Reg()
    hblk = [P.sb("hblk%d" % i, [128, 8, 2, 128], BF16) for i in range(2)]; r_hblk = [Reg(), Reg()]
    Zsb = [P.sb("Zsb%d" % i, [128, 1024], BF16) for i in range(2)]; r_Zsb = [Reg(), Reg()]
    gtab = [P.sb("gtab%d" % i, [128, 512], BF16) for i in range(2)]; r_gtab = [Reg(), Reg()]
    Abuf = P.sb("Abuf", [128, 4, 2, 128, 64], BF16)
    r_Ab = [Reg() for _ in range(64)]
    ATs = [P.sb("ATs%d" % i, [128, 2, 4, 128], BF16) for i in range(2)]; r_ATs = [Reg(), Reg()]
    zs = [P.sb("zs%d" % i, [128, 512], F32) for i in range(2)]; r_zs = [Reg(), Reg()]
    ygt = [P.sb("ygt%d" % i, [128, 512], BF16) for i in range(2)]; r_ygt = [Reg(), Reg()]
    r_out = Reg()

    pb = [P.ps("pb%d" % i, [128, 512], F32) for i in range(6)]; r_pb = [Reg() for _ in range(6)]
    pt = [P.ps("pt%d" % i, [128, 1024], BF16) for i in range(2)]; r_pt = [Reg(), Reg()]

    P.dma("sp", ident[:], id_d, writes=[r_ident])
    P.dma("sp", cs[:], cs_d, writes=[r_cs])
    P.dma("sp", fb[:], fb_d, writes=[r_fb])

    for kc in range(8):
        s = kc % 2
        P.dma("sp", stg[s][:], wu_d[:, kc, :], writes=[r_stg[s]])
        P.op("dve", lambda e, s=s: e.tensor_copy(out=wubk[s][:], in_=stg[s][:]), reads=[r_stg[s]], writes=[r_wubk[s]])
        for jb in range(4):
            P.op("pe", lambda e, s=s, jb=jb: e.transpose(out=pt[s][:, jb * 128:(jb + 1) * 128],
                                                         in_=wubk[s][:, jb * 128:(jb + 1) * 128], identity=ident[:]),
                 reads=[r_wubk[s], r_ident], writes=[r_pt[s]], inc=(jb == 3))
        P.op("act", lambda e, s=s, kc=kc: e.copy(out=WuT[:, :, kc * 128:(kc + 1) * 128],
                                                 in_=pt[s][:, 0:512].rearrange("p (j k) -> p j k", k=128)),
             reads=[r_pt[s]], writes=[r_WuT])
    for kc in range(8):
        s = kc % 2
        P.dma("sp", stg[s][:], wz_d[:, kc, :], writes=[r_stg[s]])
        P.op("pool", lambda e, s=s, kc=kc: e.tensor_copy(out=wzb[:, kc, :], in_=stg[s][:]),
             reads=[r_stg[s]], writes=[r_wzb])
    i = 0
    for g in range(2):
        for kc in range(8):
            bk = i % 2
            i += 1
            for jc in range(2):
                P.op("pe", lambda e, g=g, kc=kc, jc=jc, bk=bk: e.matmul(
                    pb[bk][:], lhsT=WuT[:, g * 2 + jc, kc * 128:(kc + 1) * 128], rhs=cs[:, jc, :],
                    start=(jc == 0), stop=(jc == 1)), reads=[r_WuT, r_cs], writes=[r_pb[bk]], inc=(jc == 1))
            outv = Wp[:, kc, :].rearrange("p (c g m) -> p c g m", c=2, g=2)[:, :, g, :]
            inv = pb[bk][:, :].rearrange("p (c m) -> p c m", c=2)
            if bk == 0:
                P.op("act", lambda e, outv=outv, inv=inv: e.copy(out=outv, in_=inv), reads=[r_pb[bk]], writes=[r_Wp])
            else:
                P.op("dve", lambda e, outv=outv, inv=inv: e.tensor_copy(out=outv, in_=inv), reads=[r_pb[bk]], writes=[r_Wp])

    for bp in range(32):
        hb = bp % 2
        P.dma("sp", hblk[hb][:], hp_d[:, :, 2 * bp:2 * bp + 2, :].rearrange("k p b a -> p k b a"), writes=[r_hblk[hb]])
        for bj in range(2):
            b = 2 * bp + bj
            zb = b % 2
            P.dma("sp", gtab[zb][:], gt_d[b], writes=[r_gtab[zb]])
            for half in range(2):
                for kc in range(8):
                    P.op("pe", lambda e, hb=hb, bj=bj, half=half, kc=kc: e.matmul(
                        pb[half][:], lhsT=hblk[hb][:, kc, bj, :], rhs=Wp[:, kc, half * 512:(half + 1) * 512],
                        start=(kc == 0), stop=(kc == 7)), reads=[r_hblk[hb], r_Wp], writes=[r_pb[half]], inc=(kc == 7))
            P.op("act", lambda e, zb=zb: e.copy(out=Zsb[zb][:, 0:512], in_=pb[0][:]), reads=[r_pb[0]], writes=[r_Zsb[zb]])
            P.op("dve", lambda e, zb=zb: e.tensor_copy(out=Zsb[zb][:, 512:1024], in_=pb[1][:]),
                 reads=[r_pb[1]], writes=[r_Zsb[zb]])
            for mp in range(2):
                bank = 2 + mp
                for mj in range(2):
                    mb = mp * 2 + mj
                    P.op("pe", lambda e, zb=zb, mb=mb, mj=mj, bank=bank: e.matmul(
                        pb[bank][:, mj * 256:(mj + 1) * 256], lhsT=Zsb[zb][:, mb * 128:(mb + 1) * 128],
                        rhs=gtab[zb][:, 0:256], start=True, stop=False),
                         reads=[r_Zsb[zb], r_gtab[zb]], writes=[r_pb[bank]], inc=False)
                    P.op("pe", lambda e, zb=zb, mb=mb, mj=mj, bank=bank: e.matmul(
                        pb[bank][:, mj * 256:(mj + 1) * 256], lhsT=Zsb[zb][:, 512 + mb * 128:512 + (mb + 1) * 128],
                        rhs=gtab[zb][:, 256:512], start=False, stop=True),
                         reads=[r_Zsb[zb], r_gtab[zb]], writes=[r_pb[bank]], inc=(mj == 1))
                outv = Abuf[:, mp * 2:mp * 2 + 2, :, :, b]
                inv = pb[bank][:, :].rearrange("p (mb ri k) -> p mb ri k", mb=2, ri=2)
                if mp == 0:
                    P.op("act", lambda e, outv=outv, inv=inv: e.copy(out=outv, in_=inv), reads=[r_pb[bank]], writes=[r_Ab[b]])
                else:
                    P.op("dve", lambda e, outv=outv, inv=inv: e.tensor_copy(out=outv, in_=inv),
                         reads=[r_pb[bank]], writes=[r_Ab[b]])

    for pr in range(64):
        s = pr % 2
        prm, par = pr % 32, pr // 32
        P.dma("sp", hblk[s][:], hp_d[:, :, 2 * prm:2 * prm + 2, :].rearrange("k p b a -> p k b a"), writes=[r_hblk[s]])
        for kc in range(8):
            lhs = hblk[s][:, kc, :, :].rearrange("p b (k2 two) -> p (b k2) two", two=2)[:, :, par]
            P.op("pe", lambda e, kc=kc, lhs=lhs: e.matmul(pb[4][:], lhsT=lhs, rhs=wzb[:, kc, :],
                                                          start=(kc == 0), stop=(kc == 7)),
                 reads=[r_hblk[s], r_wzb], writes=[r_pb[4]], inc=(kc == 7))
        P.op("act", lambda e, s=s: e.activation(out=zs[s][:], in_=pb[4][:], func=AF.Silu), reads=[r_pb[4]], writes=[r_zs[s]])
        for ri in range(2):
            for mb in range(4):
                src = Abuf[:, mb, ri, 2 * pr:2 * pr + 2, :].rearrange("p k b -> p (k b)")
                P.op("pe", lambda e, s=s, ri=ri, mb=mb, src=src: e.transpose(
                    out=pt[s][:, (ri * 4 + mb) * 128:(ri * 4 + mb + 1) * 128], in_=src, identity=ident[:]),
                     reads=r_Ab + [r_ident] if (ri == 0 and mb == 0) else [r_ident], writes=[r_pt[s]],
                     inc=(ri == 1 and mb == 3))
        P.op("dve", lambda e, s=s: e.tensor_copy(out=ATs[s][:].rearrange("p r m c -> p (r m c)"), in_=pt[s][:]),
             reads=[r_pt[s]], writes=[r_ATs[s]])
        for ri in range(2):
            P.op("pe", lambda e, s=s, ri=ri: e.matmul(pb[5][:], lhsT=fb[:, ri, :],
                                                      rhs=ATs[s][:, ri, :, :].rearrange("p m c -> p (m c)"),
                                                      start=(ri == 0), stop=(ri == 1)),
                 reads=[r_fb, r_ATs[s]], writes=[r_pb[5]], inc=(ri == 1))
        P.op("dve", lambda e, s=s: e.tensor_tensor(out=ygt[s][:], in0=pb[5][:], in1=zs[s][:], op=ALU.mult),
             reads=[r_pb[5], r_zs[s]], writes=[r_ygt[s]])
        yv = y_d.rearrange("(k2 r) c -> r k2 c", r=128)
        for kap in range(2):
            P.dma("pool", yv[2 * pr + kap], ygt[s][kap * 64:(kap + 1) * 64, :], reads=[r_ygt[s]], writes=[r_out])
    P.wait("sp", [r_out])
    P.emit()
    return nc, P


def fourier_tables():
    N = SEQ
    j = np.arange(256)[:, None]; m = np.arange(256)[None, :]
    ang = 2 * np.pi * (j * m % 256) / 256
    C = np.cos(ang) / 16.0; S = np.sin(ang) / 16.0
    cs = np.concatenate([C, S], axis=1).reshape(2, 128, 512).transpose(1, 0, 2)
    a = np.arange(128)[None, :, None]; b = np.arange(64)[:, None, None]; k1 = np.arange(128)[None, None, :]
    th = 2 * np.pi * ((k1 * (64 * a + b)) % N) / N
    Gr = np.cos(th) / np.sqrt(128.0); Gi = -np.sin(th) / np.sqrt(128.0)
    gt = np.concatenate([Gr, Gi, Gi, -Gr], axis=2)
    bb = np.arange(64)[:, None]; k2 = np.arange(64)[None, :]
    ph = 2 * np.pi * ((bb * k2) % 64) / 64
    Fc = np.cos(ph) / 8.0; Fs = np.sin(ph) / 8.0
    fb = np.zeros((128, 2, 128))
    for kap in range(2):
        fb[kap * 64:(kap + 1) * 64, 0, kap * 64:(kap + 1) * 64] = Fc
        fb[kap * 64:(kap + 1) * 64, 1, kap * 64:(kap + 1) * 64] = Fs
    return (np.ascontiguousarray(cs).astype(NPBF), np.ascontiguousarray(gt).astype(NPBF),
            np.ascontiguousarray(fb).astype(NPBF))


def fourier_maps(inp, h1):
    ident, _ = _consts()
    cs, gt, fb = fourier_tables()
    w_in = np.asarray(inp["ft_w_in"][0])
    maps = []
    for core in range(NCORES):
        b, gp = core // 4, core % 4
        wu = np.ascontiguousarray(w_in[:, gp * 512:(gp + 1) * 512].reshape(8, 128, 512).transpose(1, 0, 2))
        wz = np.ascontiguousarray(w_in[:, E + gp * 512:E + (gp + 1) * 512].reshape(8, 128, 512).transpose(1, 0, 2))
        hp = np.ascontiguousarray(np.asarray(h1[b]).reshape(128, 64, 8, 128).transpose(2, 3, 1, 0))
        maps.append(dict(hp=hp, wu=wu, wz=wz, cs=cs, gt=gt, fb=fb, ident=ident))
    return maps


_CACHE = {}


def _prog(key, builder):
    if key not in _CACHE:
        _CACHE[key] = builder()[0]
    return _CACHE[key]


def _ada_maps(inp, layer):
    aw = np.asarray(inp["ada_w"][layer], np.float32)
    ab = np.asarray(inp["ada_b"][layer], np.float32)
    return aw, ab


def kernel(x, c, ctx, c_ctx, ada_w, ada_b, norm_g, hg_w_in, hg_lb_logits, hg_norm_g,
           hg_w_out, ft_w_in, ft_w_out, final_g):
    inp = dict(x=np.asarray(x, np.float32), c=np.asarray(c, np.float32), ctx=np.asarray(ctx, np.float32),
               c_ctx=np.asarray(c_ctx, np.float32), ada_w=np.asarray(ada_w, np.float32),
               ada_b=np.asarray(ada_b, np.float32), norm_g=np.asarray(norm_g, np.float32),
               hg_w_in=np.asarray(hg_w_in, np.float32), hg_lb_logits=np.asarray(hg_lb_logits, np.float32),
               hg_norm_g=np.asarray(hg_norm_g, np.float32), hg_w_out=np.asarray(hg_w_out, np.float32),
               ft_w_in=np.asarray(ft_w_in, np.float32), ft_w_out=np.asarray(ft_w_out, np.float32),
               final_g=np.asarray(final_g, np.float32))
    ident, _ = _consts()
    TS = SEQ // 4
    CS_ = CTX // 4

    def awm(layer):
        aw, ab = _ada_maps(inp, layer)
        return (np.ascontiguousarray(aw[:, :2 * D].reshape(8, 128, 2 * D).transpose(1, 0, 2)), _col(ab[:2 * D]))

    def awg(layer):
        aw, ab = _ada_maps(inp, layer)
        return (np.ascontiguousarray(aw[:, 2 * D:].reshape(8, 128, D).transpose(1, 0, 2)),
                np.ascontiguousarray(ab[None, 2 * D:]))

    nc = _prog("A1", lambda: build_tok(TS, CS_, False, True, False))
    aw_m0, ab_m0 = awm(0)
    maps = []
    for core in range(NCORES):
        b, seg = core // 4, core % 4
        cv = np.stack([_col(inp["c"][b]), _col(inp["c_ctx"])], axis=-1)
        maps.append(dict(x=np.ascontiguousarray(inp["x"][b, seg * TS:(seg + 1) * TS]),
                         xc=np.ascontiguousarray(inp["ctx"][b, seg * CS_:(seg + 1) * CS_]),
                         cvec=np.ascontiguousarray(cv), ident=ident, aw_m=aw_m0, ab_m=ab_m0,
                         ng=_col(inp["norm_g"][0])))
    res = _run(nc, maps)
    hT = [np.asarray(r["hT"]) for r in res]

    nc = _prog("A2", build_hgrn)
    _, masks = _consts()
    w_in = inp["hg_w_in"][0]
    lgt = inp["hg_lb_logits"]
    maps = []
    for core in range(NCORES):
        b, hg = core // 4, core % 4
        w5 = w_in.reshape(8, 128, 5, 16, 128)[:, :, :, hg * 4:(hg + 1) * 4, :]
        w = np.ascontiguousarray(w5.transpose(1, 0, 3, 2, 4).reshape(128, 8, 4, 640))
        lg = np.ascontiguousarray(lgt.reshape(2, 2, 16, 128)[:, :, hg * 4:(hg + 1) * 4, :].transpose(3, 0, 1, 2))
        hl = np.ascontiguousarray(np.concatenate([hT[b * 4 + s][:, :, :TS] for s in range(4)], axis=2))
        hc = np.ascontiguousarray(np.concatenate([hT[b * 4 + s][:, :, TS:] for s in range(4)], axis=2))
        maps.append(dict(hl=hl, hc=hc, w=w, lg=lg, ident=ident, masks=masks))
    res = _run(nc, maps)
    y0 = [np.asarray(r["y"]) for r in res]

    nc = _prog("B", lambda: build_tok(TS, 0, True, True, False))
    aw_g0, ab_g0 = awg(0)
    aw_m1, ab_m1 = awm(1)
    maps = []
    for core in range(NCORES):
        b, seg = core // 4, core % 4
        y = np.ascontiguousarray(np.concatenate([y0[b * 4 + g][seg * TS:(seg + 1) * TS] for g in range(4)], axis=1))
        maps.append(dict(x=np.ascontiguousarray(inp["x"][b, seg * TS:(seg + 1) * TS]), y=y,
                         w_out=np.ascontiguousarray(inp["hg_w_out"][0]),
                         wsc=np.ascontiguousarray(inp["hg_norm_g"][0].reshape(128, 1)),
                         cvec=np.ascontiguousarray(_col(inp["c"][b])[:, :, None]), ident=ident,
                         aw_g=aw_g0, ab_g=ab_g0, aw_m=aw_m1, ab_m=ab_m1, ng=_col(inp["norm_g"][1])))
    res = _run(nc, maps)
    x1 = [np.asarray(r["xo"]) for r in res]
    h1T = [np.asarray(r["hT"]) for r in res]

    nc = _prog("C", build_fourier)
    cs, gt, fb = fourier_tables()
    w_in = inp["ft_w_in"][0]
    maps = []
    for core in range(NCORES):
        b, gp = core // 4, core % 4
        wu = np.ascontiguousarray(w_in[:, gp * 512:(gp + 1) * 512].reshape(8, 128, 512).transpose(1, 0, 2))
        wz = np.ascontiguousarray(w_in[:, E + gp * 512:E + (gp + 1) * 512].reshape(8, 128, 512).transpose(1, 0, 2))
        hfull = np.concatenate([h1T[b * 4 + s] for s in range(4)], axis=2)
        hp = np.ascontiguousarray(hfull.reshape(8, 128, 128, 64).transpose(0, 1, 3, 2))
        maps.append(dict(hp=hp, wu=wu, wz=wz, cs=cs, gt=gt, fb=fb, ident=ident))
    res = _run(nc, maps)
    y1 = [np.asarray(r["yg"]) for r in res]

    nc = _prog("D", lambda: build_tok(TS, 0, True, False, True))
    aw_g1, ab_g1 = awg(1)
    maps = []
    for core in range(NCORES):
        b, seg = core // 4, core % 4
        y = np.ascontiguousarray(np.concatenate([y1[b * 4 + g][seg * TS:(seg + 1) * TS] for g in range(4)], axis=1))
        maps.append(dict(x=x1[core], y=y, w_out=np.ascontiguousarray(inp["ft_w_out"][0]),
                         wsc=np.ones((128, 1), np.float32),
                         cvec=np.ascontiguousarray(_col(inp["c"][b])[:, :, None]), ident=ident,
                         aw_g=aw_g1, ab_g=ab_g1, fg=np.ascontiguousarray(inp["final_g"][None, :])))
    res = _run(nc, maps)
    out = np.stack([np.concatenate([np.asarray(res[b * 4 + s]["xo"]) for s in range(4)], axis=0) for b in range(2)])
    return out.astype(np.float32)
```

```python
from contextlib import ExitStack
import os
import numpy as np
import ml_dtypes
import concourse.bass as bass
import concourse.mybir as mybir
from concourse.bass_utils import run_bass_kernel_spmd

F32 = mybir.dt.float32
BF16 = mybir.dt.bfloat16
AF = mybir.ActivationFunctionType
ALU = mybir.AluOpType
NPBF = ml_dtypes.bfloat16

D = 1024
E = 2048
SEQ = 8192
CTX = 256
NCORES = 8
EPS = 1e-6

SAME_ENGINE_SYNC = True
N_DMA_SEMS = 24


class Reg:
    __slots__ = ("name", "w", "r")

    def __init__(self, name=""):
        self.name = name
        self.w = None
        self.r = []


class Prog:
    ENGS = ("pe", "act", "dve", "pool", "sp")

    def __init__(self, nc):
        self.nc = nc
        self.q = {e: [] for e in self.ENGS}
        self.cnt = {e: 0 for e in self.ENGS}
        self.seen = {e: {} for e in self.ENGS}
        self.pend = {e: ([], []) for e in self.ENGS}
        self.dma_cnt = {}
        self.dma_key = {}
        self.stack = ExitStack()
        self.n_ops = 0

    def sb(self, name, shape, dt):
        return self.stack.enter_context(self.nc.sbuf_tensor("sb_" + name, list(shape), dt))

    def ps(self, name, shape, dt):
        return self.stack.enter_context(self.nc.psum_tensor("ps_" + name, list(shape), dt))

    def _waits(self, eng, reads, writes):
        need = {}

        def add(tok):
            if tok is None:
                return
            s, v = tok
            if need.get(s, 0) < v:
                need[s] = v

        for r in reads:
            add(r.w)
        for w in writes:
            add(w.w)
            for t in w.r:
                add(t)
        waits = []
        for s, v in need.items():
            if s == eng and not SAME_ENGINE_SYNC:
                continue
            if self.seen[eng].get(s, 0) >= v:
                continue
            self.seen[eng][s] = v
            waits.append((s, v))
        return waits

    def op(self, eng, fn, reads=(), writes=(), inc=True):
        reads = list(reads)
        writes = list(writes)
        waits = self._waits(eng, reads, writes)
        pr, pw = self.pend[eng]
        pr.extend(reads)
        pw.extend(writes)
        tok = None
        if inc:
            self.cnt[eng] += 1
            tok = (eng, self.cnt[eng])
            for r in pr:
                r.r.append(tok)
            for w in pw:
                w.w = tok
                w.r = []
            self.pend[eng] = ([], [])
        self.q[eng].append((waits, fn, tok, 1))
        self.n_ops += 1

    def dma(self, eng, out, in_, reads=(), writes=(), **kw):
        reads = list(reads)
        writes = list(writes)
        waits = self._waits(eng, reads, writes)
        key = self.dma_key.get(id(writes[0]))
        if key is None:
            key = "dma%d" % len(self.dma_key)
            self.dma_key[id(writes[0])] = key
            self.dma_cnt[key] = 0
        self.dma_cnt[key] += 16
        tok = (key, self.dma_cnt[key])
        for r in reads:
            r.r.append(tok)
        for w in writes:
            w.w = tok
            w.r = []
        self.q[eng].append((waits, lambda e: e.dma_start(out=out, in_=in_, **kw), tok, 16))
        self.n_ops += 1

    def coll(self, kind, ins, outs, groups, reads=(), writes=()):
        reads = list(reads)
        writes = list(writes)
        waits = self._waits("pool", reads, writes)
        key = "dma%d" % len(self.dma_key)
        self.dma_key[id(writes[0])] = key
        self.dma_cnt[key] = 16
        tok = (key, 16)
        for r in reads:
            r.r.append(tok)
        for w in writes:
            w.w = tok
            w.r = []
        self.q["pool"].append((waits, lambda e: e.collective_compute(kind, ALU.bypass, replica_groups=groups,
                                                                     ins=list(ins), outs=list(outs)), tok, 16))
        self.n_ops += 1

    def wait(self, eng, regs):
        waits = self._waits(eng, list(regs), list(regs))
        self.q[eng].append((waits, None, None, 0))

    def emit(self):
        nc = self.nc
        names = ["pe", "act", "dve", "pool"] + list(self.dma_cnt.keys())
        sems = {n: self.stack.enter_context(nc.semaphore("s_" + n)) for n in names}
        block = self.stack.enter_context(nc.Block())
        attr = {"pe": "tensor", "act": "scalar", "dve": "vector", "pool": "gpsimd", "sp": "sync"}
        for eng in self.ENGS:
            q = self.q[eng]

            def body(e, q=q):
                for waits, fn, tok, amt in q:
                    for s, v in waits:
                        e.wait_ge(sems[s], v)
                    if fn is None:
                        continue
                    ins = fn(e)
                    if tok is not None:
                        ins.then_inc(sems[tok[0]], amt)

            getattr(block, attr[eng])(body)

    def close(self):
        self.stack.close()


def build_tok(ntok, nctx, has_outproj, has_normmod, has_final):
    nc = bass.Bass("TRN2", target_bir_lowering=False)
    P = Prog(nc)
    NV = 2 if nctx else 1
    ntot = ntok + nctx
    dr = {}

    def din(name, shape, dt):
        dr[name] = nc.dram_tensor(name, list(shape), dt, kind="ExternalInput").ap()
        return dr[name]

    def dout(name, shape, dt):
        dr[name] = nc.dram_tensor(name, list(shape), dt, kind="ExternalOutput").ap()
        return dr[name]

    x_d = din("x", [ntok, D], F32)
    cv_d = din("cvec", [128, 8, NV], F32)
    id_d = din("ident", [128, 128], BF16)
    if has_outproj:
        y_d = din("y", [ntok, E], BF16)
        w_d = din("w_out", [E, D], F32)
        awg_d = din("aw_g", [128, 8, D], F32)
        wsc_d = din("wsc", [128, 1], F32)
        abg_d = din("ab_g", [1, D], F32)
    if has_normmod:
        awm_d = din("aw_m", [128, 8, 2 * D], F32)
        abm_d = din("ab_m", [128, 16], F32)
        ng_d = din("ng", [128, 8], F32)
        hT_d = dout("hT", [8, 128, ntot], BF16)
    if has_final:
        fg_d = din("fg", [1, D], F32)
    if nctx:
        xc_d = din("xc", [nctx, D], F32)
    if has_outproj or has_final:
        xo_d = dout("xo", [ntok, D], F32)

    ident = P.sb("ident_sb", [128, 128], BF16)
    cvec = P.sb("cvec_sb", [128, 8, NV], F32)
    scv = P.sb("scv_sb", [128, 8, NV], F32)
    zeros = P.sb("zeros_sb", [128, 128], F32)
    ones1 = P.sb("ones1_sb", [1, 128], F32)
    epst = P.sb("eps_sb", [128, 1], F32)
    stage = P.sb("stage_sb", [128, 8, D], F32)
    r_ident, r_cvec, r_scv, r_zeros, r_ones, r_eps = (Reg() for _ in range(6))
    r_stage = [Reg() for _ in range(8)]

    ps = [P.ps("psb%d" % i, [128, 512], F32) for i in range(4)]
    r_ps = [Reg() for _ in range(4)]
    pst = [P.ps("pst%d" % i, [128, 1024], BF16) for i in range(2)]
    r_pst = [Reg() for _ in range(2)]

    P.dma("sp", ident[:], id_d, writes=[r_ident])
    P.dma("sp", cvec[:], cv_d, writes=[r_cvec])
    P.op("dve", lambda e: e.memset(zeros[:], 0.0), writes=[r_zeros])
    P.op("dve", lambda e: e.memset(ones1[:], 1.0), writes=[r_ones])
    P.op("dve", lambda e: e.memset(epst[:], EPS), writes=[r_eps])
    P.op("act", lambda e: e.activation(out=scv[:], in_=cvec[:], func=AF.Silu), reads=[r_cvec], writes=[r_scv])

    if has_outproj:
        gtb = P.sb("gtb_sb", [128, D], F32)
        r_gtb = Reg()
        scb = P.sb("scb_sb", [128, 8, 128], F32)
        r_scb = Reg()
        abg = P.sb("abg_sb", [1, D], F32)
        r_abg = Reg()
        P.dma("sp", abg[:], abg_d, writes=[r_abg])
        for kc in range(8):
            P.op("act", lambda e, kc=kc: e.activation(out=scb[:, kc, :], in_=zeros[:], func=AF.Identity,
                                                       bias=scv[:, kc, 0:1], scale=1.0),
                 reads=[r_zeros, r_scv], writes=[r_scb], inc=(kc == 7))
        for kc in range(8):
            P.dma("sp", stage[:, kc, :], awg_d[:, kc, :], writes=[r_stage[kc]])
        for hf in range(2):
            for kc in range(8):
                P.op("pe", lambda e, kc=kc, hf=hf: e.matmul(ps[hf][:], lhsT=scb[:, kc, :],
                                                            rhs=stage[:, kc, hf * 512:(hf + 1) * 512],
                                                            start=(kc == 0), stop=False),
                     reads=[r_scb, r_stage[kc]], writes=[r_ps[hf]], inc=False)
            P.op("pe", lambda e, hf=hf: e.matmul(ps[hf][:], lhsT=ones1[:], rhs=abg[:, hf * 512:(hf + 1) * 512],
                                                 start=False, stop=True),
                 reads=[r_ones, r_abg], writes=[r_ps[hf]])
            P.op("dve", lambda e, hf=hf: e.tensor_copy(out=gtb[:, hf * 512:(hf + 1) * 512], in_=ps[hf][:]),
                 reads=[r_ps[hf]], writes=[r_gtb])

    if has_normmod:
        ngc = P.sb("ngc_sb", [128, 8], F32)
        abm = P.sb("abm_sb", [128, 16], F32)
        mcol = P.sb("mcol_sb", [128, 16, NV], F32)
        acol = P.sb("acol_sb", [128, 8, NV], F32)
        r_ngc, r_abm, r_mcol, r_acol = Reg(), Reg(), Reg(), Reg()
        P.dma("sp", ngc[:], ng_d, writes=[r_ngc])
        P.dma("sp", abm[:], abm_d, writes=[r_abm])
        pcol = ps[2]
        for half in range(2):
            for kc in range(8):
                P.dma("sp", stage[:, kc, :], awm_d[:, kc, half * D:(half + 1) * D], writes=[r_stage[kc]])
            for fc in range(8):
                for kc in range(8):
                    P.op("pe", lambda e, kc=kc, fc=fc, half=half: e.matmul(
                        pcol[:, (half * 8 + fc) * NV:(half * 8 + fc + 1) * NV],
                        lhsT=stage[:, kc, fc * 128:(fc + 1) * 128], rhs=scv[:, kc, :],
                        start=(kc == 0), stop=(kc == 7)),
                         reads=[r_stage[kc], r_scv], writes=[r_ps[2]], inc=(kc == 7 and fc == 7))
        for v in range(NV):
            P.op("dve", lambda e, v=v: e.tensor_tensor(
                out=mcol[:, :, v], in0=pcol[:, 0:16 * NV].rearrange("p (f v) -> p f v", v=NV)[:, :, v],
                in1=abm[:], op=ALU.add), reads=[r_ps[2], r_abm], writes=[r_mcol])
            P.op("dve", lambda e, v=v: e.scalar_tensor_tensor(
                out=acol[:, :, v], in0=mcol[:, 8:16, v], scalar=1.0, in1=ngc[:], op0=ALU.add, op1=ALU.mult),
                 reads=[r_mcol, r_ngc], writes=[r_acol])

    if has_final:
        fgb = P.sb("fgb_sb", [128, D], F32)
        fgr = P.sb("fgr_sb", [1, D], F32)
        r_fgb, r_fgr = Reg(), Reg()
        P.dma("sp", fgr[:], fg_d, writes=[r_fgr])
        for hf in range(2):
            P.op("pe", lambda e, hf=hf: e.matmul(ps[hf][:], lhsT=ones1[:], rhs=fgr[:, hf * 512:(hf + 1) * 512],
                                                 start=True, stop=True),
                 reads=[r_ones, r_fgr], writes=[r_ps[hf]])
            P.op("dve", lambda e, hf=hf: e.tensor_copy(out=fgb[:, hf * 512:(hf + 1) * 512], in_=ps[hf][:]),
                 reads=[r_ps[hf]], writes=[r_fgb])

    if has_outproj:
        wbf = P.sb("wbf_sb", [128, 16, D], BF16)
        wsc = P.sb("wsc_sb", [128, 1], F32)
        r_wsc = Reg()
        P.dma("sp", wsc[:], wsc_d, writes=[r_wsc])
        r_wbf = [Reg() for _ in range(16)]
        for ec in range(16):
            s = ec % 8
            P.dma("sp", stage[:, s, :], w_d[ec * 128:(ec + 1) * 128, :], writes=[r_stage[s]])
            eng = "pool" if ec % 2 else "dve"
            P.op(eng, lambda e, ec=ec, s=s: e.tensor_scalar(out=wbf[:, ec, :], in0=stage[:, s, :], scalar1=wsc[:, 0:1],
                                                            scalar2=None, op0=ALU.mult),
                 reads=[r_stage[s], r_wsc], writes=[r_wbf[ec]])

    NB = 2
    xt = [P.sb("xt%d" % i, [128, D], F32) for i in range(NB)]
    r_xt = [Reg() for _ in range(NB)]
    xn = [P.sb("xn%d" % i, [128, D], F32) for i in range(NB)]
    r_xn = [Reg() for _ in range(NB)]
    sq = P.sb("sq_sb", [128, D], F32)
    r_sq = Reg()
    stat = [P.sb("stat%d" % i, [128, 4], F32) for i in range(NB)]
    r_stat = [Reg() for _ in range(NB)]
    if has_outproj:
        yt = [P.sb("yt%d" % i, [128, E], BF16) for i in range(NB)]
        r_yt = [Reg() for _ in range(NB)]
        yT = [P.sb("yT%d" % i, [128, 16, 128], BF16) for i in range(NB)]
        r_yT = [Reg() for _ in range(NB)]
    if has_normmod:
        xb = [P.sb("xb%d" % i, [128, D], BF16) for i in range(NB)]
        r_xb = [Reg() for _ in range(NB)]
        hTt = [P.sb("hTt%d" % i, [128, 8, 128], BF16) for i in range(NB)]
        r_hTt = [Reg() for _ in range(NB)]
    if has_final:
        ot = [P.sb("ot%d" % i, [128, D], F32) for i in range(NB)]
        r_ot = [Reg() for _ in range(NB)]
    r_out = Reg()

    tiles = [(False, t * 128, min(128, ntok - t * 128)) for t in range((ntok + 127) // 128)]
    tiles += [(True, t * 128, min(128, nctx - t * 128)) for t in range((nctx + 127) // 128)]
    for it, (is_ctx, t0, n) in enumerate(tiles):
        b = it % NB
        src = xc_d if is_ctx else x_d
        v = 1 if is_ctx else 0
        P.dma("sp", xt[b][:n, :], src[t0:t0 + n, :], writes=[r_xt[b]])
        cur, r_cur = xt[b], r_xt[b]
        if has_outproj:
            P.dma("sp", yt[b][:n, :], y_d[t0:t0 + n, :], writes=[r_yt[b]])
            for g in range(2):
                for j in range(8):
                    ec = g * 8 + j
                    P.op("pe", lambda e, b=b, g=g, j=j, ec=ec, n=n: e.transpose(
                        out=pst[g][:, j * 128:j * 128 + n], in_=yt[b][:n, ec * 128:(ec + 1) * 128],
                        identity=ident[:n, :n]),
                         reads=[r_yt[b], r_ident], writes=[r_pst[g]], inc=(j == 7))
                eng = "act" if g == 0 else "dve"
                if eng == "act":
                    P.op("act", lambda e, b=b, g=g, n=n: e.copy(
                        out=yT[b][:, g * 8:(g + 1) * 8, :n],
                        in_=pst[g][:, :].rearrange("p (j t) -> p j t", t=128)[:, :, :n]),
                         reads=[r_pst[g]], writes=[r_yT[b]])
                else:
                    P.op("dve", lambda e, b=b, g=g, n=n: e.tensor_copy(
                        out=yT[b][:, g * 8:(g + 1) * 8, :n],
                        in_=pst[g][:, :].rearrange("p (j t) -> p j t", t=128)[:, :, :n]),
                         reads=[r_pst[g]], writes=[r_yT[b]])
            for hf in range(2):
                for ec in range(16):
                    P.op("pe", lambda e, b=b, hf=hf, ec=ec, n=n: e.matmul(
                        ps[hf][:n, :], lhsT=yT[b][:, ec, :n], rhs=wbf[:, ec, hf * 512:(hf + 1) * 512],
                        start=(ec == 0), stop=(ec == 15)),
                         reads=[r_yT[b], r_wbf[ec]], writes=[r_ps[hf]], inc=(ec == 15))
                P.op("dve", lambda e, b=b, hf=hf, n=n: e.tensor_tensor(
                    out=xn[b][:n, hf * 512:(hf + 1) * 512], in0=ps[hf][:n, :],
                    in1=gtb[:n, hf * 512:(hf + 1) * 512], op=ALU.mult),
                     reads=[r_ps[hf], r_gtb], writes=[r_xn[b]])
            P.op("pool", lambda e, b=b, n=n: e.tensor_tensor(
                out=xn[b][:n, :], in0=xn[b][:n, :], in1=xt[b][:n, :], op=ALU.add),
                 reads=[r_xn[b], r_xt[b]], writes=[r_xn[b]])
            cur, r_cur = xn[b], r_xn[b]
            if has_normmod:
                P.dma("pool", xo_d[t0:t0 + n, :], xn[b][:n, :], reads=[r_xn[b]], writes=[r_out])
        P.op("act", lambda e, b=b, n=n, cur=cur: e.activation(
            out=sq[:n, :], in_=cur[:n, :], func=AF.Square, accum_out=stat[b][:n, 0:1]),
             reads=[r_cur], writes=[r_sq, r_stat[b]])
        P.op("act", lambda e, b=b, n=n: e.activation(
            out=stat[b][:n, 1:2], in_=stat[b][:n, 0:1], func=AF.Ln, bias=epst[:n, :], scale=1.0 / D),
             reads=[r_stat[b], r_eps], writes=[r_stat[b]])
        P.op("act", lambda e, b=b, n=n: e.activation(
            out=stat[b][:n, 2:3], in_=stat[b][:n, 1:2], func=AF.Exp, scale=-0.5),
             reads=[r_stat[b]], writes=[r_stat[b]])
        if has_normmod:
            P.op("dve", lambda e, b=b, n=n, cur=cur: e.tensor_scalar(
                out=xb[b][:n, :], in0=cur[:n, :], scalar1=stat[b][:n, 2:3], scalar2=None, op0=ALU.mult),
                 reads=[r_cur, r_stat[b]], writes=[r_xb[b]])
            for j in range(8):
                P.op("pe", lambda e, b=b, j=j, n=n: e.transpose(
                    out=pst[0][:, j * 128:j * 128 + n], in_=xb[b][:n, j * 128:(j + 1) * 128],
                    identity=ident[:n, :n]),
                     reads=[r_xb[b], r_ident], writes=[r_pst[0]], inc=(j == 7))
            for j in range(8):
                eng = "act" if j % 2 == 0 else "dve"
                if eng == "act":
                    P.op("act", lambda e, b=b, j=j, n=n, v=v: e.activation(
                        out=hTt[b][:, j, :n], in_=pst[0][:, j * 128:j * 128 + n], func=AF.Identity,
                        bias=mcol[:, j, v:v + 1], scale=acol[:, j, v:v + 1]),
                         reads=[r_pst[0], r_mcol, r_acol], writes=[r_hTt[b]])
                else:
                    P.op("dve", lambda e, b=b, j=j, n=n, v=v: e.tensor_scalar(
                        out=hTt[b][:, j, :n], in0=pst[0][:, j * 128:j * 128 + n],
                        scalar1=acol[:, j, v:v + 1], scalar2=mcol[:, j, v:v + 1], op0=ALU.mult, op1=ALU.add),
                         reads=[r_pst[0], r_mcol, r_acol], writes=[r_hTt[b]])
            c0 = (ntok + t0) if is_ctx else t0
            P.dma("pool", hT_d[:, :, c0:c0 + n].rearrange("k p t -> p k t"), hTt[b][:, :, :n],
                  reads=[r_hTt[b]], writes=[r_out])
        if has_final:
            P.op("dve", lambda e, b=b, n=n, cur=cur: e.scalar_tensor_tensor(
                out=ot[b][:n, :], in0=cur[:n, :], scalar=stat[b][:n, 2:3], in1=fgb[:n, :],
                op0=ALU.mult, op1=ALU.mult), reads=[r_cur, r_stat[b], r_fgb], writes=[r_ot[b]])
            P.dma("pool", xo_d[t0:t0 + n, :], ot[b][:n, :], reads=[r_ot[b]], writes=[r_out])
    P.wait("sp", [r_out])
    P.emit()
    return nc, P


def _col(v):
    v = np.asarray(v, np.float32)
    return np.ascontiguousarray(v.reshape(-1, 128).T)


def _run(nc, in_maps):
    res = run_bass_kernel_spmd(nc, in_maps, core_ids=list(range(NCORES)))
    return res.results


def build_hgrn(nheads=4, nlat=SEQ, nctx=CTX, upto=3):
    nc = bass.Bass("TRN2", target_bir_lowering=False)
    P = Prog(nc)
    NCH = nlat // 128
    NCC = nctx // 128
    NBLK = nlat // 512

    def din(name, shape, dt):
        return nc.dram_tensor(name, list(shape), dt, kind="ExternalInput").ap()

    hl_d = din("hl", [8, 128, nlat], BF16)
    hc_d = din("hc", [8, 128, nctx], BF16)
    w_d = din("w", [128, 8, nheads, 640], F32)
    lg_d = din("lg", [128, 2, 2, nheads], F32)
    id_d = din("ident", [128, 128], BF16)
    mk_d = din("masks", [128, 2, 128], F32)
    y_d = nc.dram_tensor("y", [nlat, nheads * 128], BF16, kind="ExternalOutput").ap()

    ident = P.sb("ident", [128, 128], BF16); r_ident = Reg()
    masks = P.sb("masks", [128, 2, 128], F32); r_masks = Reg()
    lg = P.sb("lg", [128, 2, 2, nheads], F32); r_lg = Reg()
    lb = P.sb("lb", [128, 2, nheads], F32)
    oml = P.sb("oml", [128, 2, nheads], F32)
    noml = P.sb("noml", [128, 2, nheads], F32)
    r_lb = Reg()
    ones = P.sb("ones", [128, 512], F32); r_ones = Reg()
    epst = P.sb("epst", [128, 1], F32); r_eps = Reg()
    wst = [P.sb("wst%d" % i, [128, 640], F32) for i in range(2)]; r_wst = [Reg(), Reg()]
    wbf = P.sb("wbf", [128, 8, 640], BF16); r_wbf = Reg()
    hblk = [P.sb("hblk%d" % i, [128, 8, 512], BF16) for i in range(2)]
    r_hblk = [Reg() for _ in range(2)]
    QF = P.sb("QF", [128, nlat], BF16); KF = P.sb("KF", [128, nlat + nctx], BF16)
    QB = P.sb("QB", [128, nlat], BF16); KB = P.sb("KB", [128, nlat + nctx], BF16)
    V = P.sb("V", [128, NCH + NCC, 128], BF16)
    ZG = P.sb("ZG", [128, NCH, 128], BF16)
    SB = P.sb("SB", [128, NCH, 128], BF16)
    SF = P.sb("SF", [128, NCH, 128], BF16)
    es = [P.sb("es%d" % d, [128, NCH + NCC, 3], F32) for d in range(2)]
    r_QF = [Reg() for _ in range(NBLK)]; r_QB = [Reg() for _ in range(NBLK)]
    r_KF = [Reg() for _ in range(NBLK + 1)]; r_KB = [Reg() for _ in range(NBLK + 1)]
    r_V = [Reg() for _ in range(NBLK + 1)]; r_ZG = [Reg() for _ in range(NBLK)]
    r_es = [[Reg() for _ in range(NBLK + 1)] for _ in range(2)]
    r_SB = [Reg() for _ in range(NCH)]
    r_SF = [Reg() for _ in range(NCH)]
    qf = [P.sb("qf%d" % i, [128, 512], F32) for i in range(2)]; r_qf = [Reg(), Reg()]
    sg = [[P.sb("sg%d_%d" % (d, i), [128, 512], F32) for i in range(2)] for d in range(2)]
    gg = [[P.sb("gg%d_%d" % (d, i), [128, 512], F32) for i in range(2)] for d in range(2)]
    kk = [[P.sb("kk%d_%d" % (d, i), [128, 512], F32) for i in range(2)] for d in range(2)]
    r_sg = [[Reg(), Reg()] for _ in range(2)]; r_gg = [[Reg(), Reg()] for _ in range(2)]
    r_kk = [[Reg(), Reg()] for _ in range(2)]
    Bc = [[P.sb("Bc%d_%d" % (d, i), [128, 513], F32) for i in range(2)] for d in range(2)]
    r_Bc = [[Reg(), Reg()] for _ in range(2)]
    Rt = [[P.sb("Rt%d_%d" % (d, i), [128, 4, 2], F32) for i in range(2)] for d in range(2)]
    r_Rt = [[Reg(), Reg()] for _ in range(2)]
    dd = [[P.sb("dd%d_%d" % (d, i), [128, 4, 3], F32) for i in range(2)] for d in range(2)]
    r_dd = [[Reg(), Reg()] for _ in range(2)]
    Sd = [P.sb("S%d" % i, [128, 128], F32) for i in range(2)]; r_Sd = [Reg(), Reg()]
    kh = [P.sb("kh%d" % i, [128, 128], BF16) for i in range(2)]; r_kh = [Reg(), Reg()]
    kT = [P.sb("kT%d" % i, [128, 128], BF16) for i in range(2)]; r_kT = [Reg(), Reg()]
    Am = [[P.sb("Am%d_%d" % (d, i), [128, 128], BF16) for i in range(2)] for d in range(2)]
    r_Am = [[Reg(), Reg()] for _ in range(2)]
    sq = P.sb("sq", [128, 128], F32); r_sq = Reg()
    st = [P.sb("st%d" % i, [128, 3, 4], F32) for i in range(2)]; r_st = [Reg(), Reg()]
    y4 = [P.sb("y4_%d" % i, [128, 4, 128], BF16) for i in range(2)]; r_y4 = [Reg(), Reg()]
    r_out = Reg()
    r_ser = [Reg(), Reg()]

    pb = [P.ps("pb%d" % i, [128, 512], F32) for i in range(8)]; r_pb = [Reg() for _ in range(8)]
    pt = [pb[6][:, :].bitcast(BF16), pb[7][:, :].bitcast(BF16)]; r_pt = r_pb[6:8]
    psc2 = [pb[0:3], pb[3:6]]; r_psc2 = [r_pb[0:3], r_pb[3:6]]
    ptm2 = [pb[6][:, :].rearrange("p (j c) -> p j c", c=256), pb[7][:, :].rearrange("p (j c) -> p j c", c=256)]
    r_ptm2 = r_pb[6:8]
    ppo = [pb[4][:, :].rearrange("p (j c) -> p j c", c=128), pb[5][:, :].rearrange("p (j c) -> p j c", c=128)]
    r_ppo = r_pb[4:6]

    P.dma("sp", ident[:], id_d, writes=[r_ident])
    P.dma("sp", masks[:], mk_d, writes=[r_masks])
    P.dma("sp", lg[:], lg_d, writes=[r_lg])
    P.op("dve", lambda e: e.memset(ones[:], 1.0), writes=[r_ones])
    P.op("dve", lambda e: e.memset(epst[:], EPS), writes=[r_eps])
    P.op("dve", lambda e: e.tensor_tensor(out=lb[:], in0=lg[:, 0, :, :], in1=lg[:, 1, :, :], op=ALU.subtract),
         reads=[r_lg], writes=[r_lb])
    P.op("act", lambda e: e.activation(out=lb[:], in_=lb[:], func=AF.Sigmoid), reads=[r_lb], writes=[r_lb])
    P.op("dve", lambda e: e.tensor_scalar(out=oml[:], in0=lb[:], scalar1=-1.0, scalar2=1.0, op0=ALU.mult, op1=ALU.add),
         reads=[r_lb], writes=[r_lb])
    P.op("dve", lambda e: e.tensor_scalar(out=noml[:], in0=lb[:], scalar1=-1.0, scalar2=None, op0=ALU.add),
         reads=[r_lb], writes=[r_lb])

    def load_w(h):
        for kc in range(8):
            P.dma("sp", wst[kc % 2][:], w_d[:, kc, h, :], writes=[r_wst[kc % 2]])
            eng = ("dve", "act")[kc % 2]
            P.op(eng, (lambda e, kc=kc: e.tensor_copy(out=wbf[:, kc, :], in_=wst[kc % 2][:])) if eng == "dve" else
                 (lambda e, kc=kc: e.copy(out=wbf[:, kc, :], in_=wst[kc % 2][:])),
                 reads=[r_wst[kc % 2]], writes=[r_wbf])

    blk_i = [0]

    def mk_block(h, is_ctx, t0, n):
        bb = blk_i[0] % 2
        blk_i[0] += 1
        return dict(h=h, is_ctx=is_ctx, t0=t0, n=n, nch=n // 128, bb=bb,
                    bi=NBLK if is_ctx else t0 // 512, c0=NCH if is_ctx else t0 // 128,
                    kcol0=nlat if is_ctx else t0, first=is_ctx or t0 == 0)

    def stage_A1(c):
        n, bb, is_ctx = c["n"], c["bb"], c["is_ctx"]
        psc, r_psc = psc2[bb], r_psc2[bb]
        src = hc_d if is_ctx else hl_d
        P.dma("sp", hblk[bb][:, :, :n], src[:, :, c["t0"]:c["t0"] + n].rearrange("k p t -> p k t"), writes=[r_hblk[bb]])
        for s_ in ((1, 2) if is_ctx else (0, 1, 2)):
            for kc in range(8):
                P.op("pe", lambda e, s_=s_, kc=kc: e.matmul(psc[s_][:, :n], lhsT=wbf[:, kc, s_ * 128:(s_ + 1) * 128],
                                                            rhs=hblk[bb][:, kc, :n], start=(kc == 0), stop=(kc == 7)),
                     reads=[r_wbf, r_hblk[bb]], writes=[r_psc[s_]], inc=(kc == 7))
        if not is_ctx:
            P.op("act", lambda e: e.activation(out=qf[bb][:, :n], in_=psc[0][:, :n], func=AF.Silu),
                 reads=[r_psc[0]], writes=[r_qf[bb]])

    def stage_Tmm(c):
        n, bb, is_ctx, nch = c["n"], c["bb"], c["is_ctx"], c["nch"]
        ncol = 128 if is_ctx else 256
        for cp in range(nch // 2):
            for j in range(2):
                cc_ = cp * 2 + j
                for kc in range(8):
                    P.op("pe", lambda e, j=j, cc_=cc_, kc=kc, cp=cp: e.matmul(
                        ptm2[cp][:, j, :ncol], lhsT=hblk[bb][:, kc, cc_ * 128:(cc_ + 1) * 128],
                        rhs=wbf[:, kc, 384:384 + ncol], start=(kc == 0), stop=(kc == 7)),
                         reads=[r_wbf, r_hblk[bb]], writes=[r_ptm2[cp]], inc=(kc == 7 and j == 1))

    def stage_Tev(c):
        is_ctx, nch, bi, c0 = c["is_ctx"], c["nch"], c["bi"], c["c0"]
        for cp in range(nch // 2):
            cc = c0 + cp * 2
            P.op("act", lambda e, cc=cc, cp=cp: e.copy(out=V[:, cc:cc + 2, :], in_=ptm2[cp][:, :, 0:128]),
                 reads=[r_ptm2[cp]], writes=[r_V[bi]])
            if not is_ctx:
                P.op("act", lambda e, cc=cc, cp=cp: e.activation(out=ZG[:, cc:cc + 2, :], in_=ptm2[cp][:, :, 128:256],
                                                                 func=AF.Silu),
                     reads=[r_ptm2[cp]], writes=[r_ZG[bi]])

    def stage_A2(c):
        n, bb, is_ctx, nch, h = c["n"], c["bb"], c["is_ctx"], c["nch"], c["h"]
        psc, r_psc = psc2[bb], r_psc2[bb]
        for d in range(2):
            P.op("act", lambda e, d=d: e.activation(out=sg[d][bb][:, :n], in_=psc[1 + d][:, :n], func=AF.Sigmoid),
                 reads=[r_psc[1 + d]], writes=[r_sg[d][bb]])
        for d in range(2):
            P.op("act", lambda e, d=d: e.activation(out=gg[d][bb][:, :n], in_=sg[d][bb][:, :n], func=AF.Ln,
                                                    bias=lb[:, d, h:h + 1], scale=oml[:, d, h:h + 1]),
                 reads=[r_sg[d][bb], r_lb], writes=[r_gg[d][bb]])
            P.op("dve", lambda e, d=d: e.tensor_scalar(out=kk[d][bb][:, :n], in0=sg[d][bb][:, :n],
                                                       scalar1=noml[:, d, h:h + 1], scalar2=oml[:, d, h:h + 1],
                                                       op0=ALU.mult, op1=ALU.add),
                 reads=[r_sg[d][bb], r_lb], writes=[r_kk[d][bb]])
        for d in range(2):
            cur, prv = Bc[d][bb], Bc[d][1 - bb]
            if c["first"]:
                P.op("dve", lambda e, cur=cur: e.memset(cur[:, 0:1], 0.0), writes=[r_Bc[d][bb]])
                P.op("dve", lambda e, cur=cur, d=d: e.tensor_tensor_scan(
                    out=cur[:, 1:1 + n], data0=ones[:, :n], data1=gg[d][bb][:, :n], initial=0.0,
                    op0=ALU.mult, op1=ALU.add), reads=[r_ones, r_gg[d][bb]], writes=[r_Bc[d][bb]])
            else:
                P.op("dve", lambda e, cur=cur, prv=prv: e.tensor_copy(out=cur[:, 0:1], in_=prv[:, 512:513]),
                     reads=[r_Bc[d][1 - bb]], writes=[r_Bc[d][bb]])
                P.op("dve", lambda e, cur=cur, prv=prv, d=d: e.tensor_tensor_scan(
                    out=cur[:, 1:1 + n], data0=ones[:, :n], data1=gg[d][bb][:, :n], initial=prv[:, 512:513],
                    op0=ALU.mult, op1=ALU.add), reads=[r_ones, r_gg[d][bb], r_Bc[d][1 - bb]], writes=[r_Bc[d][bb]])
            lo = cur[:, 0:n].rearrange("p (c t) -> p c t", t=128)
            hi = cur[:, 1:1 + n].rearrange("p (c t) -> p c t", t=128)
            P.op("dve", lambda e, lo=lo, d=d: e.tensor_copy(out=Rt[d][bb][:, :nch, 0], in_=lo[:, :, 64]),
                 reads=[r_Bc[d][bb]], writes=[r_Rt[d][bb]])
            P.op("dve", lambda e, lo=lo, hi=hi, d=d: e.tensor_tensor(out=dd[d][bb][:, :nch, 0], in0=hi[:, :, 127],
                                                                     in1=lo[:, :, 0], op=ALU.subtract),
                 reads=[r_Bc[d][bb]], writes=[r_dd[d][bb]])
            P.op("dve", lambda e, lo=lo, hi=hi, d=d: e.tensor_tensor(out=dd[d][bb][:, :nch, 1], in0=hi[:, :, 127],
                                                                     in1=lo[:, :, 64], op=ALU.subtract),
                 reads=[r_Bc[d][bb]], writes=[r_dd[d][bb]])
            P.op("dve", lambda e, lo=lo, d=d: e.tensor_tensor(out=dd[d][bb][:, :nch, 2], in0=lo[:, :, 64],
                                                              in1=lo[:, :, 0], op=ALU.subtract),
                 reads=[r_Bc[d][bb]], writes=[r_dd[d][bb]])
            bsl = cur[:, 1:1 + n] if d == 0 else cur[:, 0:n]
            P.op("dve", lambda e, d=d, bsl=bsl: e.tensor_tensor(
                out=gg[d][bb][:, :n].rearrange("p (c t) -> p c t", t=128), in0=bsl.rearrange("p (c t) -> p c t", t=128),
                in1=Rt[d][bb][:, :nch, 0:1].broadcast_to([128, nch, 128]), op=ALU.subtract),
                 reads=[r_Bc[d][bb], r_Rt[d][bb]], writes=[r_gg[d][bb]])

    def stage_B(c):
        n, bb, is_ctx, nch, bi, c0, t0, kcol0 = (c["n"], c["bb"], c["is_ctx"], c["nch"], c["bi"], c["c0"], c["t0"],
                                                 c["kcol0"])
        for d in range(2):
            P.op("act", lambda e, d=d: e.activation(out=es[d][:, c0:c0 + nch, :], in_=dd[d][bb][:, :nch, :], func=AF.Exp),
                 reads=[r_dd[d][bb]], writes=[r_es[d][bi]])
            qs, ks = (1.0, -1.0) if d == 0 else (-1.0, 1.0)
            P.op("act", lambda e, d=d, ks=ks: e.activation(out=sg[d][bb][:, :n], in_=gg[d][bb][:, :n], func=AF.Exp, scale=ks),
                 reads=[r_gg[d][bb]], writes=[r_sg[d][bb]])
            if not is_ctx:
                P.op("act", lambda e, d=d, qs=qs: e.activation(out=gg[d][bb][:, :n], in_=gg[d][bb][:, :n], func=AF.Exp,
                                                               scale=qs),
                     reads=[r_gg[d][bb]], writes=[r_gg[d][bb]])
        for d in range(2):
            Kd, r_Kd = (KF, r_KF) if d == 0 else (KB, r_KB)
            P.op("dve", lambda e, d=d, Kd=Kd: e.tensor_tensor(out=Kd[:, kcol0:kcol0 + n], in0=kk[d][bb][:, :n],
                                                               in1=sg[d][bb][:, :n], op=ALU.mult),
                 reads=[r_kk[d][bb], r_sg[d][bb]], writes=[r_Kd[bi]])
            if not is_ctx:
                Qd, r_Qd = (QF, r_QF) if d == 0 else (QB, r_QB)
                P.op("dve", lambda e, d=d, Qd=Qd: e.tensor_tensor(out=Qd[:, t0:t0 + n], in0=qf[bb][:, :n],
                                                                in1=gg[d][bb][:, :n], op=ALU.mult),
                     reads=[r_qf[bb], r_gg[d][bb]], writes=[r_Qd[bi]])

    def pass1(h):
        blocks = [mk_block(h, True, 0, nctx)] + [mk_block(h, False, b_ * 512, 512) for b_ in range(NBLK)]
        prev = None
        for c in blocks:
            if prev is not None:
                stage_Tev(prev)
            stage_A1(c)
            stage_Tmm(c)
            stage_A2(c)
            if prev is not None:
                stage_B(prev)
            prev = c
        stage_Tev(prev)
        stage_B(prev)

    def state_step(d, ck, kcol, bi):
        Kd, r_Kd = (KF, r_KF) if d == 0 else (KB, r_KB)
        i1_, i2_ = (0, 1) if d == 0 else (0, 2)
        P.op("act", lambda e: e.activation(out=kh[d][:], in_=Kd[:, kcol:kcol + 128], func=AF.Identity,
                                           scale=es[d][:, ck, i2_:i2_ + 1]),
             reads=[r_Kd[bi], r_es[d][bi]], writes=[r_kh[d]])
        P.op("pe", lambda e: e.transpose(out=pt[d][:, 0:128], in_=kh[d][:], identity=ident[:]),
             reads=[r_kh[d], r_ident], writes=[r_pt[d]])
        P.op("act", lambda e: e.copy(out=kT[d][:], in_=pt[d][:, 0:128]), reads=[r_pt[d]], writes=[r_kT[d]])
        P.op("pe", lambda e: e.matmul(pb[d][:, 0:128], lhsT=kT[d][:], rhs=V[:, ck, :], start=True, stop=True),
             reads=[r_kT[d], r_V[bi]], writes=[r_pb[d]])
        P.op("dve", lambda e: e.scalar_tensor_tensor(out=Sd[d][:], in0=Sd[d][:], scalar=es[d][:, ck, i1_:i1_ + 1],
                                                     in1=pb[d][:, 0:128], op0=ALU.mult, op1=ALU.add),
             reads=[r_pb[d], r_es[d][bi], r_Sd[d]], writes=[r_Sd[d]])

    def chain_step(d, item, last):
        kind, c = item
        if kind == "lat":
            dst, r_dst, i3 = (SF, r_SF, 2) if d == 0 else (SB, r_SB, 1)
            P.op("act", lambda e: e.activation(out=dst[:, c, :], in_=Sd[d][:], func=AF.Identity,
                                               scale=es[d][:, c, i3:i3 + 1]),
                 reads=[r_Sd[d], r_es[d][c // 4]], writes=[r_dst[c]])
            if not last:
                state_step(d, c, c * 128, c // 4)
        else:
            state_step(d, NCH + c, nlat + c * 128, NBLK)

    for h in range(nheads):
        load_w(h)
        pass1(h)
        if upto < 2:
            continue
        for d in range(2):
            P.op("dve", lambda e, d=d: e.memset(Sd[d][:], 0.0), writes=[r_Sd[d]])
        seq_f = [("ctx", c) for c in range(NCC)] + [("lat", n) for n in range(NCH)]
        seq_b = [("ctx", c) for c in range(NCC - 1, -1, -1)] + [("lat", n) for n in range(NCH - 1, -1, -1)]
        for i in range(len(seq_f)):
            chain_step(0, seq_f[i], i == len(seq_f) - 1)
            chain_step(1, seq_b[i], i == len(seq_b) - 1)
        if upto < 3:
            continue
        def stage_X(n):
                bi = n // 4
                g, j = n // 4, n % 4
                gb = g % 2
                sl = n % 2
                csl = slice(n * 128, (n + 1) * 128)
                c0_, c1_, c2_ = n * 128, n * 128 + 64, (n + 1) * 128
                bf_, bb_ = sl * 2, sl * 2 + 1
                P.op("pe", lambda e, c0_=c0_, c1_=c1_, c2_=c2_, bf_=bf_: e.matmul(
                    pb[bf_][0:64, 0:128], lhsT=KF[:, c0_:c1_], rhs=QF[:, c0_:c2_], start=True, stop=True),
                     reads=[r_KF[bi], r_QF[bi]], writes=[r_pb[bf_]], inc=False)
                P.op("pe", lambda e, c0_=c0_, c1_=c1_, c2_=c2_, bf_=bf_: e.matmul(
                    pb[bf_][64:128, 64:128], lhsT=KF[:, c1_:c2_], rhs=QF[:, c1_:c2_], start=True, stop=True),
                     reads=[r_KF[bi], r_QF[bi]], writes=[r_pb[bf_]])
                P.op("pe", lambda e, c0_=c0_, c1_=c1_, c2_=c2_, bb_=bb_: e.matmul(
                    pb[bb_][64:128, 0:128], lhsT=KB[:, c1_:c2_], rhs=QB[:, c0_:c2_], start=True, stop=True),
                     reads=[r_KB[bi], r_QB[bi]], writes=[r_pb[bb_]], inc=False)
                P.op("pe", lambda e, c0_=c0_, c1_=c1_, c2_=c2_, bb_=bb_: e.matmul(
                    pb[bb_][0:64, 0:64], lhsT=KB[:, c0_:c1_], rhs=QB[:, c0_:c1_], start=True, stop=True),
                     reads=[r_KB[bi], r_QB[bi]], writes=[r_pb[bb_]])
                for d, bk in ((0, bf_), (1, bb_)):
                    P.op("dve", lambda e, d=d, sl=sl, bk=bk: e.tensor_tensor(out=Am[d][sl][:], in0=pb[bk][:, 0:128],
                                                                             in1=masks[:, d, :], op=ALU.mult),
                         reads=[r_pb[bk], r_masks], writes=[r_Am[d][sl]])

        def stage_Y(n):
                bi = n // 4
                g, j = n // 4, n % 4
                gb = g % 2
                sl = n % 2
                csl = slice(n * 128, (n + 1) * 128)
                c0_, c1_, c2_ = n * 128, n * 128 + 64, (n + 1) * 128
                bf_, bb_ = sl * 2, sl * 2 + 1
                P.op("pe", lambda e, sl=sl, gb=gb, j=j, n=n: e.matmul(ppo[gb][:, j, :], lhsT=Am[0][sl][:], rhs=V[:, n, :],
                                                                      start=True, stop=False),
                     reads=[r_Am[0][sl], r_V[bi]], writes=[r_ppo[gb]], inc=False)
                P.op("pe", lambda e, sl=sl, gb=gb, j=j, n=n: e.matmul(ppo[gb][:, j, :], lhsT=Am[1][sl][:], rhs=V[:, n, :],
                                                                      start=False, stop=False),
                     reads=[r_Am[1][sl]], writes=[r_ppo[gb]], inc=False)
                P.op("pe", lambda e, gb=gb, j=j, csl=csl, n=n: e.matmul(ppo[gb][:, j, :], lhsT=QF[:, csl], rhs=SF[:, n, :],
                                                                        start=False, stop=False),
                     reads=[r_QF[bi], r_SF[n]], writes=[r_ppo[gb]], inc=False)
                P.op("pe", lambda e, gb=gb, j=j, csl=csl, n=n: e.matmul(ppo[gb][:, j, :], lhsT=QB[:, csl], rhs=SB[:, n, :],
                                                                        start=False, stop=True),
                     reads=[r_QB[bi], r_SB[n]], writes=[r_ppo[gb]])
                P.op("act", lambda e, gb=gb, j=j: e.activation(out=sq[:], in_=ppo[gb][:, j, :], func=AF.Square,
                                                               accum_out=st[gb][:, 0, j:j + 1]),
                     reads=[r_ppo[gb]], writes=[r_sq, r_st[gb]])
                if j == 3:
                    P.op("act", lambda e, gb=gb: e.activation(out=st[gb][:, 1, :], in_=st[gb][:, 0, :], func=AF.Ln,
                                                              bias=epst[:], scale=1.0 / 128),
                         reads=[r_st[gb], r_eps], writes=[r_st[gb]])
                    P.op("act", lambda e, gb=gb: e.activation(out=st[gb][:, 2, :], in_=st[gb][:, 1, :], func=AF.Exp,
                                                              scale=-0.5),
                         reads=[r_st[gb]], writes=[r_st[gb]])
                    for jj in range(4):
                        nn = g * 4 + jj
                        P.op("dve", lambda e, gb=gb, jj=jj, nn=nn: e.scalar_tensor_tensor(
                            out=y4[gb][:, jj, :], in0=ppo[gb][:, jj, :], scalar=st[gb][:, 2, jj:jj + 1],
                            in1=ZG[:, nn, :], op0=ALU.mult, op1=ALU.mult),
                             reads=[r_ppo[gb], r_st[gb], r_ZG[bi]], writes=[r_y4[gb]])
                    P.dma("pool", y_d[g * 512:(g + 1) * 512, h * 128:(h + 1) * 128].rearrange("(j p) v -> p j v", p=128),
                          y4[gb][:], reads=[r_y4[gb]], writes=[r_out])

        stage_X(0)
        for n in range(NCH):
            if n + 1 < NCH:
                stage_X(n + 1)
            stage_Y(n)
    P.wait("sp", [r_out])
    P.emit()
    return nc, P


def _consts():
    ident = np.eye(128, dtype=NPBF)
    s = np.arange(128)[:, None]
    t = np.arange(128)[None, :]
    masks = np.stack([(s <= t), (s >= t)], axis=1).astype(np.float32)
    return ident, np.ascontiguousarray(masks)


def hgrn_maps(inp, hl, hc):
    ident, masks = _consts()
    w_in = np.asarray(inp["hg_w_in"][0])
    lgt = np.asarray(inp["hg_lb_logits"])
    maps = []
    for core in range(NCORES):
        b, hg = core // 4, core % 4
        w5 = w_in.reshape(8, 128, 5, 16, 128)[:, :, :, hg * 4:(hg + 1) * 4, :]
        w = np.ascontiguousarray(w5.transpose(1, 0, 3, 2, 4).reshape(128, 8, 4, 640))
        lg = lgt.reshape(2, 2, 16, 128)[:, :, hg * 4:(hg + 1) * 4, :]
        lg = np.ascontiguousarray(lg.transpose(3, 0, 1, 2))
        hlT = np.ascontiguousarray(np.asarray(hl[b]).reshape(SEQ, 8, 128).transpose(1, 2, 0))
        hcT = np.ascontiguousarray(np.asarray(hc[b]).reshape(CTX, 8, 128).transpose(1, 2, 0))
        maps.append(dict(hl=hlT, hc=hcT, w=w, lg=lg, ident=ident, masks=masks))
    return maps


def build_fourier():
    nc = bass.Bass("TRN2", target_bir_lowering=False)
    P = Prog(nc)

    def din(name, shape, dt):
        return nc.dram_tensor(name, list(shape), dt, kind="ExternalInput").ap()

    hp_d = din("hp", [8, 128, 64, 128], BF16)
    wu_d = din("wu", [128, 8, 512], F32)
    wz_d = din("wz", [128, 8, 512], F32)
    cs_d = din("cs", [128, 2, 512], BF16)
    gt_d = din("gt", [64, 128, 512], BF16)
    fb_d = din("fb", [128, 2, 128], BF16)
    id_d = din("ident", [128, 128], BF16)
    y_d = nc.dram_tensor("yg", [SEQ, 512], BF16, kind="ExternalOutput").ap()

    ident = P.sb("ident", [128, 128], BF16); r_ident = Reg()
    cs = P.sb("cs", [128, 2, 512], BF16); r_cs = Reg()
    fb = P.sb("fb", [128, 2, 128], BF16); r_fb = Reg()
    stg = [P.sb("stg%d" % i, [128, 512], F32) for i in range(2)]; r_stg = [Reg(), Reg()]
    wubk = [P.sb("wubk%d" % i, [128, 512], BF16) for i in range(2)]; r_wubk = [Reg(), Reg()]
    WuT = P.sb("WuT", [128, 4, 1024], BF16); r_WuT = Reg()
    Wp = P.sb("Wp", [128, 8, 1024], BF16); r_Wp = Reg()
    wzb = P.sb("wzb", [128, 8, 512], BF16); r_wzb = Reg()
    hblk = [P.sb("hblk%d" % i, [128, 8, 2, 128], BF16) for i in range(2)]; r_hblk = [Reg(), Reg()]
    Zsb = [P.sb("Zsb%d" % i, [128, 1024], BF16) for i in range(2)]; r_Zsb = [Reg(), Reg()]
    gtab = [P.sb("gtab%d" % i, [128, 512], BF16) for i in range(2)]; r_gtab = [Reg(), Reg()]
    Abuf = P.sb("Abuf", [128, 4, 2, 128, 64], BF16)
    r_Ab = [Reg() for _ in range(64)]
    ATs = [P.sb("ATs%d" % i, [128, 2, 4, 128], BF16) for i in range(2)]; r_ATs = [Reg(), Reg()]
    zs = [P.sb("zs%d" % i, [128, 512], F32) for i in range(2)]; r_zs = [Reg(), Reg()]
    ygt = [P.sb("ygt%d" % i, [128, 512], BF16) for i in range(2)]; r_ygt = [Reg(), Reg()]
    r_out = Reg()

    pb = [P.ps("pb%d" % i, [128, 512], F32) for i in range(6)]; r_pb = [Reg() for _ in range(6)]
    pt = [P.ps("pt%d" % i, [128, 1024], BF16) for i in range(2)]; r_pt = [Reg(), Reg()]

    P.dma("sp", ident[:], id_d, writes=[r_ident])
    P.dma("sp", cs[:], cs_d, writes=[r_cs])
    P.dma("sp", fb[:], fb_d, writes=[r_fb])

    for kc in range(8):
        s = kc % 2
        P.dma("sp", stg[s][:], wu_d[:, kc, :], writes=[r_stg[s]])
        P.op("dve", lambda e, s=s: e.tensor_copy(out=wubk[s][:], in_=stg[s][:]), reads=[r_stg[s]], writes=[r_wubk[s]])
        for jb in range(4):
            P.op("pe", lambda e, s=s, jb=jb: e.transpose(out=pt[s][:, jb * 128:(jb + 1) * 128],
                                                         in_=wubk[s][:, jb * 128:(jb + 1) * 128], identity=ident[:]),
                 reads=[r_wubk[s], r_ident], writes=[r_pt[s]], inc=(jb == 3))
        P.op("act", lambda e, s=s, kc=kc: e.copy(out=WuT[:, :, kc * 128:(kc + 1) * 128],
                                                 in_=pt[s][:, 0:512].rearrange("p (j k) -> p j k", k=128)),
             reads=[r_pt[s]], writes=[r_WuT])
    for kc in range(8):
        s = kc % 2
        P.dma("sp", stg[s][:], wz_d[:, kc, :], writes=[r_stg[s]])
        P.op("pool", lambda e, s=s, kc=kc: e.tensor_copy(out=wzb[:, kc, :], in_=stg[s][:]),
             reads=[r_stg[s]], writes=[r_wzb])
    i = 0
    for g in range(2):
        for kc in range(8):
            bk = i % 2
            i += 1
            for jc in range(2):
                P.op("pe", lambda e, g=g, kc=kc, jc=jc, bk=bk: e.matmul(
                    pb[bk][:], lhsT=WuT[:, g * 2 + jc, kc * 128:(kc + 1) * 128], rhs=cs[:, jc, :],
                    start=(jc == 0), stop=(jc == 1)), reads=[r_WuT, r_cs], writes=[r_pb[bk]], inc=(jc == 1))
            outv = Wp[:, kc, :].rearrange("p (c g m) -> p c g m", c=2, g=2)[:, :, g, :]
            inv = pb[bk][:, :].rearrange("p (c m) -> p c m", c=2)
            if bk == 0:
                P.op("act", lambda e, outv=outv, inv=inv: e.copy(out=outv, in_=inv), reads=[r_pb[bk]], writes=[r_Wp])
            else:
                P.op("dve", lambda e, outv=outv, inv=inv: e.tensor_copy(out=outv, in_=inv), reads=[r_pb[bk]], writes=[r_Wp])

    for bp in range(32):
        hb = bp % 2
        P.dma("sp", hblk[hb][:], hp_d[:, :, 2 * bp:2 * bp + 2, :].rearrange("k p b a -> p k b a"), writes=[r_hblk[hb]])
        for bj in range(2):
            b = 2 * bp + bj
            zb = b % 2
            P.dma("sp", gtab[zb][:], gt_d[b], writes=[r_gtab[zb]])
            for half in range(2):
                for kc in range(8):
                    P.op("pe", lambda e, hb=hb, bj=bj, half=half, kc=kc: e.matmul(
                        pb[half][:], lhsT=hblk[hb][:, kc, bj, :], rhs=Wp[:, kc, half * 512:(half + 1) * 512],
                        start=(kc == 0), stop=(kc == 7)), reads=[r_hblk[hb], r_Wp], writes=[r_pb[half]], inc=(kc == 7))
            P.op("act", lambda e, zb=zb: e.copy(out=Zsb[zb][:, 0:512], in_=pb[0][:]), reads=[r_pb[0]], writes=[r_Zsb[zb]])
            P.op("dve", lambda e, zb=zb: e.tensor_copy(out=Zsb[zb][:, 512:1024], in_=pb[1][:]),
                 reads=[r_pb[1]], writes=[r_Zsb[zb]])
            for mp in range(2):
                bank = 2 + mp
                for mj in range(2):
                    mb = mp * 2 + mj
                    P.op("pe", lambda e, zb=zb, mb=mb, mj=mj, bank=bank: e.matmul(
                        pb[bank][:, mj * 256:(mj + 1) * 256], lhsT=Zsb[zb][:, mb * 128:(mb + 1) * 128],
                        rhs=gtab[zb][:, 0:256], start=True, stop=False),
                         reads=[r_Zsb[zb], r_gtab[zb]], writes=[r_pb[bank]], inc=False)
                    P.op("pe", lambda e, zb=zb, mb=mb, mj=mj, bank=bank: e.matmul(
                        pb[bank][:, mj * 256:(mj + 1) * 256], lhsT=Zsb[zb][:, 512 + mb * 128:512 + (mb + 1) * 128],
                        rhs=gtab[zb][:, 256:512], start=False, stop=True),
                         reads=[r_Zsb[zb], r_gtab[zb]], writes=[r_pb[bank]], inc=(mj == 1))
                outv = Abuf[:, mp * 2:mp * 2 + 2, :, :, b]
                inv = pb[bank][:, :].rearrange("p (mb ri k) -> p mb ri k", mb=2, ri=2)
                if mp == 0:
                    P.op("act", lambda e, outv=outv, inv=inv: e.copy(out=outv, in_=inv), reads=[r_pb[bank]], writes=[r_Ab[b]])
                else:
                    P.op("dve", lambda e, outv=outv, inv=inv: e.tensor_copy(out=outv, in_=inv),
                         reads=[r_pb[bank]], writes=[r_Ab[b]])

    for pr in range(64):
        s = pr % 2
        prm, par = pr % 32, pr // 32
        P.dma("sp", hblk[s][:], hp_d[:, :, 2 * prm:2 * prm + 2, :].rearrange("k p b a -> p k b a"), writes=[r_hblk[s]])
        for kc in range(8):
            lhs = hblk[s][:, kc, :, :].rearrange("p b (k2 two) -> p (b k2) two", two=2)[:, :, par]
            P.op("pe", lambda e, kc=kc, lhs=lhs: e.matmul(pb[4][:], lhsT=lhs, rhs=wzb[:, kc, :],
                                                          start=(kc == 0), stop=(kc == 7)),
                 reads=[r_hblk[s], r_wzb], writes=[r_pb[4]], inc=(kc == 7))
        P.op("act", lambda e, s=s: e.activation(out=zs[s][:], in_=pb[4][:], func=AF.Silu), reads=[r_pb[4]], writes=[r_zs[s]])
        for ri in range(2):
            for mb in range(4):
                src = Abuf[:, mb, ri, 2 * pr:2 * pr + 2, :].rearrange("p k b -> p (k b)")
                P.op("pe", lambda e, s=s, ri=ri, mb=mb, src=src: e.transpose(
                    out=pt[s][:, (ri * 4 + mb) * 128:(ri * 4 + mb + 1) * 128], in_=src, identity=ident[:]),
                     reads=r_Ab + [r_ident] if (ri == 0 and mb == 0) else [r_ident], writes=[r_pt[s]],
                     inc=(ri == 1 and mb == 3))
        P.op("dve", lambda e, s=s: e.tensor_copy(out=ATs[s][:].rearrange("p r m c -> p (r m c)"), in_=pt[s][:]),
             reads=[r_pt[s]], writes=[r_ATs[s]])
        for ri in range(2):
            P.op("pe", lambda e, s=s, ri=ri: e.matmul(pb[5][:], lhsT=fb[:, ri, :],
                                                      rhs=ATs[s][:, ri, :, :].rearrange("p m c -> p (m c)"),
                                                      start=(ri == 0), stop=(ri == 1)),
                 reads=[r_fb, r_ATs[s]], writes=[r_pb[5]], inc=(ri == 1))
        P.op("dve", lambda e, s=s: e.tensor_tensor(out=ygt[s][:], in0=pb[5][:], in1=zs[s][:], op=ALU.mult),
             reads=[r_pb[5], r_zs[s]], writes=[r_ygt[s]])
        yv = y_d.rearrange("(k2 r) c -> r k2 c", r=128)
        for kap in range(2):
            P.dma("pool", yv[2 * pr + kap], ygt[s][kap * 64:(kap + 1) * 64, :], reads=[r_ygt[s]], writes=[r_out])
    P.wait("sp", [r_out])
    P.emit()
    return nc, P


def fourier_tables():
    N = SEQ
    j = np.arange(256)[:, None]; m = np.arange(256)[None, :]
    ang = 2 * np.pi * (j * m % 256) / 256
    C = np.cos(ang) / 16.0; S = np.sin(ang) / 16.0
    cs = np.concatenate([C, S], axis=1).reshape(2, 128, 512).transpose(1, 0, 2)
    a = np.arange(128)[None, :, None]; b = np.arange(64)[:, None, None]; k1 = np.arange(128)[None, None, :]
    th = 2 * np.pi * ((k1 * (64 * a + b)) % N) / N
    Gr = np.cos(th) / np.sqrt(128.0); Gi = -np.sin(th) / np.sqrt(128.0)
    gt = np.concatenate([Gr, Gi, Gi, -Gr], axis=2)
    bb = np.arange(64)[:, None]; k2 = np.arange(64)[None, :]
    ph = 2 * np.pi * ((bb * k2) % 64) / 64
    Fc = np.cos(ph) / 8.0; Fs = np.sin(ph) / 8.0
    fb = np.zeros((128, 2, 128))
    for kap in range(2):
        fb[kap * 64:(kap + 1) * 64, 0, kap * 64:(kap + 1) * 64] = Fc
        fb[kap * 64:(kap + 1) * 64, 1, kap * 64:(kap + 1) * 64] = Fs
    return (np.ascontiguousarray(cs).astype(NPBF), np.ascontiguousarray(gt).astype(NPBF),
            np.ascontiguousarray(fb).astype(NPBF))


def fourier_maps(inp, h1):
    ident, _ = _consts()
    cs, gt, fb = fourier_tables()
    w_in = np.asarray(inp["ft_w_in"][0])
    maps = []
    for core in range(NCORES):
        b, gp = core // 4, core % 4
        wu = np.ascontiguousarray(w_in[:, gp * 512:(gp + 1) * 512].reshape(8, 128, 512).transpose(1, 0, 2))
        wz = np.ascontiguousarray(w_in[:, E + gp * 512:E + (gp + 1) * 512].reshape(8, 128, 512).transpose(1, 0, 2))
        hp = np.ascontiguousarray(np.asarray(h1[b]).reshape(128, 64, 8, 128).transpose(2, 3, 1, 0))
        maps.append(dict(hp=hp, wu=wu, wz=wz, cs=cs, gt=gt, fb=fb, ident=ident))
    return maps


_CACHE = {}


def _prog(key, builder):
    if key not in _CACHE:
        _CACHE[key] = builder()[0]
    return _CACHE[key]


def _ada_maps(inp, layer):
    aw = np.asarray(inp["ada_w"][layer], np.float32)
    ab = np.asarray(inp["ada_b"][layer], np.float32)
    return aw, ab


def kernel(x, c, ctx, c_ctx, ada_w, ada_b, norm_g, hg_w_in, hg_lb_logits, hg_norm_g,
           hg_w_out, ft_w_in, ft_w_out, final_g):
    inp = dict(x=np.asarray(x, np.float32), c=np.asarray(c, np.float32), ctx=np.asarray(ctx, np.float32),
               c_ctx=np.asarray(c_ctx, np.float32), ada_w=np.asarray(ada_w, np.float32),
               ada_b=np.asarray(ada_b, np.float32), norm_g=np.asarray(norm_g, np.float32),
               hg_w_in=np.asarray(hg_w_in, np.float32), hg_lb_logits=np.asarray(hg_lb_logits, np.float32),
               hg_norm_g=np.asarray(hg_norm_g, np.float32), hg_w_out=np.asarray(hg_w_out, np.float32),
               ft_w_in=np.asarray(ft_w_in, np.float32), ft_w_out=np.asarray(ft_w_out, np.float32),
               final_g=np.asarray(final_g, np.float32))
    ident, _ = _consts()
    TS = SEQ // 4
    CS_ = CTX // 4

    def awm(layer):
        aw, ab = _ada_maps(inp, layer)
        return (np.ascontiguousarray(aw[:, :2 * D].reshape(8, 128, 2 * D).transpose(1, 0, 2)), _col(ab[:2 * D]))

    def awg(layer):
        aw, ab = _ada_maps(inp, layer)
        return (np.ascontiguousarray(aw[:, 2 * D:].reshape(8, 128, D).transpose(1, 0, 2)),
                np.ascontiguousarray(ab[None, 2 * D:]))

    nc = _prog("A1", lambda: build_tok(TS, CS_, False, True, False))
    aw_m0, ab_m0 = awm(0)
    maps = []
    for core in range(NCORES):
        b, seg = core // 4, core % 4
        cv = np.stack([_col(inp["c"][b]), _col(inp["c_ctx"])], axis=-1)
        maps.append(dict(x=np.ascontiguousarray(inp["x"][b, seg * TS:(seg + 1) * TS]),
                         xc=np.ascontiguousarray(inp["ctx"][b, seg * CS_:(seg + 1) * CS_]),
                         cvec=np.ascontiguousarray(cv), ident=ident, aw_m=aw_m0, ab_m=ab_m0,
                         ng=_col(inp["norm_g"][0])))
    res = _run(nc, maps)
    hT = [np.asarray(r["hT"]) for r in res]

    nc = _prog("A2", build_hgrn)
    _, masks = _consts()
    w_in = inp["hg_w_in"][0]
    lgt = inp["hg_lb_logits"]
    maps = []
    for core in range(NCORES):
        b, hg = core // 4, core % 4
        w5 = w_in.reshape(8, 128, 5, 16, 128)[:, :, :, hg * 4:(hg + 1) * 4, :]
        w = np.ascontiguousarray(w5.transpose(1, 0, 3, 2, 4).reshape(128, 8, 4, 640))
        lg = np.ascontiguousarray(lgt.reshape(2, 2, 16, 128)[:, :, hg * 4:(hg + 1) * 4, :].transpose(3, 0, 1, 2))
        hl = np.ascontiguousarray(np.concatenate([hT[b * 4 + s][:, :, :TS] for s in range(4)], axis=2))
        hc = np.ascontiguousarray(np.concatenate([hT[b * 4 + s][:, :, TS:] for s in range(4)], axis=2))
        maps.append(dict(hl=hl, hc=hc, w=w, lg=lg, ident=ident, masks=masks))
    res = _run(nc, maps)
    y0 = [np.asarray(r["y"]) for r in res]

    nc = _prog("B", lambda: build_tok(TS, 0, True, True, False))
    aw_g0, ab_g0 = awg(0)
    aw_m1, ab_m1 = awm(1)
    maps = []
    for core in range(NCORES):
        b, seg = core // 4, core % 4
        y = np.ascontiguousarray(np.concatenate([y0[b * 4 + g][seg * TS:(seg + 1) * TS] for g in range(4)], axis=1))
        maps.append(dict(x=np.ascontiguousarray(inp["x"][b, seg * TS:(seg + 1) * TS]), y=y,
                         w_out=np.ascontiguousarray(inp["hg_w_out"][0]),
                         wsc=np.ascontiguousarray(inp["hg_norm_g"][0].reshape(128, 1)),
                         cvec=np.ascontiguousarray(_col(inp["c"][b])[:, :, None]), ident=ident,
                         aw_g=aw_g0, ab_g=ab_g0, aw_m=aw_m1, ab_m=ab_m1, ng=_col(inp["norm_g"][1])))
    res = _run(nc, maps)
    x1 = [np.asarray(r["xo"]) for r in res]
    h1T = [np.asarray(r["hT"]) for r in res]

    nc = _prog("C", build_fourier)
    cs, gt, fb = fourier_tables()
    w_in = inp["ft_w_in"][0]
    maps = []
    for core in range(NCORES):
        b, gp = core // 4, core % 4
        wu = np.ascontiguousarray(w_in[:, gp * 512:(gp + 1) * 512].reshape(8, 128, 512).transpose(1, 0, 2))
        wz = np.ascontiguousarray(w_in[:, E + gp * 512:E + (gp + 1) * 512].reshape(8, 128, 512).transpose(1, 0, 2))
        hfull = np.concatenate([h1T[b * 4 + s] for s in range(4)], axis=2)
        hp = np.ascontiguousarray(hfull.reshape(8, 128, 128, 64).transpose(0, 1, 3, 2))
        maps.append(dict(hp=hp, wu=wu, wz=wz, cs=cs, gt=gt, fb=fb, ident=ident))
    res = _run(nc, maps)
    y1 = [np.asarray(r["yg"]) for r in res]

    nc = _prog("D", lambda: build_tok(TS, 0, True, False, True))
    aw_g1, ab_g1 = awg(1)
    maps = []
    for core in range(NCORES):
        b, seg = core // 4, core % 4
        y = np.ascontiguousarray(np.concatenate([y1[b * 4 + g][seg * TS:(seg + 1) * TS] for g in range(4)], axis=1))
        maps.append(dict(x=x1[core], y=y, w_out=np.ascontiguousarray(inp["ft_w_out"][0]),
                         wsc=np.ones((128, 1), np.float32),
                         cvec=np.ascontiguousarray(_col(inp["c"][b])[:, :, None]), ident=ident,
                         aw_g=aw_g1, ab_g=ab_g1, fg=np.ascontiguousarray(inp["final_g"][None, :])))
    res = _run(nc, maps)
    out = np.stack([np.concatenate([np.asarray(res[b * 4 + s]["xo"]) for s in range(4)], axis=0) for b in range(2)])
    return out.astype(np.float32)
```

```python
from contextlib import ExitStack
import os
import numpy as np
import ml_dtypes
import concourse.bass as bass
import concourse.mybir as mybir
from concourse.bass_utils import run_bass_kernel_spmd

F32 = mybir.dt.float32
BF16 = mybir.dt.bfloat16
AF = mybir.ActivationFunctionType
ALU = mybir.AluOpType
NPBF = ml_dtypes.bfloat16

D = 1024
E = 2048
SEQ = 8192
CTX = 256
NCORES = 8
EPS = 1e-6

SAME_ENGINE_SYNC = True
N_DMA_SEMS = 24


class Reg:
    __slots__ = ("name", "w", "r")

    def __init__(self, name=""):
        self.name = name
        self.w = None
        self.r = []


class Prog:
    ENGS = ("pe", "act", "dve", "pool", "sp")

    def __init__(self, nc):
        self.nc = nc
        self.q = {e: [] for e in self.ENGS}
        self.cnt = {e: 0 for e in self.ENGS}
        self.seen = {e: {} for e in self.ENGS}
        self.pend = {e: ([], []) for e in self.ENGS}
        self.dma_cnt = {}
        self.dma_key = {}
        self.stack = ExitStack()
        self.n_ops = 0

    def sb(self, name, shape, dt):
        return self.stack.enter_context(self.nc.sbuf_tensor("sb_" + name, list(shape), dt))

    def ps(self, name, shape, dt):
        return self.stack.enter_context(self.nc.psum_tensor("ps_" + name, list(shape), dt))

    def _waits(self, eng, reads, writes):
        need = {}

        def add(tok):
            if tok is None:
                return
            s, v = tok
            if need.get(s, 0) < v:
                need[s] = v

        for r in reads:
            add(r.w)
        for w in writes:
            add(w.w)
            for t in w.r:
                add(t)
        waits = []
        for s, v in need.items():
            if s == eng and not SAME_ENGINE_SYNC:
                continue
            if self.seen[eng].get(s, 0) >= v:
                continue
            self.seen[eng][s] = v
            waits.append((s, v))
        return waits

    def op(self, eng, fn, reads=(), writes=(), inc=True):
        reads = list(reads)
        writes = list(writes)
        waits = self._waits(eng, reads, writes)
        pr, pw = self.pend[eng]
        pr.extend(reads)
        pw.extend(writes)
        tok = None
        if inc:
            self.cnt[eng] += 1
            tok = (eng, self.cnt[eng])
            for r in pr:
                r.r.append(tok)
            for w in pw:
                w.w = tok
                w.r = []
            self.pend[eng] = ([], [])
        self.q[eng].append((waits, fn, tok, 1))
        self.n_ops += 1

    def dma(self, eng, out, in_, reads=(), writes=(), **kw):
        reads = list(reads)
        writes = list(writes)
        waits = self._waits(eng, reads, writes)
        key = self.dma_key.get(id(writes[0]))
        if key is None:
            key = "dma%d" % len(self.dma_key)
            self.dma_key[id(writes[0])] = key
            self.dma_cnt[key] = 0
        self.dma_cnt[key] += 16
        tok = (key, self.dma_cnt[key])
        for r in reads:
            r.r.append(tok)
        for w in writes:
            w.w = tok
            w.r = []
        self.q[eng].append((waits, lambda e: e.dma_start(out=out, in_=in_, **kw), tok, 16))
        self.n_ops += 1

    def coll(self, kind, ins, outs, groups, reads=(), writes=()):
        reads = list(reads)
        writes = list(writes)
        waits = self._waits("pool", reads, writes)
        key = "dma%d" % len(self.dma_key)
        self.dma_key[id(writes[0])] = key
        self.dma_cnt[key] = 16
        tok = (key, 16)
        for r in reads:
            r.r.append(tok)
        for w in writes:
            w.w = tok
            w.r = []
        self.q["pool"].append((waits, lambda e: e.collective_compute(kind, ALU.bypass, replica_groups=groups,
                                                                     ins=list(ins), outs=list(outs)), tok, 16))
        self.n_ops += 1

    def wait(self, eng, regs):
        waits = self._waits(eng, list(regs), list(regs))
        self.q[eng].append((waits, None, None, 0))

    def emit(self):
        nc = self.nc
        names = ["pe", "act", "dve", "pool"] + list(self.dma_cnt.keys())
        sems = {n: self.stack.enter_context(nc.semaphore("s_" + n)) for n in names}
        block = self.stack.enter_context(nc.Block())
        attr = {"pe": "tensor", "act": "scalar", "dve": "vector", "pool": "gpsimd", "sp": "sync"}
        for eng in self.ENGS:
            q = self.q[eng]

            def body(e, q=q):
                for waits, fn, tok, amt in q:
                    for s, v in waits:
                        e.wait_ge(sems[s], v)
                    if fn is None:
                        continue
                    ins = fn(e)
                    if tok is not None:
                        ins.then_inc(sems[tok[0]], amt)

            getattr(block, attr[eng])(body)

    def close(self):
        self.stack.close()


def build_tok(ntok, nctx, has_outproj, has_normmod, has_final):
    nc = bass.Bass("TRN2", target_bir_lowering=False)
    P = Prog(nc)
    NV = 2 if nctx else 1
    ntot = ntok + nctx
    dr = {}

    def din(name, shape, dt):
        dr[name] = nc.dram_tensor(name, list(shape), dt, kind="ExternalInput").ap()
        return dr[name]

    def dout(name, shape, dt):
        dr[name] = nc.dram_tensor(name, list(shape), dt, kind="ExternalOutput").ap()
        return dr[name]

    x_d = din("x", [ntok, D], F32)
    cv_d = din("cvec", [128, 8, NV], F32)
    id_d = din("ident", [128, 128], BF16)
    if has_outproj:
        y_d = din("y", [ntok, E], BF16)
        w_d = din("w_out", [E, D], F32)
        awg_d = din("aw_g", [128, 8, D], F32)
        wsc_d = din("wsc", [128, 1], F32)
        abg_d = din("ab_g", [1, D], F32)
    if has_normmod:
        awm_d = din("aw_m", [128, 8, 2 * D], F32)
        abm_d = din("ab_m", [128, 16], F32)
        ng_d = din("ng", [128, 8], F32)
        hT_d = dout("hT", [8, 128, ntot], BF16)
    if has_final:
        fg_d = din("fg", [1, D], F32)
    if nctx:
        xc_d = din("xc", [nctx, D], F32)
    if has_outproj or has_final:
        xo_d = dout("xo", [ntok, D], F32)

    ident = P.sb("ident_sb", [128, 128], BF16)
    cvec = P.sb("cvec_sb", [128, 8, NV], F32)
    scv = P.sb("scv_sb", [128, 8, NV], F32)
    zeros = P.sb("zeros_sb", [128, 128], F32)
    ones1 = P.sb("ones1_sb", [1, 128], F32)
    epst = P.sb("eps_sb", [128, 1], F32)
    stage = P.sb("stage_sb", [128, 8, D], F32)
    r_ident, r_cvec, r_scv, r_zeros, r_ones, r_eps = (Reg() for _ in range(6))
    r_stage = [Reg() for _ in range(8)]

    ps = [P.ps("psb%d" % i, [128, 512], F32) for i in range(4)]
    r_ps = [Reg() for _ in range(4)]
    pst = [P.ps("pst%d" % i, [128, 1024], BF16) for i in range(2)]
    r_pst = [Reg() for _ in range(2)]

    P.dma("sp", ident[:], id_d, writes=[r_ident])
    P.dma("sp", cvec[:], cv_d, writes=[r_cvec])
    P.op("dve", lambda e: e.memset(zeros[:], 0.0), writes=[r_zeros])
    P.op("dve", lambda e: e.memset(ones1[:], 1.0), writes=[r_ones])
    P.op("dve", lambda e: e.memset(epst[:], EPS), writes=[r_eps])
    P.op("act", lambda e: e.activation(out=scv[:], in_=cvec[:], func=AF.Silu), reads=[r_cvec], writes=[r_scv])

    if has_outproj:
        gtb = P.sb("gtb_sb", [128, D], F32)
        r_gtb = Reg()
        scb = P.sb("scb_sb", [128, 8, 128], F32)
        r_scb = Reg()
        abg = P.sb("abg_sb", [1, D], F32)
        r_abg = Reg()
        P.dma("sp", abg[:], abg_d, writes=[r_abg])
        for kc in range(8):
            P.op("act", lambda e, kc=kc: e.activation(out=scb[:, kc, :], in_=zeros[:], func=AF.Identity,
                                                       bias=scv[:, kc, 0:1], scale=1.0),
                 reads=[r_zeros, r_scv], writes=[r_scb], inc=(kc == 7))
        for kc in range(8):
            P.dma("sp", stage[:, kc, :], awg_d[:, kc, :], writes=[r_stage[kc]])
        for hf in range(2):
            for kc in range(8):
                P.op("pe", lambda e, kc=kc, hf=hf: e.matmul(ps[hf][:], lhsT=scb[:, kc, :],
                                                            rhs=stage[:, kc, hf * 512:(hf + 1) * 512],
                                                            start=(kc == 0), stop=False),
                     reads=[r_scb, r_stage[kc]], writes=[r_ps[hf]], inc=False)
            P.op("pe", lambda e, hf=hf: e.matmul(ps[hf][:], lhsT=ones1[:], rhs=abg[:, hf * 512:(hf + 1) * 512],
                                                 start=False, stop=True),
                 reads=[r_ones, r_abg], writes=[r_ps[hf]])
            P.op("dve", lambda e, hf=hf: e.tensor_copy(out=gtb[:, hf * 512:(hf + 1) * 512], in_=ps[hf][:]),
                 reads=[r_ps[hf]], writes=[r_gtb])

    if has_normmod:
        ngc = P.sb("ngc_sb", [128, 8], F32)
        abm = P.sb("abm_sb", [128, 16], F32)
        mcol = P.sb("mcol_sb", [128, 16, NV], F32)
        acol = P.sb("acol_sb", [128, 8, NV], F32)
        r_ngc, r_abm, r_mcol, r_acol = Reg(), Reg(), Reg(), Reg()
        P.dma("sp", ngc[:], ng_d, writes=[r_ngc])
        P.dma("sp", abm[:], abm_d, writes=[r_abm])
        pcol = ps[2]
        for half in range(2):
            for kc in range(8):
                P.dma("sp", stage[:, kc, :], awm_d[:, kc, half * D:(half + 1) * D], writes=[r_stage[kc]])
            for fc in range(8):
                for kc in range(8):
                    P.op("pe", lambda e, kc=kc, fc=fc, half=half: e.matmul(
                        pcol[:, (half * 8 + fc) * NV:(half * 8 + fc + 1) * NV],
                        lhsT=stage[:, kc, fc * 128:(fc + 1) * 128], rhs=scv[:, kc, :],
                        start=(kc == 0), stop=(kc == 7)),
                         reads=[r_stage[kc], r_scv], writes=[r_ps[2]], inc=(kc == 7 and fc == 7))
        for v in range(NV):
            P.op("dve", lambda e, v=v: e.tensor_tensor(
                out=mcol[:, :, v], in0=pcol[:, 0:16 * NV].rearrange("p (f v) -> p f v", v=NV)[:, :, v],
                in1=abm[:], op=ALU.add), reads=[r_ps[2], r_abm], writes=[r_mcol])
            P.op("dve", lambda e, v=v: e.scalar_tensor_tensor(
                out=acol[:, :, v], in0=mcol[:, 8:16, v], scalar=1.0, in1=ngc[:], op0=ALU.add, op1=ALU.mult),
                 reads=[r_mcol, r_ngc], writes=[r_acol])

    if has_final:
        fgb = P.sb("fgb_sb", [128, D], F32)
        fgr = P.sb("fgr_sb", [1, D], F32)
        r_fgb, r_fgr = Reg(), Reg()
        P.dma("sp", fgr[:], fg_d, writes=[r_fgr])
        for hf in range(2):
            P.op("pe", lambda e, hf=hf: e.matmul(ps[hf][:], lhsT=ones1[:], rhs=fgr[:, hf * 512:(hf + 1) * 512],
                                                 start=True, stop=True),
                 reads=[r_ones, r_fgr], writes=[r_ps[hf]])
            P.op("dve", lambda e, hf=hf: e.tensor_copy(out=fgb[:, hf * 512:(hf + 1) * 512], in_=ps[hf][:]),
                 reads=[r_ps[hf]], writes=[r_fgb])

    if has_outproj:
        wbf = P.sb("wbf_sb", [128, 16, D], BF16)
        wsc = P.sb("wsc_sb", [128, 1], F32)
        r_wsc = Reg()
        P.dma("sp", wsc[:], wsc_d, writes=[r_wsc])
        r_wbf = [Reg() for _ in range(16)]
        for ec in range(16):
            s = ec % 8
            P.dma("sp", stage[:, s, :], w_d[ec * 128:(ec + 1) * 128, :], writes=[r_stage[s]])
            if ec % 2:
                P.op("act", lambda e, ec=ec, s=s: e.activation(out=wbf[:, ec, :], in_=stage[:, s, :], func=AF.Identity,
                                                               scale=wsc[:, 0:1]),
                     reads=[r_stage[s], r_wsc], writes=[r_wbf[ec]])
            else:
                P.op("dve", lambda e, ec=ec, s=s: e.tensor_scalar(out=wbf[:, ec, :], in0=stage[:, s, :],
                                                                  scalar1=wsc[:, 0:1], scalar2=None, op0=ALU.mult),
                     reads=[r_stage[s], r_wsc], writes=[r_wbf[ec]])

    NB = 2
    xt = [P.sb("xt%d" % i, [128, D], F32) for i in range(NB)]
    r_xt = [Reg() for _ in range(NB)]
    xn = [P.sb("xn%d" % i, [128, D], F32) for i in range(NB)]
    r_xn = [Reg() for _ in range(NB)]
    sq = P.sb("sq_sb", [128, D], F32)
    r_sq = Reg()
    stat = [P.sb("stat%d" % i, [128, 4], F32) for i in range(NB)]
    r_stat = [Reg() for _ in range(NB)]
    if has_outproj:
        yt = [P.sb("yt%d" % i, [128, E], BF16) for i in range(NB)]
        r_yt = [Reg() for _ in range(NB)]
        yT = [P.sb("yT%d" % i, [128, 16, 128], BF16) for i in range(NB)]
        r_yT = [Reg() for _ in range(NB)]
    if has_normmod:
        xb = [P.sb("xb%d" % i, [128, D], BF16) for i in range(NB)]
        r_xb = [Reg() for _ in range(NB)]
        hTt = [P.sb("hTt%d" % i, [128, 8, 128], BF16) for i in range(NB)]
        r_hTt = [Reg() for _ in range(NB)]
    if has_final:
        ot = [P.sb("ot%d" % i, [128, D], F32) for i in range(NB)]
        r_ot = [Reg() for _ in range(NB)]
    r_out = Reg()

    tiles = [(False, t * 128, min(128, ntok - t * 128)) for t in range((ntok + 127) // 128)]
    tiles += [(True, t * 128, min(128, nctx - t * 128)) for t in range((nctx + 127) // 128)]
    for it, (is_ctx, t0, n) in enumerate(tiles):
        b = it % NB
        src = xc_d if is_ctx else x_d
        v = 1 if is_ctx else 0
        P.dma("sp", xt[b][:n, :], src[t0:t0 + n, :], writes=[r_xt[b]])
        cur, r_cur = xt[b], r_xt[b]
        if has_outproj:
            P.dma("sp", yt[b][:n, :], y_d[t0:t0 + n, :], writes=[r_yt[b]])
            for g in range(2):
                for j in range(8):
                    ec = g * 8 + j
                    P.op("pe", lambda e, b=b, g=g, j=j, ec=ec, n=n: e.transpose(
                        out=pst[g][:, j * 128:j * 128 + n], in_=yt[b][:n, ec * 128:(ec + 1) * 128],
                        identity=ident[:n, :n]),
                         reads=[r_yt[b], r_ident], writes=[r_pst[g]], inc=(j == 7))
                eng = "act" if g == 0 else "dve"
                if eng == "act":
                    P.op("act", lambda e, b=b, g=g, n=n: e.copy(
                        out=yT[b][:, g * 8:(g + 1) * 8, :n],
                        in_=pst[g][:, :].rearrange("p (j t) -> p j t", t=128)[:, :, :n]),
                         reads=[r_pst[g]], writes=[r_yT[b]])
                else:
                    P.op("dve", lambda e, b=b, g=g, n=n: e.tensor_copy(
                        out=yT[b][:, g * 8:(g + 1) * 8, :n],
                        in_=pst[g][:, :].rearrange("p (j t) -> p j t", t=128)[:, :, :n]),
                         reads=[r_pst[g]], writes=[r_yT[b]])
            for hf in range(2):
                for ec in range(16):
                    P.op("pe", lambda e, b=b, hf=hf, ec=ec, n=n: e.matmul(
                        ps[hf][:n, :], lhsT=yT[b][:, ec, :n], rhs=wbf[:, ec, hf * 512:(hf + 1) * 512],
                        start=(ec == 0), stop=(ec == 15)),
                         reads=[r_yT[b], r_wbf[ec]], writes=[r_ps[hf]], inc=(ec == 15))
                P.op("dve", lambda e, b=b, hf=hf, n=n: e.tensor_tensor(
                    out=xn[b][:n, hf * 512:(hf + 1) * 512], in0=ps[hf][:n, :],
                    in1=gtb[:n, hf * 512:(hf + 1) * 512], op=ALU.mult),
                     reads=[r_ps[hf], r_gtb], writes=[r_xn[b]])
            P.op("dve", lambda e, b=b, n=n: e.tensor_tensor(
                out=xn[b][:n, :], in0=xn[b][:n, :], in1=xt[b][:n, :], op=ALU.add),
                 reads=[r_xn[b], r_xt[b]], writes=[r_xn[b]])
            cur, r_cur = xn[b], r_xn[b]
            if has_normmod:
                P.dma("pool", xo_d[t0:t0 + n, :], xn[b][:n, :], reads=[r_xn[b]], writes=[r_out])
        P.op("act", lambda e, b=b, n=n, cur=cur: e.activation(
            out=sq[:n, :], in_=cur[:n, :], func=AF.Square, accum_out=stat[b][:n, 0:1]),
             reads=[r_cur], writes=[r_sq, r_stat[b]])
        P.op("act", lambda e, b=b, n=n: e.activation(
            out=stat[b][:n, 1:2], in_=stat[b][:n, 0:1], func=AF.Ln, bias=epst[:n, :], scale=1.0 / D),
             reads=[r_stat[b], r_eps], writes=[r_stat[b]])
        P.op("act", lambda e, b=b, n=n: e.activation(
            out=stat[b][:n, 2:3], in_=stat[b][:n, 1:2], func=AF.Exp, scale=-0.5),
             reads=[r_stat[b]], writes=[r_stat[b]])
        if has_normmod:
            P.op("dve", lambda e, b=b, n=n, cur=cur: e.tensor_scalar(
                out=xb[b][:n, :], in0=cur[:n, :], scalar1=stat[b][:n, 2:3], scalar2=None, op0=ALU.mult),
                 reads=[r_cur, r_stat[b]], writes=[r_xb[b]])
            for j in range(8):
                P.op("pe", lambda e, b=b, j=j, n=n: e.transpose(
                    out=pst[0][:, j * 128:j * 128 + n], in_=xb[b][:n, j * 128:(j + 1) * 128],
                    identity=ident[:n, :n]),
                     reads=[r_xb[b], r_ident], writes=[r_pst[0]], inc=(j == 7))
            for j in range(8):
                eng = "act" if j % 2 == 0 else "dve"
                if eng == "act":
                    P.op("act", lambda e, b=b, j=j, n=n, v=v: e.activation(
                        out=hTt[b][:, j, :n], in_=pst[0][:, j * 128:j * 128 + n], func=AF.Identity,
                        bias=mcol[:, j, v:v + 1], scale=acol[:, j, v:v + 1]),
                         reads=[r_pst[0], r_mcol, r_acol], writes=[r_hTt[b]])
                else:
                    P.op("dve", lambda e, b=b, j=j, n=n, v=v: e.tensor_scalar(
                        out=hTt[b][:, j, :n], in0=pst[0][:, j * 128:j * 128 + n],
                        scalar1=acol[:, j, v:v + 1], scalar2=mcol[:, j, v:v + 1], op0=ALU.mult, op1=ALU.add),
                         reads=[r_pst[0], r_mcol, r_acol], writes=[r_hTt[b]])
            c0 = (ntok + t0) if is_ctx else t0
            P.dma("pool", hT_d[:, :, c0:c0 + n].rearrange("k p t -> p k t"), hTt[b][:, :, :n],
                  reads=[r_hTt[b]], writes=[r_out])
        if has_final:
            P.op("dve", lambda e, b=b, n=n, cur=cur: e.scalar_tensor_tensor(
                out=ot[b][:n, :], in0=cur[:n, :], scalar=stat[b][:n, 2:3], in1=fgb[:n, :],
                op0=ALU.mult, op1=ALU.mult), reads=[r_cur, r_stat[b], r_fgb], writes=[r_ot[b]])
            P.dma("pool", xo_d[t0:t0 + n, :], ot[b][:n, :], reads=[r_ot[b]], writes=[r_out])
    P.wait("sp", [r_out])
    P.emit()
    return nc, P


def _col(v):
    v = np.asarray(v, np.float32)
    return np.ascontiguousarray(v.reshape(-1, 128).T)


def _run(nc, in_maps):
    res = run_bass_kernel_spmd(nc, in_maps, core_ids=list(range(NCORES)))
    return res.results


def build_hgrn(nheads=4, nlat=SEQ, nctx=CTX, upto=3):
    nc = bass.Bass("TRN2", target_bir_lowering=False)
    P = Prog(nc)
    NCH = nlat // 128
    NCC = nctx // 128
    NBLK = nlat // 512

    def din(name, shape, dt):
        return nc.dram_tensor(name, list(shape), dt, kind="ExternalInput").ap()

    hl_d = din("hl", [8, 128, nlat], BF16)
    hc_d = din("hc", [8, 128, nctx], BF16)
    w_d = din("w", [128, 8, nheads, 640], F32)
    lg_d = din("lg", [128, 2, 2, nheads], F32)
    id_d = din("ident", [128, 128], BF16)
    mk_d = din("masks", [128, 2, 128], F32)
    y_d = nc.dram_tensor("y", [nlat, nheads * 128], BF16, kind="ExternalOutput").ap()

    ident = P.sb("ident", [128, 128], BF16); r_ident = Reg()
    masks = P.sb("masks", [128, 2, 128], F32); r_masks = Reg()
    lg = P.sb("lg", [128, 2, 2, nheads], F32); r_lg = Reg()
    lb = P.sb("lb", [128, 2, nheads], F32)
    oml = P.sb("oml", [128, 2, nheads], F32)
    noml = P.sb("noml", [128, 2, nheads], F32)
    r_lb = Reg()
    ones = P.sb("ones", [128, 512], BF16); r_ones = Reg()
    epst = P.sb("epst", [128, 1], F32); r_eps = Reg()
    wst = [P.sb("wst%d" % i, [128, 640], F32) for i in range(2)]; r_wst = [Reg(), Reg()]
    wbf = P.sb("wbf", [128, 8, 640], BF16); r_wbf = Reg()
    hblk = [P.sb("hblk%d" % i, [128, 8, 512], BF16) for i in range(2)]
    r_hblk = [Reg() for _ in range(2)]
    QF = P.sb("QF", [128, nlat], BF16); KF = P.sb("KF", [128, nlat + nctx], BF16)
    QB = P.sb("QB", [128, nlat], BF16); KB = P.sb("KB", [128, nlat + nctx], BF16)
    V = P.sb("V", [128, NCH + NCC, 128], BF16)
    ZG = P.sb("ZG", [128, NCH, 128], BF16)
    SB = P.sb("SB", [128, NCH, 128], BF16)
    SF = P.sb("SF", [128, NCH, 128], BF16)
    es = [P.sb("es%d" % d, [128, NCH + NCC, 3], F32) for d in range(2)]
    r_QF = [Reg() for _ in range(NBLK)]; r_QB = [Reg() for _ in range(NBLK)]
    r_KF = [Reg() for _ in range(NBLK + 1)]; r_KB = [Reg() for _ in range(NBLK + 1)]
    r_V = [Reg() for _ in range(NBLK + 1)]; r_ZG = [Reg() for _ in range(NBLK)]
    r_es = [[Reg() for _ in range(NBLK + 1)] for _ in range(2)]
    r_SB = [Reg() for _ in range(NCH)]
    r_SF = [Reg() for _ in range(NCH)]
    qf = [P.sb("qf%d" % i, [128, 512], F32) for i in range(2)]; r_qf = [Reg(), Reg()]
    sg = [[P.sb("sg%d_%d" % (d, i), [128, 512], F32) for i in range(2)] for d in range(2)]
    gg = [[P.sb("gg%d_%d" % (d, i), [128, 512], F32) for i in range(2)] for d in range(2)]
    kk = [[P.sb("kk%d_%d" % (d, i), [128, 512], F32) for i in range(2)] for d in range(2)]
    r_sg = [[Reg(), Reg()] for _ in range(2)]; r_gg = [[Reg(), Reg()] for _ in range(2)]
    r_kk = [[Reg(), Reg()] for _ in range(2)]
    Bc = [[P.sb("Bc%d_%d" % (d, i), [128, 513], F32) for i in range(2)] for d in range(2)]
    r_Bc = [[Reg(), Reg()] for _ in range(2)]
    Rt = [[P.sb("Rt%d_%d" % (d, i), [128, 4, 2], F32) for i in range(2)] for d in range(2)]
    r_Rt = [[Reg(), Reg()] for _ in range(2)]
    dd = [[P.sb("dd%d_%d" % (d, i), [128, 4, 3], F32) for i in range(2)] for d in range(2)]
    r_dd = [[Reg(), Reg()] for _ in range(2)]
    Sd = [P.sb("S%d" % i, [128, 128], F32) for i in range(2)]; r_Sd = [Reg(), Reg()]
    kh = [[P.sb("kh%d_%d" % (d, i), [128, 128], BF16) for i in range(2)] for d in range(2)]
    r_kh = [[Reg(), Reg()] for _ in range(2)]
    kT = [[P.sb("kT%d_%d" % (d, i), [128, 128], BF16) for i in range(2)] for d in range(2)]
    r_kT = [[Reg(), Reg()] for _ in range(2)]
    Am = [[P.sb("Am%d_%d" % (d, i), [128, 128], BF16) for i in range(2)] for d in range(2)]
    r_Am = [[Reg(), Reg()] for _ in range(2)]
    sq = P.sb("sq", [128, 128], F32); r_sq = Reg()
    st = [P.sb("st%d" % i, [128, 3, 4], F32) for i in range(2)]; r_st = [Reg(), Reg()]
    y4 = [P.sb("y4_%d" % i, [128, 4, 128], BF16) for i in range(2)]; r_y4 = [Reg(), Reg()]
    r_out = Reg()
    r_ser = [Reg(), Reg()]

    pb = [P.ps("pb%d" % i, [128, 512], F32) for i in range(8)]; r_pb = [Reg() for _ in range(8)]
    pt = [pb[6][:, :].bitcast(BF16), pb[7][:, :].bitcast(BF16)]; r_pt = r_pb[6:8]
    psc2 = [pb[0:3], pb[3:6]]; r_psc2 = [r_pb[0:3], r_pb[3:6]]
    ptm2 = [pb[6][:, :].rearrange("p (j c) -> p j c", c=256), pb[7][:, :].rearrange("p (j c) -> p j c", c=256)]
    r_ptm2 = r_pb[6:8]
    ppo = [pb[4][:, :].rearrange("p (j c) -> p j c", c=128), pb[5][:, :].rearrange("p (j c) -> p j c", c=128)]
    r_ppo = r_pb[4:6]

    P.dma("sp", ident[:], id_d, writes=[r_ident])
    P.dma("sp", masks[:], mk_d, writes=[r_masks])
    P.dma("sp", lg[:], lg_d, writes=[r_lg])
    P.op("dve", lambda e: e.memset(ones[:], 1.0), writes=[r_ones])
    P.op("dve", lambda e: e.memset(epst[:], EPS), writes=[r_eps])
    P.op("dve", lambda e: e.tensor_tensor(out=lb[:], in0=lg[:, 0, :, :], in1=lg[:, 1, :, :], op=ALU.subtract),
         reads=[r_lg], writes=[r_lb])
    P.op("act", lambda e: e.activation(out=lb[:], in_=lb[:], func=AF.Sigmoid), reads=[r_lb], writes=[r_lb])
    P.op("dve", lambda e: e.tensor_scalar(out=oml[:], in0=lb[:], scalar1=-1.0, scalar2=1.0, op0=ALU.mult, op1=ALU.add),
         reads=[r_lb], writes=[r_lb])
    P.op("dve", lambda e: e.tensor_scalar(out=noml[:], in0=lb[:], scalar1=-1.0, scalar2=None, op0=ALU.add),
         reads=[r_lb], writes=[r_lb])

    def load_w(h):
        for kc in range(8):
            P.dma("sp", wst[kc % 2][:], w_d[:, kc, h, :], writes=[r_wst[kc % 2]])
            eng = ("dve", "act")[kc % 2]
            P.op(eng, (lambda e, kc=kc: e.tensor_copy(out=wbf[:, kc, :], in_=wst[kc % 2][:])) if eng == "dve" else
                 (lambda e, kc=kc: e.copy(out=wbf[:, kc, :], in_=wst[kc % 2][:])),
                 reads=[r_wst[kc % 2]], writes=[r_wbf])

    blk_i = [0]

    def mk_block(h, is_ctx, t0, n):
        bb = blk_i[0] % 2
        blk_i[0] += 1
        return dict(h=h, is_ctx=is_ctx, t0=t0, n=n, nch=n // 128, bb=bb,
                    bi=NBLK if is_ctx else t0 // 512, c0=NCH if is_ctx else t0 // 128,
                    kcol0=nlat if is_ctx else t0, first=is_ctx or t0 == 0)

    def stage_A1(c):
        n, bb, is_ctx = c["n"], c["bb"], c["is_ctx"]
        psc, r_psc = psc2[bb], r_psc2[bb]
        src = hc_d if is_ctx else hl_d
        P.dma("sp", hblk[bb][:, :, :n], src[:, :, c["t0"]:c["t0"] + n].rearrange("k p t -> p k t"), writes=[r_hblk[bb]])
        for s_ in ((1, 2) if is_ctx else (0, 1, 2)):
            for kc in range(8):
                P.op("pe", lambda e, s_=s_, kc=kc: e.matmul(psc[s_][:, :n], lhsT=wbf[:, kc, s_ * 128:(s_ + 1) * 128],
                                                            rhs=hblk[bb][:, kc, :n], start=(kc == 0), stop=(kc == 7)),
                     reads=[r_wbf, r_hblk[bb]], writes=[r_psc[s_]], inc=(kc == 7))
        if not is_ctx:
            P.op("act", lambda e: e.activation(out=qf[bb][:, :n], in_=psc[0][:, :n], func=AF.Silu),
                 reads=[r_psc[0]], writes=[r_qf[bb]])

    def stage_Tmm(c):
        n, bb, is_ctx, nch = c["n"], c["bb"], c["is_ctx"], c["nch"]
        ncol = 128 if is_ctx else 256
        for cp in range(nch // 2):
            for j in range(2):
                cc_ = cp * 2 + j
                for kc in range(8):
                    P.op("pe", lambda e, j=j, cc_=cc_, kc=kc, cp=cp: e.matmul(
                        ptm2[cp][:, j, :ncol], lhsT=hblk[bb][:, kc, cc_ * 128:(cc_ + 1) * 128],
                        rhs=wbf[:, kc, 384:384 + ncol], start=(kc == 0), stop=(kc == 7)),
                         reads=[r_wbf, r_hblk[bb]], writes=[r_ptm2[cp]], inc=(kc == 7 and j == 1))

    def stage_Tev(c):
        is_ctx, nch, bi, c0 = c["is_ctx"], c["nch"], c["bi"], c["c0"]
        for cp in range(nch // 2):
            cc = c0 + cp * 2
            P.op("act", lambda e, cc=cc, cp=cp: e.copy(out=V[:, cc:cc + 2, :], in_=ptm2[cp][:, :, 0:128]),
                 reads=[r_ptm2[cp]], writes=[r_V[bi]])
            if not is_ctx:
                P.op("act", lambda e, cc=cc, cp=cp: e.activation(out=ZG[:, cc:cc + 2, :], in_=ptm2[cp][:, :, 128:256],
                                                                 func=AF.Silu),
                     reads=[r_ptm2[cp]], writes=[r_ZG[bi]])

    def stage_A2(c):
        n, bb, is_ctx, nch, h = c["n"], c["bb"], c["is_ctx"], c["nch"], c["h"]
        psc, r_psc = psc2[bb], r_psc2[bb]
        for d in range(2):
            P.op("act", lambda e, d=d: e.activation(out=sg[d][bb][:, :n], in_=psc[1 + d][:, :n], func=AF.Sigmoid),
                 reads=[r_psc[1 + d]], writes=[r_sg[d][bb]])
        for d in range(2):
            P.op("act", lambda e, d=d: e.activation(out=gg[d][bb][:, :n], in_=sg[d][bb][:, :n], func=AF.Ln,
                                                    bias=lb[:, d, h:h + 1], scale=oml[:, d, h:h + 1]),
                 reads=[r_sg[d][bb], r_lb], writes=[r_gg[d][bb]])
            P.op("dve", lambda e, d=d: e.tensor_scalar(out=kk[d][bb][:, :n], in0=sg[d][bb][:, :n],
                                                       scalar1=noml[:, d, h:h + 1], scalar2=oml[:, d, h:h + 1],
                                                       op0=ALU.mult, op1=ALU.add),
                 reads=[r_sg[d][bb], r_lb], writes=[r_kk[d][bb]])
        for d in range(2):
            cur, prv = Bc[d][bb], Bc[d][1 - bb]
            if c["first"]:
                P.op("dve", lambda e, cur=cur: e.memset(cur[:, 0:1], 0.0), writes=[r_Bc[d][bb]])
                P.op("dve", lambda e, cur=cur, d=d: e.tensor_tensor_scan(
                    out=cur[:, 1:1 + n], data0=ones[:, :n], data1=gg[d][bb][:, :n], initial=0.0,
                    op0=ALU.mult, op1=ALU.add), reads=[r_ones, r_gg[d][bb]], writes=[r_Bc[d][bb]])
            else:
                P.op("dve", lambda e, cur=cur, prv=prv: e.tensor_copy(out=cur[:, 0:1], in_=prv[:, 512:513]),
                     reads=[r_Bc[d][1 - bb]], writes=[r_Bc[d][bb]])
                P.op("dve", lambda e, cur=cur, prv=prv, d=d: e.tensor_tensor_scan(
                    out=cur[:, 1:1 + n], data0=ones[:, :n], data1=gg[d][bb][:, :n], initial=prv[:, 512:513],
                    op0=ALU.mult, op1=ALU.add), reads=[r_ones, r_gg[d][bb], r_Bc[d][1 - bb]], writes=[r_Bc[d][bb]])
            lo = cur[:, 0:n].rearrange("p (c t) -> p c t", t=128)
            hi = cur[:, 1:1 + n].rearrange("p (c t) -> p c t", t=128)
            P.op("dve", lambda e, lo=lo, d=d: e.tensor_copy(out=Rt[d][bb][:, :nch, 0], in_=lo[:, :, 64]),
                 reads=[r_Bc[d][bb]], writes=[r_Rt[d][bb]])
            P.op("dve", lambda e, lo=lo, hi=hi, d=d: e.tensor_tensor(out=dd[d][bb][:, :nch, 0], in0=hi[:, :, 127],
                                                                     in1=lo[:, :, 0], op=ALU.subtract),
                 reads=[r_Bc[d][bb]], writes=[r_dd[d][bb]])
            P.op("dve", lambda e, lo=lo, hi=hi, d=d: e.tensor_tensor(out=dd[d][bb][:, :nch, 1], in0=hi[:, :, 127],
                                                                     in1=lo[:, :, 64], op=ALU.subtract),
                 reads=[r_Bc[d][bb]], writes=[r_dd[d][bb]])
            P.op("dve", lambda e, lo=lo, d=d: e.tensor_tensor(out=dd[d][bb][:, :nch, 2], in0=lo[:, :, 64],
                                                              in1=lo[:, :, 0], op=ALU.subtract),
                 reads=[r_Bc[d][bb]], writes=[r_dd[d][bb]])
            bsl = cur[:, 1:1 + n] if d == 0 else cur[:, 0:n]
            P.op("dve", lambda e, d=d, bsl=bsl: e.tensor_tensor(
                out=gg[d][bb][:, :n].rearrange("p (c t) -> p c t", t=128), in0=bsl.rearrange("p (c t) -> p c t", t=128),
                in1=Rt[d][bb][:, :nch, 0:1].broadcast_to([128, nch, 128]), op=ALU.subtract),
                 reads=[r_Bc[d][bb], r_Rt[d][bb]], writes=[r_gg[d][bb]])

    def stage_B(c):
        n, bb, is_ctx, nch, bi, c0, t0, kcol0 = (c["n"], c["bb"], c["is_ctx"], c["nch"], c["bi"], c["c0"], c["t0"],
                                                 c["kcol0"])
        for d in range(2):
            P.op("act", lambda e, d=d: e.activation(out=es[d][:, c0:c0 + nch, :], in_=dd[d][bb][:, :nch, :], func=AF.Exp),
                 reads=[r_dd[d][bb]], writes=[r_es[d][bi]])
            qs, ks = (1.0, -1.0) if d == 0 else (-1.0, 1.0)
            P.op("act", lambda e, d=d, ks=ks: e.activation(out=sg[d][bb][:, :n], in_=gg[d][bb][:, :n], func=AF.Exp, scale=ks),
                 reads=[r_gg[d][bb]], writes=[r_sg[d][bb]])
            if not is_ctx:
                P.op("act", lambda e, d=d, qs=qs: e.activation(out=gg[d][bb][:, :n], in_=gg[d][bb][:, :n], func=AF.Exp,
                                                               scale=qs),
                     reads=[r_gg[d][bb]], writes=[r_gg[d][bb]])
        for d in range(2):
            Kd, r_Kd = (KF, r_KF) if d == 0 else (KB, r_KB)
            P.op("dve", lambda e, d=d, Kd=Kd: e.tensor_tensor(out=Kd[:, kcol0:kcol0 + n], in0=kk[d][bb][:, :n],
                                                               in1=sg[d][bb][:, :n], op=ALU.mult),
                 reads=[r_kk[d][bb], r_sg[d][bb]], writes=[r_Kd[bi]])
            if not is_ctx:
                Qd, r_Qd = (QF, r_QF) if d == 0 else (QB, r_QB)
                P.op("dve", lambda e, d=d, Qd=Qd: e.tensor_tensor(out=Qd[:, t0:t0 + n], in0=qf[bb][:, :n],
                                                                in1=gg[d][bb][:, :n], op=ALU.mult),
                     reads=[r_qf[bb], r_gg[d][bb]], writes=[r_Qd[bi]])

    def pass1(h):
        blocks = [mk_block(h, True, 0, nctx)] + [mk_block(h, False, b_ * 512, 512) for b_ in range(NBLK)]
        prev = None
        for c in blocks:
            if prev is not None:
                stage_Tev(prev)
            stage_A1(c)
            stage_Tmm(c)
            stage_A2(c)
            if prev is not None:
                stage_B(prev)
            prev = c
        stage_Tev(prev)
        stage_B(prev)

    pt4 = [[pb[6][:, :].bitcast(BF16), pb[4][:, :].bitcast(BF16)], [pb[7][:, :].bitcast(BF16), pb[5][:, :].bitcast(BF16)]]
    r_pt4 = [[r_pb[6], r_pb[4]], [r_pb[7], r_pb[5]]]
    pP = [[pb[0], pb[1]], [pb[2], pb[3]]]
    r_pP = [[r_pb[0], r_pb[1]], [r_pb[2], r_pb[3]]]

    def step_info(d, item, last):
        kind, c = item
        if kind == "lat":
            return dict(d=d, lat=True, c=c, ck=c, kcol=c * 128, bi=c // 4, upd=not last)
        return dict(d=d, lat=False, c=c, ck=NCH + c, kcol=nlat + c * 128, bi=NBLK, upd=True)

    def pre_a(st_, sl):
        if not st_["upd"]:
            return
        d, ck, kcol, bi = st_["d"], st_["ck"], st_["kcol"], st_["bi"]
        Kd, r_Kd = (KF, r_KF) if d == 0 else (KB, r_KB)
        i2_ = 1 if d == 0 else 2
        P.op("act", lambda e: e.activation(out=kh[d][sl][:], in_=Kd[:, kcol:kcol + 128], func=AF.Identity,
                                           scale=es[d][:, ck, i2_:i2_ + 1]),
             reads=[r_Kd[bi], r_es[d][bi]], writes=[r_kh[d][sl]])
        P.op("pe", lambda e: e.transpose(out=pt4[d][sl][:, 0:128], in_=kh[d][sl][:], identity=ident[:]),
             reads=[r_kh[d][sl], r_ident], writes=[r_pt4[d][sl]])

    def pre_b(st_, sl):
        if not st_["upd"]:
            return
        d, ck, bi = st_["d"], st_["ck"], st_["bi"]
        P.op("dve", lambda e: e.tensor_copy(out=kT[d][sl][:], in_=pt4[d][sl][:, 0:128]),
             reads=[r_pt4[d][sl]], writes=[r_kT[d][sl]])
        P.op("pe", lambda e: e.matmul(pP[d][sl][:, 0:128], lhsT=kT[d][sl][:], rhs=V[:, ck, :], start=True, stop=True),
             reads=[r_kT[d][sl], r_V[bi]], writes=[r_pP[d][sl]])

    def chain(st_, sl):
        d, ck, bi, c = st_["d"], st_["ck"], st_["bi"], st_["c"]
        if st_["lat"]:
            dst, r_dst, i3 = (SF, r_SF, 2) if d == 0 else (SB, r_SB, 1)
            if d == 0:
                P.op("act", lambda e: e.activation(out=dst[:, c, :], in_=Sd[d][:], func=AF.Identity,
                                                   scale=es[d][:, c, i3:i3 + 1]),
                     reads=[r_Sd[d], r_es[d][bi]], writes=[r_dst[c]])
            else:
                P.op("dve", lambda e: e.tensor_scalar(out=dst[:, c, :], in0=Sd[d][:], scalar1=es[d][:, c, i3:i3 + 1],
                                                      scalar2=None, op0=ALU.mult),
                     reads=[r_Sd[d], r_es[d][bi]], writes=[r_dst[c]])
        if st_["upd"]:
            P.op("dve", lambda e: e.scalar_tensor_tensor(out=Sd[d][:], in0=Sd[d][:], scalar=es[d][:, ck, 0:1],
                                                         in1=pP[d][sl][:, 0:128], op0=ALU.mult, op1=ALU.add),
                 reads=[r_pP[d][sl], r_es[d][bi], r_Sd[d]], writes=[r_Sd[d]])

    for h in range(nheads):
        load_w(h)
        pass1(h)
        if upto < 2:
            continue
        for d in range(2):
            P.op("dve", lambda e, d=d: e.memset(Sd[d][:], 0.0), writes=[r_Sd[d]])
        seq_f = [("ctx", c) for c in range(NCC)] + [("lat", n) for n in range(NCH)]
        seq_b = [("ctx", c) for c in range(NCC - 1, -1, -1)] + [("lat", n) for n in range(NCH - 1, -1, -1)]
        ns = len(seq_f)
        steps = [[step_info(0, seq_f[i], i == ns - 1) for i in range(ns)],
                 [step_info(1, seq_b[i], i == ns - 1) for i in range(ns)]]
        for d in range(2):
            pre_a(steps[d][0], 0)
        for d in range(2):
            pre_b(steps[d][0], 0)
        for i in range(ns):
            if i + 1 < ns:
                for d in range(2):
                    pre_a(steps[d][i + 1], (i + 1) % 2)
            for d in range(2):
                chain(steps[d][i], i % 2)
            if i + 1 < ns:
                for d in range(2):
                    pre_b(steps[d][i + 1], (i + 1) % 2)
        if upto < 3:
            continue
        def stage_X(n):
                bi = n // 4
                g, j = n // 4, n % 4
                gb = g % 2
                sl = n % 2
                csl = slice(n * 128, (n + 1) * 128)
                c0_, c1_, c2_ = n * 128, n * 128 + 64, (n + 1) * 128
                bf_, bb_ = sl * 2, sl * 2 + 1
                P.op("pe", lambda e, c0_=c0_, c1_=c1_, c2_=c2_, bf_=bf_: e.matmul(
                    pb[bf_][0:64, 0:128], lhsT=KF[:, c0_:c1_], rhs=QF[:, c0_:c2_], start=True, stop=True),
                     reads=[r_KF[bi], r_QF[bi]], writes=[r_pb[bf_]], inc=False)
                P.op("pe", lambda e, c0_=c0_, c1_=c1_, c2_=c2_, bf_=bf_: e.matmul(
                    pb[bf_][64:128, 64:128], lhsT=KF[:, c1_:c2_], rhs=QF[:, c1_:c2_], start=True, stop=True),
                     reads=[r_KF[bi], r_QF[bi]], writes=[r_pb[bf_]])
                P.op("pe", lambda e, c0_=c0_, c1_=c1_, c2_=c2_, bb_=bb_: e.matmul(
                    pb[bb_][64:128, 0:128], lhsT=KB[:, c1_:c2_], rhs=QB[:, c0_:c2_], start=True, stop=True),
                     reads=[r_KB[bi], r_QB[bi]], writes=[r_pb[bb_]], inc=False)
                P.op("pe", lambda e, c0_=c0_, c1_=c1_, c2_=c2_, bb_=bb_: e.matmul(
                    pb[bb_][0:64, 0:64], lhsT=KB[:, c0_:c1_], rhs=QB[:, c0_:c1_], start=True, stop=True),
                     reads=[r_KB[bi], r_QB[bi]], writes=[r_pb[bb_]])
                for d, bk in ((0, bf_), (1, bb_)):
                    P.op("dve", lambda e, d=d, sl=sl, bk=bk: e.tensor_tensor(out=Am[d][sl][:], in0=pb[bk][:, 0:128],
                                                                             in1=masks[:, d, :], op=ALU.mult),
                         reads=[r_pb[bk], r_masks], writes=[r_Am[d][sl]])

        def stage_Y(n):
                bi = n // 4
                g, j = n // 4, n % 4
                gb = g % 2
                sl = n % 2
                csl = slice(n * 128, (n + 1) * 128)
                c0_, c1_, c2_ = n * 128, n * 128 + 64, (n + 1) * 128
                bf_, bb_ = sl * 2, sl * 2 + 1
                P.op("pe", lambda e, sl=sl, gb=gb, j=j, n=n: e.matmul(ppo[gb][:, j, :], lhsT=Am[0][sl][:], rhs=V[:, n, :],
                                                                      start=True, stop=False),
                     reads=[r_Am[0][sl], r_V[bi]], writes=[r_ppo[gb]], inc=False)
                P.op("pe", lambda e, sl=sl, gb=gb, j=j, n=n: e.matmul(ppo[gb][:, j, :], lhsT=Am[1][sl][:], rhs=V[:, n, :],
                                                                      start=False, stop=False),
                     reads=[r_Am[1][sl]], writes=[r_ppo[gb]], inc=False)
                P.op("pe", lambda e, gb=gb, j=j, csl=csl, n=n: e.matmul(ppo[gb][:, j, :], lhsT=QF[:, csl], rhs=SF[:, n, :],
                                                                        start=False, stop=False),
                     reads=[r_QF[bi], r_SF[n]], writes=[r_ppo[gb]], inc=False)
                P.op("pe", lambda e, gb=gb, j=j, csl=csl, n=n: e.matmul(ppo[gb][:, j, :], lhsT=QB[:, csl], rhs=SB[:, n, :],
                                                                        start=False, stop=True),
                     reads=[r_QB[bi], r_SB[n]], writes=[r_ppo[gb]])
                P.op("act", lambda e, gb=gb, j=j: e.activation(out=sq[:], in_=ppo[gb][:, j, :], func=AF.Square,
                                                               accum_out=st[gb][:, 0, j:j + 1]),
                     reads=[r_ppo[gb]], writes=[r_sq, r_st[gb]])
                if j == 3:
                    P.op("act", lambda e, gb=gb: e.activation(out=st[gb][:, 1, :], in_=st[gb][:, 0, :], func=AF.Ln,
                                                              bias=epst[:], scale=1.0 / 128),
                         reads=[r_st[gb], r_eps], writes=[r_st[gb]])
                    P.op("act", lambda e, gb=gb: e.activation(out=st[gb][:, 2, :], in_=st[gb][:, 1, :], func=AF.Exp,
                                                              scale=-0.5),
                         reads=[r_st[gb]], writes=[r_st[gb]])
                    for jj in range(4):
                        nn = g * 4 + jj
                        P.op("dve", lambda e, gb=gb, jj=jj, nn=nn: e.scalar_tensor_tensor(
                            out=y4[gb][:, jj, :], in0=ppo[gb][:, jj, :], scalar=st[gb][:, 2, jj:jj + 1],
                            in1=ZG[:, nn, :], op0=ALU.mult, op1=ALU.mult),
                             reads=[r_ppo[gb], r_st[gb], r_ZG[bi]], writes=[r_y4[gb]])
                    P.dma("pool", y_d[g * 512:(g + 1) * 512, h * 128:(h + 1) * 128].rearrange("(j p) v -> p j v", p=128),
                          y4[gb][:], reads=[r_y4[gb]], writes=[r_out])

        stage_X(0)
        for n in range(NCH):
            if n + 1 < NCH:
                stage_X(n + 1)
            stage_Y(n)
    P.wait("sp", [r_out])
    P.emit()
    return nc, P


def _consts():
    ident = np.eye(128, dtype=NPBF)
    s = np.arange(128)[:, None]
    t = np.arange(128)[None, :]
    masks = np.stack([(s <= t), (s >= t)], axis=1).astype(np.float32)
    return ident, np.ascontiguousarray(masks)


def hgrn_maps(inp, hl, hc):
    ident, masks = _consts()
    w_in = np.asarray(inp["hg_w_in"][0])
    lgt = np.asarray(inp["hg_lb_logits"])
    maps = []
    for core in range(NCORES):
        b, hg = core // 4, core % 4
        w5 = w_in.reshape(8, 128, 5, 16, 128)[:, :, :, hg * 4:(hg + 1) * 4, :]
        w = np.ascontiguousarray(w5.transpose(1, 0, 3, 2, 4).reshape(128, 8, 4, 640))
        lg = lgt.reshape(2, 2, 16, 128)[:, :, hg * 4:(hg + 1) * 4, :]
        lg = np.ascontiguousarray(lg.transpose(3, 0, 1, 2))
        hlT = np.ascontiguousarray(np.asarray(hl[b]).reshape(SEQ, 8, 128).transpose(1, 2, 0))
        hcT = np.ascontiguousarray(np.asarray(hc[b]).reshape(CTX, 8, 128).transpose(1, 2, 0))
        maps.append(dict(hl=hlT, hc=hcT, w=w, lg=lg, ident=ident, masks=masks))
    return maps


def build_fourier():
    nc = bass.Bass("TRN2", target_bir_lowering=False)
    P = Prog(nc)

    def din(name, shape, dt):
        return nc.dram_tensor(name, list(shape), dt, kind="ExternalInput").ap()

    hp_d = din("hp", [8, 128, 64, 128], BF16)
    wu_d = din("wu", [128, 8, 512], F32)
    wz_d = din("wz", [128, 8, 512], F32)
    cs_d = din("cs", [128, 2, 512], BF16)
    gt_d = din("gt", [64, 128, 512], BF16)
    fb_d = din("fb", [128, 2, 128], BF16)
    id_d = din("ident", [128, 128], BF16)
    y_d = nc.dram_tensor("yg", [SEQ, 512], BF16, kind="ExternalOutput").ap()

    ident = P.sb("ident", [128, 128], BF16); r_ident = Reg()
    cs = P.sb("cs", [128, 2, 512], BF16); r_cs = Reg()
    fb = P.sb("fb", [128, 2, 128], BF16); r_fb = Reg()
    stg = [P.sb("stg%d" % i, [128, 512], F32) for i in range(2)]; r_stg = [Reg(), Reg()]
    wubk = [P.sb("wubk%d" % i, [128, 512], BF16) for i in range(2)]; r_wubk = [Reg(), Reg()]
    WuT = P.sb("WuT", [128, 4, 1024], BF16); r_WuT = Reg()
    Wp = P.sb("Wp", [128, 8, 1024], BF16); r_Wp = Reg()
    wzb = P.sb("wzb", [128, 8, 512], BF16); r_wzb = Reg()
    hblk = [P.sb("hblk%d" % i, [128, 8, 2, 128], BF16) for i in range(2)]; r_hblk = [Reg(), Reg()]
    Zsb = [P.sb("Zsb%d" % i, [128, 1024], BF16) for i in range(2)]; r_Zsb = [Reg(), Reg()]
    gtab = [P.sb("gtab%d" % i, [128, 512], BF16) for i in range(2)]; r_gtab = [Reg(), Reg()]
    Abuf = P.sb("Abuf", [128, 4, 2, 128, 64], BF16)
    r_Ab = [Reg() for _ in range(64)]
    ATs = [P.sb("ATs%d" % i, [128, 2, 4, 128], BF16) for i in range(2)]; r_ATs = [Reg(), Reg()]
    zs = [P.sb("zs%d" % i, [128, 512], F32) for i in range(2)]; r_zs = [Reg(), Reg()]
    ygt = [P.sb("ygt%d" % i, [128, 512], BF16) for i in range(2)]; r_ygt = [Reg(), Reg()]
    r_out = Reg()

    pb = [P.ps("pb%d" % i, [128, 512], F32) for i in range(6)]; r_pb = [Reg() for _ in range(6)]
    pt = [P.ps("pt%d" % i, [128, 1024], BF16) for i in range(2)]; r_pt = [Reg(), Reg()]

    P.dma("sp", ident[:], id_d, writes=[r_ident])
    P.dma("sp", cs[:], cs_d, writes=[r_cs])
    P.dma("sp", fb[:], fb_d, writes=[r_fb])

    for kc in range(8):
        s = kc % 2
        P.dma("sp", stg[s][:], wu_d[:, kc, :], writes=[r_stg[s]])
        P.op("dve", lambda e, s=s: e.tensor_copy(out=wubk[s][:], in_=stg[s][:]), reads=[r_stg[s]], writes=[r_wubk[s]])
        for jb in range(4):
            P.op("pe", lambda e, s=s, jb=jb: e.transpose(out=pt[s][:, jb * 128:(jb + 1) * 128],
                                                         in_=wubk[s][:, jb * 128:(jb + 1) * 128], identity=ident[:]),
                 reads=[r_wubk[s], r_ident], writes=[r_pt[s]], inc=(jb == 3))
        P.op("act", lambda e, s=s, kc=kc: e.copy(out=WuT[:, :, kc * 128:(kc + 1) * 128],
                                                 in_=pt[s][:, 0:512].rearrange("p (j k) -> p j k", k=128)),
             reads=[r_pt[s]], writes=[r_WuT])
    for kc in range(8):
        s = kc % 2
        P.dma("sp", stg[s][:], wz_d[:, kc, :], writes=[r_stg[s]])
        P.op("act", lambda e, s=s, kc=kc: e.copy(out=wzb[:, kc, :], in_=stg[s][:]),
             reads=[r_stg[s]], writes=[r_wzb])
    i = 0
    for g in range(2):
        for kc in range(8):
            bk = i % 2
            i += 1
            for jc in range(2):
                P.op("pe", lambda e, g=g, kc=kc, jc=jc, bk=bk: e.matmul(
                    pb[bk][:], lhsT=WuT[:, g * 2 + jc, kc * 128:(kc + 1) * 128], rhs=cs[:, jc, :],
                    start=(jc == 0), stop=(jc == 1)), reads=[r_WuT, r_cs], writes=[r_pb[bk]], inc=(jc == 1))
            outv = Wp[:, kc, :].rearrange("p (c g m) -> p c g m", c=2, g=2)[:, :, g, :]
            inv = pb[bk][:, :].rearrange("p (c m) -> p c m", c=2)
            if bk == 0:
                P.op("act", lambda e, outv=outv, inv=inv: e.copy(out=outv, in_=inv), reads=[r_pb[bk]], writes=[r_Wp])
            else:
                P.op("dve", lambda e, outv=outv, inv=inv: e.tensor_copy(out=outv, in_=inv), reads=[r_pb[bk]], writes=[r_Wp])

    def s1a(b):
        bp, bj = b // 2, b % 2
        hb = bp % 2
        zb = b % 2
        zbank = (0, 1) if b % 2 == 0 else (4, 5)
        if bj == 0:
            P.dma("sp", hblk[hb][:], hp_d[:, :, 2 * bp:2 * bp + 2, :].rearrange("k p b a -> p k b a"), writes=[r_hblk[hb]])
        P.dma("sp", gtab[zb][:], gt_d[b], writes=[r_gtab[zb]])
        for half in range(2):
            bk = zbank[half]
            for kc in range(8):
                P.op("pe", lambda e, hb=hb, bj=bj, half=half, kc=kc, bk=bk: e.matmul(
                    pb[bk][:], lhsT=hblk[hb][:, kc, bj, :], rhs=Wp[:, kc, half * 512:(half + 1) * 512],
                    start=(kc == 0), stop=(kc == 7)), reads=[r_hblk[hb], r_Wp], writes=[r_pb[bk]], inc=(kc == 7))
        P.op("act", lambda e, zb=zb, bk=zbank[0]: e.copy(out=Zsb[zb][:, 0:512], in_=pb[bk][:]),
             reads=[r_pb[zbank[0]]], writes=[r_Zsb[zb]])
        P.op("dve", lambda e, zb=zb, bk=zbank[1]: e.tensor_copy(out=Zsb[zb][:, 512:1024], in_=pb[bk][:]),
             reads=[r_pb[zbank[1]]], writes=[r_Zsb[zb]])

    def s1b(b):
        zb = b % 2
        for mp in range(2):
            bank = 2 + mp
            for mj in range(2):
                mb = mp * 2 + mj
                P.op("pe", lambda e, zb=zb, mb=mb, mj=mj, bank=bank: e.matmul(
                    pb[bank][:, mj * 256:(mj + 1) * 256], lhsT=Zsb[zb][:, mb * 128:(mb + 1) * 128],
                    rhs=gtab[zb][:, 0:256], start=True, stop=False),
                     reads=[r_Zsb[zb], r_gtab[zb]], writes=[r_pb[bank]], inc=False)
                P.op("pe", lambda e, zb=zb, mb=mb, mj=mj, bank=bank: e.matmul(
                    pb[bank][:, mj * 256:(mj + 1) * 256], lhsT=Zsb[zb][:, 512 + mb * 128:512 + (mb + 1) * 128],
                    rhs=gtab[zb][:, 256:512], start=False, stop=True),
                     reads=[r_Zsb[zb], r_gtab[zb]], writes=[r_pb[bank]], inc=(mj == 1))
            outv = Abuf[:, mp * 2:mp * 2 + 2, :, :, b]
            inv = pb[bank][:, :].rearrange("p (mb ri k) -> p mb ri k", mb=2, ri=2)
            if mp == 0:
                P.op("act", lambda e, outv=outv, inv=inv: e.copy(out=outv, in_=inv), reads=[r_pb[bank]], writes=[r_Ab[b]])
            else:
                P.op("dve", lambda e, outv=outv, inv=inv: e.tensor_copy(out=outv, in_=inv),
                     reads=[r_pb[bank]], writes=[r_Ab[b]])

    s1a(0)
    for b in range(64):
        if b + 1 < 64:
            s1a(b + 1)
        s1b(b)

    yv = y_d.rearrange("(k2 r) c -> r k2 c", r=128)

    def s3a(pr):
        s_ = pr % 2
        prm, par = pr % 32, pr // 32
        zbk = s_
        P.dma("sp", hblk[s_][:], hp_d[:, :, 2 * prm:2 * prm + 2, :].rearrange("k p b a -> p k b a"), writes=[r_hblk[s_]])
        for kc in range(8):
            lhs = hblk[s_][:, kc, :, :].rearrange("p b (k2 two) -> p (b k2) two", two=2)[:, :, par]
            P.op("pe", lambda e, kc=kc, lhs=lhs, zbk=zbk: e.matmul(pb[zbk][:], lhsT=lhs, rhs=wzb[:, kc, :],
                                                                   start=(kc == 0), stop=(kc == 7)),
                 reads=[r_hblk[s_], r_wzb], writes=[r_pb[zbk]], inc=(kc == 7))
        P.op("act", lambda e, s_=s_, zbk=zbk: e.activation(out=zs[s_][:], in_=pb[zbk][:], func=AF.Silu),
             reads=[r_pb[zbk]], writes=[r_zs[s_]])
        for ri in range(2):
            for mb in range(4):
                src = Abuf[:, mb, ri, 2 * pr:2 * pr + 2, :].rearrange("p k b -> p (k b)")
                P.op("pe", lambda e, s_=s_, ri=ri, mb=mb, src=src: e.transpose(
                    out=pt[s_][:, (ri * 4 + mb) * 128:(ri * 4 + mb + 1) * 128], in_=src, identity=ident[:]),
                     reads=r_Ab + [r_ident] if (pr == 0 and ri == 0 and mb == 0) else [r_ident], writes=[r_pt[s_]],
                     inc=(ri == 1 and mb == 3))
        P.op("dve", lambda e, s_=s_: e.tensor_copy(out=ATs[s_][:].rearrange("p r m c -> p (r m c)"), in_=pt[s_][:]),
             reads=[r_pt[s_]], writes=[r_ATs[s_]])

    def s3b(pr):
        s_ = pr % 2
        ybk = 2 + s_
        for ri in range(2):
            P.op("pe", lambda e, s_=s_, ri=ri, ybk=ybk: e.matmul(pb[ybk][:], lhsT=fb[:, ri, :],
                                                                 rhs=ATs[s_][:, ri, :, :].rearrange("p m c -> p (m c)"),
                                                                 start=(ri == 0), stop=(ri == 1)),
                 reads=[r_fb, r_ATs[s_]], writes=[r_pb[ybk]], inc=(ri == 1))
        P.op("dve", lambda e, s_=s_, ybk=ybk: e.tensor_tensor(out=ygt[s_][:], in0=pb[ybk][:], in1=zs[s_][:], op=ALU.mult),
             reads=[r_pb[ybk], r_zs[s_]], writes=[r_ygt[s_]])
        for kap in range(2):
            P.dma("pool", yv[2 * pr + kap], ygt[s_][kap * 64:(kap + 1) * 64, :], reads=[r_ygt[s_]], writes=[r_out])

    s3a(0)
    for pr in range(64):
        if pr + 1 < 64:
            s3a(pr + 1)
        s3b(pr)
    P.wait("sp", [r_out])
    P.emit()
    return nc, P


def fourier_tables():
    N = SEQ
    j = np.arange(256)[:, None]; m = np.arange(256)[None, :]
    ang = 2 * np.pi * (j * m % 256) / 256
    C = np.cos(ang) / 16.0; S = np.sin(ang) / 16.0
    cs = np.concatenate([C, S], axis=1).reshape(2, 128, 512).transpose(1, 0, 2)
    a = np.arange(128)[None, :, None]; b = np.arange(64)[:, None, None]; k1 = np.arange(128)[None, None, :]
    th = 2 * np.pi * ((k1 * (64 * a + b)) % N) / N
    Gr = np.cos(th) / np.sqrt(128.0); Gi = -np.sin(th) / np.sqrt(128.0)
    gt = np.concatenate([Gr, Gi, Gi, -Gr], axis=2)
    bb = np.arange(64)[:, None]; k2 = np.arange(64)[None, :]
    ph = 2 * np.pi * ((bb * k2) % 64) / 64
    Fc = np.cos(ph) / 8.0; Fs = np.sin(ph) / 8.0
    fb = np.zeros((128, 2, 128))
    for kap in range(2):
        fb[kap * 64:(kap + 1) * 64, 0, kap * 64:(kap + 1) * 64] = Fc
        fb[kap * 64:(kap + 1) * 64, 1, kap * 64:(kap + 1) * 64] = Fs
    return (np.ascontiguousarray(cs).astype(NPBF), np.ascontiguousarray(gt).astype(NPBF),
            np.ascontiguousarray(fb).astype(NPBF))


def fourier_maps(inp, h1):
    ident, _ = _consts()
    cs, gt, fb = fourier_tables()
    w_in = np.asarray(inp["ft_w_in"][0])
    maps = []
    for core in range(NCORES):
        b, gp = core // 4, core % 4
        wu = np.ascontiguousarray(w_in[:, gp * 512:(gp + 1) * 512].reshape(8, 128, 512).transpose(1, 0, 2))
        wz = np.ascontiguousarray(w_in[:, E + gp * 512:E + (gp + 1) * 512].reshape(8, 128, 512).transpose(1, 0, 2))
        hp = np.ascontiguousarray(np.asarray(h1[b]).reshape(128, 64, 8, 128).transpose(2, 3, 1, 0))
        maps.append(dict(hp=hp, wu=wu, wz=wz, cs=cs, gt=gt, fb=fb, ident=ident))
    return maps


_CACHE = {}


def _prog(key, builder):
    if key not in _CACHE:
        _CACHE[key] = builder()[0]
    return _CACHE[key]


def _ada_maps(inp, layer):
    aw = np.asarray(inp["ada_w"][layer], np.float32)
    ab = np.asarray(inp["ada_b"][layer], np.float32)
    return aw, ab


def kernel(x, c, ctx, c_ctx, ada_w, ada_b, norm_g, hg_w_in, hg_lb_logits, hg_norm_g,
           hg_w_out, ft_w_in, ft_w_out, final_g):
    inp = dict(x=np.asarray(x, np.float32), c=np.asarray(c, np.float32), ctx=np.asarray(ctx, np.float32),
               c_ctx=np.asarray(c_ctx, np.float32), ada_w=np.asarray(ada_w, np.float32),
               ada_b=np.asarray(ada_b, np.float32), norm_g=np.asarray(norm_g, np.float32),
               hg_w_in=np.asarray(hg_w_in, np.float32), hg_lb_logits=np.asarray(hg_lb_logits, np.float32),
               hg_norm_g=np.asarray(hg_norm_g, np.float32), hg_w_out=np.asarray(hg_w_out, np.float32),
               ft_w_in=np.asarray(ft_w_in, np.float32), ft_w_out=np.asarray(ft_w_out, np.float32),
               final_g=np.asarray(final_g, np.float32))
    ident, _ = _consts()
    TS = SEQ // 4
    CS_ = CTX // 4

    def awm(layer):
        aw, ab = _ada_maps(inp, layer)
        return (np.ascontiguousarray(aw[:, :2 * D].reshape(8, 128, 2 * D).transpose(1, 0, 2)), _col(ab[:2 * D]))

    def awg(layer):
        aw, ab = _ada_maps(inp, layer)
        return (np.ascontiguousarray(aw[:, 2 * D:].reshape(8, 128, D).transpose(1, 0, 2)),
                np.ascontiguousarray(ab[None, 2 * D:]))

    nc = _prog("A1", lambda: build_tok(TS, CS_, False, True, False))
    aw_m0, ab_m0 = awm(0)
    maps = []
    for core in range(NCORES):
        b, seg = core // 4, core % 4
        cv = np.stack([_col(inp["c"][b]), _col(inp["c_ctx"])], axis=-1)
        maps.append(dict(x=np.ascontiguousarray(inp["x"][b, seg * TS:(seg + 1) * TS]),
                         xc=np.ascontiguousarray(inp["ctx"][b, seg * CS_:(seg + 1) * CS_]),
                         cvec=np.ascontiguousarray(cv), ident=ident, aw_m=aw_m0, ab_m=ab_m0,
                         ng=_col(inp["norm_g"][0])))
    res = _run(nc, maps)
    hT = [np.asarray(r["hT"]) for r in res]

    nc = _prog("A2", build_hgrn)
    _, masks = _consts()
    w_in = inp["hg_w_in"][0]
    lgt = inp["hg_lb_logits"]
    maps = []
    for core in range(NCORES):
        b, hg = core // 4, core % 4
        w5 = w_in.reshape(8, 128, 5, 16, 128)[:, :, :, hg * 4:(hg + 1) * 4, :]
        w = np.ascontiguousarray(w5.transpose(1, 0, 3, 2, 4).reshape(128, 8, 4, 640))
        lg = np.ascontiguousarray(lgt.reshape(2, 2, 16, 128)[:, :, hg * 4:(hg + 1) * 4, :].transpose(3, 0, 1, 2))
        hl = np.ascontiguousarray(np.concatenate([hT[b * 4 + s][:, :, :TS] for s in range(4)], axis=2))
        hc = np.ascontiguousarray(np.concatenate([hT[b * 4 + s][:, :, TS:] for s in range(4)], axis=2))
        maps.append(dict(hl=hl, hc=hc, w=w, lg=lg, ident=ident, masks=masks))
    res = _run(nc, maps)
    y0 = [np.asarray(r["y"]) for r in res]

    nc = _prog("B", lambda: build_tok(TS, 0, True, True, False))
    aw_g0, ab_g0 = awg(0)
    aw_m1, ab_m1 = awm(1)
    maps = []
    for core in range(NCORES):
        b, seg = core // 4, core % 4
        y = np.ascontiguousarray(np.concatenate([y0[b * 4 + g][seg * TS:(seg + 1) * TS] for g in range(4)], axis=1))
        maps.append(dict(x=np.ascontiguousarray(inp["x"][b, seg * TS:(seg + 1) * TS]), y=y,
                         w_out=np.ascontiguousarray(inp["hg_w_out"][0]),
                         wsc=np.ascontiguousarray(inp["hg_norm_g"][0].reshape(128, 1)),
                         cvec=np.ascontiguousarray(_col(inp["c"][b])[:, :, None]), ident=ident,
                         aw_g=aw_g0, ab_g=ab_g0, aw_m=aw_m1, ab_m=ab_m1, ng=_col(inp["norm_g"][1])))
    res = _run(nc, maps)
    x1 = [np.asarray(r["xo"]) for r in res]
    h1T = [np.asarray(r["hT"]) for r in res]

    nc = _prog("C", build_fourier)
    cs, gt, fb = fourier_tables()
    w_in = inp["ft_w_in"][0]
    maps = []
    for core in range(NCORES):
        b, gp = core // 4, core % 4
        wu = np.ascontiguousarray(w_in[:, gp * 512:(gp + 1) * 512].reshape(8, 128, 512).transpose(1, 0, 2))
        wz = np.ascontiguousarray(w_in[:, E + gp * 512:E + (gp + 1) * 512].reshape(8, 128, 512).transpose(1, 0, 2))
        hfull = np.concatenate([h1T[b * 4 + s] for s in range(4)], axis=2)
        hp = np.ascontiguousarray(hfull.reshape(8, 128, 128, 64).transpose(0, 1, 3, 2))
        maps.append(dict(hp=hp, wu=wu, wz=wz, cs=cs, gt=gt, fb=fb, ident=ident))
    res = _run(nc, maps)
    y1 = [np.asarray(r["yg"]) for r in res]

    nc = _prog("D", lambda: build_tok(TS, 0, True, False, True))
    aw_g1, ab_g1 = awg(1)
    maps = []
    for core in range(NCORES):
        b, seg = core // 4, core % 4
        y = np.ascontiguousarray(np.concatenate([y1[b * 4 + g][seg * TS:(seg + 1) * TS] for g in range(4)], axis=1))
        maps.append(dict(x=x1[core], y=y, w_out=np.ascontiguousarray(inp["ft_w_out"][0]),
                         wsc=np.ones((128, 1), np.float32),
                         cvec=np.ascontiguousarray(_col(inp["c"][b])[:, :, None]), ident=ident,
                         aw_g=aw_g1, ab_g=ab_g1, fg=np.ascontiguousarray(inp["final_g"][None, :])))
    res = _run(nc, maps)
    out = np.stack([np.concatenate([np.asarray(res[b * 4 + s]["xo"]) for s in range(4)], axis=0) for b in range(2)])
    return out.astype(np.float32)
```

```python
from contextlib import ExitStack
import os
import numpy as np
import ml_dtypes
import concourse.bass as bass
import concourse.mybir as mybir
from concourse.bass_utils import run_bass_kernel_spmd

F32 = mybir.dt.float32
BF16 = mybir.dt.bfloat16
AF = mybir.ActivationFunctionType
ALU = mybir.AluOpType
NPBF = ml_dtypes.bfloat16

D = 1024
E = 2048
SEQ = 8192
CTX = 256
NCORES = 8
EPS = 1e-6

SAME_ENGINE_SYNC = True
N_DMA_SEMS = 24


class Reg:
    __slots__ = ("name", "w", "r")

    def __init__(self, name=""):
        self.name = name
        self.w = None
        self.r = []


class Prog:
    ENGS = ("pe", "act", "dve", "pool", "sp")

    def __init__(self, nc):
        self.nc = nc
        self.q = {e: [] for e in self.ENGS}
        self.cnt = {e: 0 for e in self.ENGS}
        self.seen = {e: {} for e in self.ENGS}
        self.pend = {e: ([], []) for e in self.ENGS}
        self.dma_cnt = {}
        self.dma_key = {}
        self.stack = ExitStack()
        self.n_ops = 0

    def sb(self, name, shape, dt):
        return self.stack.enter_context(self.nc.sbuf_tensor("sb_" + name, list(shape), dt))

    def ps(self, name, shape, dt):
        return self.stack.enter_context(self.nc.psum_tensor("ps_" + name, list(shape), dt))

    def _waits(self, eng, reads, writes):
        need = {}

        def add(tok):
            if tok is None:
                return
            s, v = tok
            if need.get(s, 0) < v:
                need[s] = v

        for r in reads:
            add(r.w)
        for w in writes:
            add(w.w)
            for t in w.r:
                add(t)
        waits = []
        for s, v in need.items():
            if s == eng and not SAME_ENGINE_SYNC:
                continue
            if self.seen[eng].get(s, 0) >= v:
                continue
            self.seen[eng][s] = v
            waits.append((s, v))
        return waits

    def op(self, eng, fn, reads=(), writes=(), inc=True):
        reads = list(reads)
        writes = list(writes)
        waits = self._waits(eng, reads, writes)
        pr, pw = self.pend[eng]
        pr.extend(reads)
        pw.extend(writes)
        tok = None
        if inc:
            self.cnt[eng] += 1
            tok = (eng, self.cnt[eng])
            for r in pr:
                r.r.append(tok)
            for w in pw:
                w.w = tok
                w.r = []
            self.pend[eng] = ([], [])
        self.q[eng].append((waits, fn, tok, 1))
        self.n_ops += 1

    def dma(self, eng, out, in_, reads=(), writes=(), **kw):
        reads = list(reads)
        writes = list(writes)
        waits = self._waits(eng, reads, writes)
        key = self.dma_key.get(id(writes[0]))
        if key is None:
            key = "dma%d" % len(self.dma_key)
            self.dma_key[id(writes[0])] = key
            self.dma_cnt[key] = 0
        self.dma_cnt[key] += 16
        tok = (key, self.dma_cnt[key])
        for r in reads:
            r.r.append(tok)
        for w in writes:
            w.w = tok
            w.r = []
        self.q[eng].append((waits, lambda e: e.dma_start(out=out, in_=in_, **kw), tok, 16))
        self.n_ops += 1

    def coll(self, kind, ins, outs, groups, reads=(), writes=()):
        reads = list(reads)
        writes = list(writes)
        waits = self._waits("pool", reads, writes)
        key = "dma%d" % len(self.dma_key)
        self.dma_key[id(writes[0])] = key
        self.dma_cnt[key] = 16
        tok = (key, 16)
        for r in reads:
            r.r.append(tok)
        for w in writes:
            w.w = tok
            w.r = []
        self.q["pool"].append((waits, lambda e: e.collective_compute(kind, ALU.bypass, replica_groups=groups,
                                                                     ins=list(ins), outs=list(outs)), tok, 16))
        self.n_ops += 1

    def wait(self, eng, regs):
        waits = self._waits(eng, list(regs), list(regs))
        self.q[eng].append((waits, None, None, 0))

    def emit(self):
        nc = self.nc
        names = ["pe", "act", "dve", "pool"] + list(self.dma_cnt.keys())
        sems = {n: self.stack.enter_context(nc.semaphore("s_" + n)) for n in names}
        block = self.stack.enter_context(nc.Block())
        attr = {"pe": "tensor", "act": "scalar", "dve": "vector", "pool": "gpsimd", "sp": "sync"}
        for eng in self.ENGS:
            q = self.q[eng]

            def body(e, q=q):
                for waits, fn, tok, amt in q:
                    for s, v in waits:
                        e.wait_ge(sems[s], v)
                    if fn is None:
                        continue
                    ins = fn(e)
                    if tok is not None:
                        ins.then_inc(sems[tok[0]], amt)

            getattr(block, attr[eng])(body)

    def close(self):
        self.stack.close()


def build_tok(ntok, nctx, has_outproj, has_normmod, has_final):
    nc = bass.Bass("TRN2", target_bir_lowering=False)
    P = Prog(nc)
    NV = 2 if nctx else 1
    ntot = ntok + nctx
    dr = {}

    def din(name, shape, dt):
        dr[name] = nc.dram_tensor(name, list(shape), dt, kind="ExternalInput").ap()
        return dr[name]

    def dout(name, shape, dt):
        dr[name] = nc.dram_tensor(name, list(shape), dt, kind="ExternalOutput").ap()
        return dr[name]

    x_d = din("x", [ntok, D], F32)
    cv_d = din("cvec", [128, 8, NV], F32)
    id_d = din("ident", [128, 128], BF16)
    if has_outproj:
        y_d = din("y", [ntok, E], BF16)
        w_d = din("w_out", [E, D], F32)
        awg_d = din("aw_g", [128, 8, D], F32)
        wsc_d = din("wsc", [128, 1], F32)
        abg_d = din("ab_g", [1, D], F32)
    if has_normmod:
        awm_d = din("aw_m", [128, 8, 2 * D], F32)
        abm_d = din("ab_m", [128, 16], F32)
        ng_d = din("ng", [128, 8], F32)
        hT_d = dout("hT", [8, 128, ntot], BF16)
    if has_final:
        fg_d = din("fg", [1, D], F32)
    if nctx:
        xc_d = din("xc", [nctx, D], F32)
    if has_outproj or has_final:
        xo_d = dout("xo", [ntok, D], F32)

    ident = P.sb("ident_sb", [128, 128], BF16)
    cvec = P.sb("cvec_sb", [128, 8, NV], F32)
    scv = P.sb("scv_sb", [128, 8, NV], F32)
    zeros = P.sb("zeros_sb", [128, 128], F32)
    ones1 = P.sb("ones1_sb", [1, 128], F32)
    epst = P.sb("eps_sb", [128, 1], F32)
    stage = P.sb("stage_sb", [128, 8, D], F32)
    r_ident, r_cvec, r_scv, r_zeros, r_ones, r_eps = (Reg() for _ in range(6))
    r_stage = [Reg() for _ in range(8)]

    ps = [P.ps("psb%d" % i, [128, 512], F32) for i in range(4)]
    r_ps = [Reg() for _ in range(4)]
    pst = [P.ps("pst%d" % i, [128, 1024], BF16) for i in range(2)]
    r_pst = [Reg() for _ in range(2)]

    P.dma("sp", ident[:], id_d, writes=[r_ident])
    P.dma("sp", cvec[:], cv_d, writes=[r_cvec])
    P.op("dve", lambda e: e.memset(zeros[:], 0.0), writes=[r_zeros])
    P.op("dve", lambda e: e.memset(ones1[:], 1.0), writes=[r_ones])
    P.op("dve", lambda e: e.memset(epst[:], EPS), writes=[r_eps])
    P.op("act", lambda e: e.activation(out=scv[:], in_=cvec[:], func=AF.Silu), reads=[r_cvec], writes=[r_scv])

    if has_outproj:
        gtb = P.sb("gtb_sb", [128, D], F32)
        r_gtb = Reg()
        scb = P.sb("scb_sb", [128, 8, 128], F32)
        r_scb = Reg()
        abg = P.sb("abg_sb", [1, D], F32)
        r_abg = Reg()
        P.dma("sp", abg[:], abg_d, writes=[r_abg])
        for kc in range(8):
            P.op("act", lambda e, kc=kc: e.activation(out=scb[:, kc, :], in_=zeros[:], func=AF.Identity,
                                                       bias=scv[:, kc, 0:1], scale=1.0),
                 reads=[r_zeros, r_scv], writes=[r_scb], inc=(kc == 7))
        for kc in range(8):
            P.dma("sp", stage[:, kc, :], awg_d[:, kc, :], writes=[r_stage[kc]])
        for hf in range(2):
            for kc in range(8):
                P.op("pe", lambda e, kc=kc, hf=hf: e.matmul(ps[hf][:], lhsT=scb[:, kc, :],
                                                            rhs=stage[:, kc, hf * 512:(hf + 1) * 512],
                                                            start=(kc == 0), stop=False),
                     reads=[r_scb, r_stage[kc]], writes=[r_ps[hf]], inc=False)
            P.op("pe", lambda e, hf=hf: e.matmul(ps[hf][:], lhsT=ones1[:], rhs=abg[:, hf * 512:(hf + 1) * 512],
                                                 start=False, stop=True),
                 reads=[r_ones, r_abg], writes=[r_ps[hf]])
            P.op("dve", lambda e, hf=hf: e.tensor_copy(out=gtb[:, hf * 512:(hf + 1) * 512], in_=ps[hf][:]),
                 reads=[r_ps[hf]], writes=[r_gtb])

    if has_normmod:
        ngc = P.sb("ngc_sb", [128, 8], F32)
        abm = P.sb("abm_sb", [128, 16], F32)
        mcol = P.sb("mcol_sb", [128, 16, NV], F32)
        acol = P.sb("acol_sb", [128, 8, NV], F32)
        r_ngc, r_abm, r_mcol, r_acol = Reg(), Reg(), Reg(), Reg()
        P.dma("sp", ngc[:], ng_d, writes=[r_ngc])
        P.dma("sp", abm[:], abm_d, writes=[r_abm])
        pcol = ps[2]
        for half in range(2):
            for kc in range(8):
                P.dma("sp", stage[:, kc, :], awm_d[:, kc, half * D:(half + 1) * D], writes=[r_stage[kc]])
            for fc in range(8):
                for kc in range(8):
                    P.op("pe", lambda e, kc=kc, fc=fc, half=half: e.matmul(
                        pcol[:, (half * 8 + fc) * NV:(half * 8 + fc + 1) * NV],
                        lhsT=stage[:, kc, fc * 128:(fc + 1) * 128], rhs=scv[:, kc, :],
                        start=(kc == 0), stop=(kc == 7)),
                         reads=[r_stage[kc], r_scv], writes=[r_ps[2]], inc=(kc == 7 and fc == 7))
        for v in range(NV):
            P.op("dve", lambda e, v=v: e.tensor_tensor(
                out=mcol[:, :, v], in0=pcol[:, 0:16 * NV].rearrange("p (f v) -> p f v", v=NV)[:, :, v],
                in1=abm[:], op=ALU.add), reads=[r_ps[2], r_abm], writes=[r_mcol])
            P.op("dve", lambda e, v=v: e.scalar_tensor_tensor(
                out=acol[:, :, v], in0=mcol[:, 8:16, v], scalar=1.0, in1=ngc[:], op0=ALU.add, op1=ALU.mult),
                 reads=[r_mcol, r_ngc], writes=[r_acol])

    if has_final:
        fgb = P.sb("fgb_sb", [128, D], F32)
        fgr = P.sb("fgr_sb", [1, D], F32)
        r_fgb, r_fgr = Reg(), Reg()
        P.dma("sp", fgr[:], fg_d, writes=[r_fgr])
        for hf in range(2):
            P.op("pe", lambda e, hf=hf: e.matmul(ps[hf][:], lhsT=ones1[:], rhs=fgr[:, hf * 512:(hf + 1) * 512],
                                                 start=True, stop=True),
                 reads=[r_ones, r_fgr], writes=[r_ps[hf]])
            P.op("dve", lambda e, hf=hf: e.tensor_copy(out=fgb[:, hf * 512:(hf + 1) * 512], in_=ps[hf][:]),
                 reads=[r_ps[hf]], writes=[r_fgb])

    if has_outproj:
        wbf = P.sb("wbf_sb", [128, 16, D], BF16)
        wsc = P.sb("wsc_sb", [128, 1], F32)
        r_wsc = Reg()
        P.dma("sp", wsc[:], wsc_d, writes=[r_wsc])
        r_wbf = [Reg() for _ in range(16)]
        for ec in range(16):
            s = ec % 8
            P.dma("sp", stage[:, s, :], w_d[ec * 128:(ec + 1) * 128, :], writes=[r_stage[s]])
            if ec % 2:
                P.op("act", lambda e, ec=ec, s=s: e.activation(out=wbf[:, ec, :], in_=stage[:, s, :], func=AF.Identity,
                                                               scale=wsc[:, 0:1]),
                     reads=[r_stage[s], r_wsc], writes=[r_wbf[ec]])
            else:
                P.op("dve", lambda e, ec=ec, s=s: e.tensor_scalar(out=wbf[:, ec, :], in0=stage[:, s, :],
                                                                  scalar1=wsc[:, 0:1], scalar2=None, op0=ALU.mult),
                     reads=[r_stage[s], r_wsc], writes=[r_wbf[ec]])

    NB = 2
    NL = 3
    xt = [P.sb("xt%d" % i, [128, D], F32) for i in range(NL)]
    r_xt = [Reg() for _ in range(NL)]
    xn = [P.sb("xn%d" % i, [128, D], F32) for i in range(NB)]
    r_xn = [Reg() for _ in range(NB)]
    sq = P.sb("sq_sb", [128, D], F32)
    r_sq = Reg()
    stat = [P.sb("stat%d" % i, [128, 4], F32) for i in range(NB)]
    r_stat = [Reg() for _ in range(NB)]
    if has_outproj:
        yt = [P.sb("yt%d" % i, [128, E], BF16) for i in range(NL)]
        r_yt = [Reg() for _ in range(NL)]
        yT = [P.sb("yT%d" % i, [128, 16, 128], BF16) for i in range(NB)]
        r_yT = [Reg() for _ in range(NB)]
    if has_normmod:
        xb = [P.sb("xb%d" % i, [128, D], BF16) for i in range(NB)]
        r_xb = [Reg() for _ in range(NB)]
        hTt = [P.sb("hTt%d" % i, [128, 8, 128], BF16) for i in range(NB)]
        r_hTt = [Reg() for _ in range(NB)]
    if has_final:
        ot = [P.sb("ot%d" % i, [128, D], F32) for i in range(NB)]
        r_ot = [Reg() for _ in range(NB)]
    r_out = Reg()

    tiles = [(False, t * 128, min(128, ntok - t * 128)) for t in range((ntok + 127) // 128)]
    tiles += [(True, t * 128, min(128, nctx - t * 128)) for t in range((nctx + 127) // 128)]
    pstn2 = [P.ps("pstn%d" % i, [128, 512], BF16) for i in range(2)]
    r_pstn2 = [Reg(), Reg()]

    def S1(it):
        is_ctx, t0, n = tiles[it]
        b = it % NB
        l = it % NL
        src = xc_d if is_ctx else x_d
        P.dma("sp", xt[l][:n, :], src[t0:t0 + n, :], writes=[r_xt[l]])
        if has_outproj:
            P.dma("sp", yt[l][:n, :], y_d[t0:t0 + n, :], writes=[r_yt[l]])
            for g in range(2):
                for j in range(8):
                    ec = g * 8 + j
                    P.op("pe", lambda e, l=l, g=g, j=j, ec=ec, n=n: e.transpose(
                        out=pst[g][:, j * 128:j * 128 + n], in_=yt[l][:n, ec * 128:(ec + 1) * 128],
                        identity=ident[:n, :n]),
                         reads=[r_yt[l], r_ident], writes=[r_pst[g]], inc=(j == 7))
                if g == 0:
                    P.op("act", lambda e, b=b, g=g, n=n: e.copy(
                        out=yT[b][:, g * 8:(g + 1) * 8, :n],
                        in_=pst[g][:, :].rearrange("p (j t) -> p j t", t=128)[:, :, :n]),
                         reads=[r_pst[g]], writes=[r_yT[b]])
                else:
                    P.op("dve", lambda e, b=b, g=g, n=n: e.tensor_copy(
                        out=yT[b][:, g * 8:(g + 1) * 8, :n],
                        in_=pst[g][:, :].rearrange("p (j t) -> p j t", t=128)[:, :, :n]),
                         reads=[r_pst[g]], writes=[r_yT[b]])

    def cur_of(it):
        b = it % NB
        return (xn[b], r_xn[b]) if has_outproj else (xt[it % NL], r_xt[it % NL])

    def S2(it):
        is_ctx, t0, n = tiles[it]
        b = it % NB
        if has_outproj:
            for hf in range(2):
                for ec in range(16):
                    P.op("pe", lambda e, b=b, hf=hf, ec=ec, n=n: e.matmul(
                        ps[hf][:n, :], lhsT=yT[b][:, ec, :n], rhs=wbf[:, ec, hf * 512:(hf + 1) * 512],
                        start=(ec == 0), stop=(ec == 15)),
                         reads=[r_yT[b], r_wbf[ec]], writes=[r_ps[hf]], inc=(ec == 15))
                P.op("dve", lambda e, b=b, hf=hf, n=n: e.tensor_tensor(
                    out=xn[b][:n, hf * 512:(hf + 1) * 512], in0=ps[hf][:n, :],
                    in1=gtb[:n, hf * 512:(hf + 1) * 512], op=ALU.mult),
                     reads=[r_ps[hf], r_gtb], writes=[r_xn[b]])
            l = it % NL
            P.op("dve", lambda e, b=b, n=n, l=l: e.tensor_tensor(
                out=xn[b][:n, :], in0=xn[b][:n, :], in1=xt[l][:n, :], op=ALU.add),
                 reads=[r_xn[b], r_xt[l]], writes=[r_xn[b]])
            if has_normmod:
                P.dma("pool", xo_d[t0:t0 + n, :], xn[b][:n, :], reads=[r_xn[b]], writes=[r_out])
        cur, r_cur = cur_of(it)
        P.op("act", lambda e, b=b, n=n, cur=cur: e.activation(
            out=sq[:n, :], in_=cur[:n, :], func=AF.Square, accum_out=stat[b][:n, 0:1]),
             reads=[r_cur], writes=[r_sq, r_stat[b]])
        P.op("act", lambda e, b=b, n=n: e.activation(
            out=stat[b][:n, 1:2], in_=stat[b][:n, 0:1], func=AF.Ln, bias=epst[:n, :], scale=1.0 / D),
             reads=[r_stat[b], r_eps], writes=[r_stat[b]])
        P.op("act", lambda e, b=b, n=n: e.activation(
            out=stat[b][:n, 2:3], in_=stat[b][:n, 1:2], func=AF.Exp, scale=-0.5),
             reads=[r_stat[b]], writes=[r_stat[b]])
        if has_normmod:
            P.op("dve", lambda e, b=b, n=n, cur=cur: e.tensor_scalar(
                out=xb[b][:n, :], in0=cur[:n, :], scalar1=stat[b][:n, 2:3], scalar2=None, op0=ALU.mult),
                 reads=[r_cur, r_stat[b]], writes=[r_xb[b]])

    def S3(it):
        is_ctx, t0, n = tiles[it]
        b = it % NB
        v = 1 if is_ctx else 0
        cur, r_cur = cur_of(it)
        if has_normmod:
            for j in range(8):
                k_, jj = j // 4, j % 4
                P.op("pe", lambda e, b=b, j=j, n=n, k_=k_, jj=jj: e.transpose(
                    out=pstn2[k_][:, jj * 128:jj * 128 + n], in_=xb[b][:n, j * 128:(j + 1) * 128],
                    identity=ident[:n, :n]),
                     reads=[r_xb[b], r_ident], writes=[r_pstn2[k_]], inc=(jj == 3))
            for j in range(8):
                k_, jj = j // 4, j % 4
                if k_ == 0:
                    P.op("act", lambda e, b=b, j=j, n=n, v=v, jj=jj: e.activation(
                        out=hTt[b][:, j, :n], in_=pstn2[0][:, jj * 128:jj * 128 + n], func=AF.Identity,
                        bias=mcol[:, j, v:v + 1], scale=acol[:, j, v:v + 1]),
                         reads=[r_pstn2[0], r_mcol, r_acol], writes=[r_hTt[b]], inc=(jj == 3))
                else:
                    P.op("dve", lambda e, b=b, j=j, n=n, v=v, jj=jj: e.tensor_scalar(
                        out=hTt[b][:, j, :n], in0=pstn2[1][:, jj * 128:jj * 128 + n],
                        scalar1=acol[:, j, v:v + 1], scalar2=mcol[:, j, v:v + 1], op0=ALU.mult, op1=ALU.add),
                         reads=[r_pstn2[1], r_mcol, r_acol], writes=[r_hTt[b]], inc=(jj == 3))
            c0 = (ntok + t0) if is_ctx else t0
            P.dma("pool", hT_d[:, :, c0:c0 + n].rearrange("k p t -> p k t"), hTt[b][:, :, :n],
                  reads=[r_hTt[b]], writes=[r_out])
        if has_final:
            P.op("dve", lambda e, b=b, n=n, cur=cur: e.scalar_tensor_tensor(
                out=ot[b][:n, :], in0=cur[:n, :], scalar=stat[b][:n, 2:3], in1=fgb[:n, :],
                op0=ALU.mult, op1=ALU.mult), reads=[r_cur, r_stat[b], r_fgb], writes=[r_ot[b]])
            P.dma("pool", xo_d[t0:t0 + n, :], ot[b][:n, :], reads=[r_ot[b]], writes=[r_out])

    NT = len(tiles)
    S1(0)
    for it in range(NT):
        if it + 1 < NT:
            S1(it + 1)
        S2(it)
        if it >= 1:
            S3(it - 1)
    S3(NT - 1)
    P.wait("sp", [r_out])
    P.emit()
    return nc, P


def _col(v):
    v = np.asarray(v, np.float32)
    return np.ascontiguousarray(v.reshape(-1, 128).T)


def _run(nc, in_maps):
    res = run_bass_kernel_spmd(nc, in_maps, core_ids=list(range(NCORES)))
    return res.results


def build_hgrn(nheads=4, nlat=SEQ, nctx=CTX, upto=3):
    nc = bass.Bass("TRN2", target_bir_lowering=False)
    P = Prog(nc)
    NCH = nlat // 128
    NCC = nctx // 128
    NBLK = nlat // 512

    def din(name, shape, dt):
        return nc.dram_tensor(name, list(shape), dt, kind="ExternalInput").ap()

    hl_d = din("hl", [8, 128, nlat], BF16)
    hc_d = din("hc", [8, 128, nctx], BF16)
    w_d = din("w", [128, 8, nheads, 640], F32)
    lg_d = din("lg", [128, 2, 2, nheads], F32)
    id_d = din("ident", [128, 128], BF16)
    mk_d = din("masks", [128, 2, 128], F32)
    y_d = nc.dram_tensor("y", [nlat, nheads * 128], BF16, kind="ExternalOutput").ap()

    ident = P.sb("ident", [128, 128], BF16); r_ident = Reg()
    masks = P.sb("masks", [128, 2, 128], F32); r_masks = Reg()
    lg = P.sb("lg", [128, 2, 2, nheads], F32); r_lg = Reg()
    lb = P.sb("lb", [128, 2, nheads], F32)
    oml = P.sb("oml", [128, 2, nheads], F32)
    noml = P.sb("noml", [128, 2, nheads], F32)
    r_lb = Reg()
    ones = P.sb("ones", [128, 512], BF16); r_ones = Reg()
    epst = P.sb("epst", [128, 1], F32); r_eps = Reg()
    wst = [P.sb("wst%d" % i, [128, 640], F32) for i in range(2)]; r_wst = [Reg(), Reg()]
    wbf = P.sb("wbf", [128, 8, 640], BF16); r_wbf = Reg()
    hblk = [P.sb("hblk%d" % i, [128, 8, 512], BF16) for i in range(2)]
    r_hblk = [Reg() for _ in range(2)]
    QF = P.sb("QF", [128, nlat], BF16); KF = P.sb("KF", [128, nlat + nctx], BF16)
    QB = P.sb("QB", [128, nlat], BF16); KB = P.sb("KB", [128, nlat + nctx], BF16)
    V = P.sb("V", [128, NCH + NCC, 128], BF16)
    ZG = P.sb("ZG", [128, NCH, 128], BF16)
    SB = P.sb("SB", [128, NCH, 128], BF16)
    SF = P.sb("SF", [128, NCH, 128], BF16)
    es = [P.sb("es%d" % d, [128, NCH + NCC, 3], F32) for d in range(2)]
    r_QF = [Reg() for _ in range(NBLK)]; r_QB = [Reg() for _ in range(NBLK)]
    r_KF = [Reg() for _ in range(NBLK + 1)]; r_KB = [Reg() for _ in range(NBLK + 1)]
    r_V = [Reg() for _ in range(NBLK + 1)]; r_ZG = [Reg() for _ in range(NBLK)]
    r_es = [[Reg() for _ in range(NBLK + 1)] for _ in range(2)]
    r_SB = [Reg() for _ in range(NCH)]
    r_SF = [Reg() for _ in range(NCH)]
    qf = [P.sb("qf%d" % i, [128, 512], F32) for i in range(2)]; r_qf = [Reg(), Reg()]
    sg = [[P.sb("sg%d_%d" % (d, i), [128, 512], F32) for i in range(2)] for d in range(2)]
    gg = [[P.sb("gg%d_%d" % (d, i), [128, 512], F32) for i in range(2)] for d in range(2)]
    kk = [[P.sb("kk%d_%d" % (d, i), [128, 512], F32) for i in range(2)] for d in range(2)]
    r_sg = [[Reg(), Reg()] for _ in range(2)]; r_gg = [[Reg(), Reg()] for _ in range(2)]
    r_kk = [[Reg(), Reg()] for _ in range(2)]
    Bc = [[P.sb("Bc%d_%d" % (d, i), [128, 513], F32) for i in range(2)] for d in range(2)]
    r_Bc = [[Reg(), Reg()] for _ in range(2)]
    Rt = [[P.sb("Rt%d_%d" % (d, i), [128, 4, 2], F32) for i in range(2)] for d in range(2)]
    r_Rt = [[Reg(), Reg()] for _ in range(2)]
    dd = [[P.sb("dd%d_%d" % (d, i), [128, 4, 3], F32) for i in range(2)] for d in range(2)]
    r_dd = [[Reg(), Reg()] for _ in range(2)]
    Sd = [P.sb("S%d" % i, [128, 128], F32) for i in range(2)]; r_Sd = [Reg(), Reg()]
    kh = [[P.sb("kh%d_%d" % (d, i), [128, 128], BF16) for i in range(2)] for d in range(2)]
    r_kh = [[Reg(), Reg()] for _ in range(2)]
    kT = [[P.sb("kT%d_%d" % (d, i), [128, 128], BF16) for i in range(2)] for d in range(2)]
    r_kT = [[Reg(), Reg()] for _ in range(2)]
    Am = [[P.sb("Am%d_%d" % (d, i), [128, 128], BF16) for i in range(2)] for d in range(2)]
    r_Am = [[Reg(), Reg()] for _ in range(2)]
    sq = P.sb("sq", [128, 128], F32); r_sq = Reg()
    st = [P.sb("st%d" % i, [128, 3, 4], F32) for i in range(2)]; r_st = [Reg(), Reg()]
    y4 = [P.sb("y4_%d" % i, [128, 4, 128], BF16) for i in range(2)]; r_y4 = [Reg(), Reg()]
    r_out = Reg()
    r_ser = [Reg(), Reg()]

    pb = [P.ps("pb%d" % i, [128, 512], F32) for i in range(8)]; r_pb = [Reg() for _ in range(8)]
    pt = [pb[6][:, :].bitcast(BF16), pb[7][:, :].bitcast(BF16)]; r_pt = r_pb[6:8]
    psc2 = [pb[0:3], pb[3:6]]; r_psc2 = [r_pb[0:3], r_pb[3:6]]
    ptm2 = [pb[6][:, :].rearrange("p (j c) -> p j c", c=256), pb[7][:, :].rearrange("p (j c) -> p j c", c=256)]
    r_ptm2 = r_pb[6:8]
    def o_tile(g, j):
        bk = 4 + 2 * (g % 2) + (j % 2)
        return pb[bk][:, (j // 2) * 128:(j // 2 + 1) * 128], r_pb[bk]

    P.dma("sp", ident[:], id_d, writes=[r_ident])
    P.dma("sp", masks[:], mk_d, writes=[r_masks])
    P.dma("sp", lg[:], lg_d, writes=[r_lg])
    P.op("dve", lambda e: e.memset(ones[:], 1.0), writes=[r_ones])
    P.op("dve", lambda e: e.memset(epst[:], EPS), writes=[r_eps])
    P.op("dve", lambda e: e.tensor_tensor(out=lb[:], in0=lg[:, 0, :, :], in1=lg[:, 1, :, :], op=ALU.subtract),
         reads=[r_lg], writes=[r_lb])
    P.op("act", lambda e: e.activation(out=lb[:], in_=lb[:], func=AF.Sigmoid), reads=[r_lb], writes=[r_lb])
    P.op("dve", lambda e: e.tensor_scalar(out=oml[:], in0=lb[:], scalar1=-1.0, scalar2=1.0, op0=ALU.mult, op1=ALU.add),
         reads=[r_lb], writes=[r_lb])
    P.op("dve", lambda e: e.tensor_scalar(out=noml[:], in0=lb[:], scalar1=-1.0, scalar2=None, op0=ALU.add),
         reads=[r_lb], writes=[r_lb])

    def load_w(h):
        for kc in range(8):
            P.dma("sp", wst[kc % 2][:], w_d[:, kc, h, :], writes=[r_wst[kc % 2]])
            eng = ("dve", "act")[kc % 2]
            P.op(eng, (lambda e, kc=kc: e.tensor_copy(out=wbf[:, kc, :], in_=wst[kc % 2][:])) if eng == "dve" else
                 (lambda e, kc=kc: e.copy(out=wbf[:, kc, :], in_=wst[kc % 2][:])),
                 reads=[r_wst[kc % 2]], writes=[r_wbf])

    blk_i = [0]

    def mk_block(h, is_ctx, t0, n):
        bb = blk_i[0] % 2
        blk_i[0] += 1
        return dict(h=h, is_ctx=is_ctx, t0=t0, n=n, nch=n // 128, bb=bb,
                    bi=NBLK if is_ctx else t0 // 512, c0=NCH if is_ctx else t0 // 128,
                    kcol0=nlat if is_ctx else t0, first=is_ctx or t0 == 0)

    def stage_A1(c):
        n, bb, is_ctx = c["n"], c["bb"], c["is_ctx"]
        psc, r_psc = psc2[bb], r_psc2[bb]
        src = hc_d if is_ctx else hl_d
        P.dma("sp", hblk[bb][:, :, :n], src[:, :, c["t0"]:c["t0"] + n].rearrange("k p t -> p k t"), writes=[r_hblk[bb]])
        for s_ in ((1, 2) if is_ctx else (0, 1, 2)):
            for kc in range(8):
                P.op("pe", lambda e, s_=s_, kc=kc: e.matmul(psc[s_][:, :n], lhsT=wbf[:, kc, s_ * 128:(s_ + 1) * 128],
                                                            rhs=hblk[bb][:, kc, :n], start=(kc == 0), stop=(kc == 7)),
                     reads=[r_wbf, r_hblk[bb]], writes=[r_psc[s_]], inc=(kc == 7))
        if not is_ctx:
            P.op("act", lambda e: e.activation(out=qf[bb][:, :n], in_=psc[0][:, :n], func=AF.Silu),
                 reads=[r_psc[0]], writes=[r_qf[bb]])

    def stage_Tmm(c):
        n, bb, is_ctx, nch = c["n"], c["bb"], c["is_ctx"], c["nch"]
        ncol = 128 if is_ctx else 256
        for cp in range(nch // 2):
            for j in range(2):
                cc_ = cp * 2 + j
                for kc in range(8):
                    P.op("pe", lambda e, j=j, cc_=cc_, kc=kc, cp=cp: e.matmul(
                        ptm2[cp][:, j, :ncol], lhsT=hblk[bb][:, kc, cc_ * 128:(cc_ + 1) * 128],
                        rhs=wbf[:, kc, 384:384 + ncol], start=(kc == 0), stop=(kc == 7)),
                         reads=[r_wbf, r_hblk[bb]], writes=[r_ptm2[cp]], inc=(kc == 7 and j == 1))

    def stage_Tev(c):
        is_ctx, nch, bi, c0 = c["is_ctx"], c["nch"], c["bi"], c["c0"]
        for cp in range(nch // 2):
            cc = c0 + cp * 2
            P.op("act", lambda e, cc=cc, cp=cp: e.copy(out=V[:, cc:cc + 2, :], in_=ptm2[cp][:, :, 0:128]),
                 reads=[r_ptm2[cp]], writes=[r_V[bi]])
            if not is_ctx:
                P.op("act", lambda e, cc=cc, cp=cp: e.activation(out=ZG[:, cc:cc + 2, :], in_=ptm2[cp][:, :, 128:256],
                                                                 func=AF.Silu),
                     reads=[r_ptm2[cp]], writes=[r_ZG[bi]])

    def stage_A2(c):
        n, bb, is_ctx, nch, h = c["n"], c["bb"], c["is_ctx"], c["nch"], c["h"]
        psc, r_psc = psc2[bb], r_psc2[bb]
        for d in range(2):
            P.op("act", lambda e, d=d: e.activation(out=sg[d][bb][:, :n], in_=psc[1 + d][:, :n], func=AF.Sigmoid),
                 reads=[r_psc[1 + d]], writes=[r_sg[d][bb]])
        for d in range(2):
            P.op("act", lambda e, d=d: e.activation(out=gg[d][bb][:, :n], in_=sg[d][bb][:, :n], func=AF.Ln,
                                                    bias=lb[:, d, h:h + 1], scale=oml[:, d, h:h + 1]),
                 reads=[r_sg[d][bb], r_lb], writes=[r_gg[d][bb]])
            P.op("dve", lambda e, d=d: e.tensor_scalar(out=kk[d][bb][:, :n], in0=sg[d][bb][:, :n],
                                                       scalar1=noml[:, d, h:h + 1], scalar2=oml[:, d, h:h + 1],
                                                       op0=ALU.mult, op1=ALU.add),
                 reads=[r_sg[d][bb], r_lb], writes=[r_kk[d][bb]])
        for d in range(2):
            cur, prv = Bc[d][bb], Bc[d][1 - bb]
            if c["first"]:
                P.op("dve", lambda e, cur=cur: e.memset(cur[:, 0:1], 0.0), writes=[r_Bc[d][bb]])
                P.op("dve", lambda e, cur=cur, d=d: e.tensor_tensor_scan(
                    out=cur[:, 1:1 + n], data0=ones[:, :n], data1=gg[d][bb][:, :n], initial=0.0,
                    op0=ALU.mult, op1=ALU.add), reads=[r_ones, r_gg[d][bb]], writes=[r_Bc[d][bb]])
            else:
                P.op("dve", lambda e, cur=cur, prv=prv: e.tensor_copy(out=cur[:, 0:1], in_=prv[:, 512:513]),
                     reads=[r_Bc[d][1 - bb]], writes=[r_Bc[d][bb]])
                P.op("dve", lambda e, cur=cur, prv=prv, d=d: e.tensor_tensor_scan(
                    out=cur[:, 1:1 + n], data0=ones[:, :n], data1=gg[d][bb][:, :n], initial=prv[:, 512:513],
                    op0=ALU.mult, op1=ALU.add), reads=[r_ones, r_gg[d][bb], r_Bc[d][1 - bb]], writes=[r_Bc[d][bb]])
            lo = cur[:, 0:n].rearrange("p (c t) -> p c t", t=128)
            hi = cur[:, 1:1 + n].rearrange("p (c t) -> p c t", t=128)
            P.op("dve", lambda e, lo=lo, d=d: e.tensor_copy(out=Rt[d][bb][:, :nch, 0], in_=lo[:, :, 64]),
                 reads=[r_Bc[d][bb]], writes=[r_Rt[d][bb]])
            P.op("dve", lambda e, lo=lo, hi=hi, d=d: e.tensor_tensor(out=dd[d][bb][:, :nch, 0], in0=hi[:, :, 127],
                                                                     in1=lo[:, :, 0], op=ALU.subtract),
                 reads=[r_Bc[d][bb]], writes=[r_dd[d][bb]])
            P.op("dve", lambda e, lo=lo, hi=hi, d=d: e.tensor_tensor(out=dd[d][bb][:, :nch, 1], in0=hi[:, :, 127],
                                                                     in1=lo[:, :, 64], op=ALU.subtract),
                 reads=[r_Bc[d][bb]], writes=[r_dd[d][bb]])
            P.op("dve", lambda e, lo=lo, d=d: e.tensor_tensor(out=dd[d][bb][:, :nch, 2], in0=lo[:, :, 64],
                                                              in1=lo[:, :, 0], op=ALU.subtract),
                 reads=[r_Bc[d][bb]], writes=[r_dd[d][bb]])
            bsl = cur[:, 1:1 + n] if d == 0 else cur[:, 0:n]
            P.op("dve", lambda e, d=d, bsl=bsl: e.tensor_tensor(
                out=gg[d][bb][:, :n].rearrange("p (c t) -> p c t", t=128), in0=bsl.rearrange("p (c t) -> p c t", t=128),
                in1=Rt[d][bb][:, :nch, 0:1].broadcast_to([128, nch, 128]), op=ALU.subtract),
                 reads=[r_Bc[d][bb], r_Rt[d][bb]], writes=[r_gg[d][bb]])

    def stage_B(c):
        n, bb, is_ctx, nch, bi, c0, t0, kcol0 = (c["n"], c["bb"], c["is_ctx"], c["nch"], c["bi"], c["c0"], c["t0"],
                                                 c["kcol0"])
        for d in range(2):
            P.op("act", lambda e, d=d: e.activation(out=es[d][:, c0:c0 + nch, :], in_=dd[d][bb][:, :nch, :], func=AF.Exp),
                 reads=[r_dd[d][bb]], writes=[r_es[d][bi]])
            qs, ks = (1.0, -1.0) if d == 0 else (-1.0, 1.0)
            P.op("act", lambda e, d=d, ks=ks: e.activation(out=sg[d][bb][:, :n], in_=gg[d][bb][:, :n], func=AF.Exp, scale=ks),
                 reads=[r_gg[d][bb]], writes=[r_sg[d][bb]])
            if not is_ctx:
                P.op("act", lambda e, d=d, qs=qs: e.activation(out=gg[d][bb][:, :n], in_=gg[d][bb][:, :n], func=AF.Exp,
                                                               scale=qs),
                     reads=[r_gg[d][bb]], writes=[r_gg[d][bb]])
        for d in range(2):
            Kd, r_Kd = (KF, r_KF) if d == 0 else (KB, r_KB)
            P.op("dve", lambda e, d=d, Kd=Kd: e.tensor_tensor(out=Kd[:, kcol0:kcol0 + n], in0=kk[d][bb][:, :n],
                                                               in1=sg[d][bb][:, :n], op=ALU.mult),
                 reads=[r_kk[d][bb], r_sg[d][bb]], writes=[r_Kd[bi]])
            if not is_ctx:
                Qd, r_Qd = (QF, r_QF) if d == 0 else (QB, r_QB)
                P.op("dve", lambda e, d=d, Qd=Qd: e.tensor_tensor(out=Qd[:, t0:t0 + n], in0=qf[bb][:, :n],
                                                                in1=gg[d][bb][:, :n], op=ALU.mult),
                     reads=[r_qf[bb], r_gg[d][bb]], writes=[r_Qd[bi]])

    def pass1(h):
        blocks = [mk_block(h, True, 0, nctx)] + [mk_block(h, False, b_ * 512, 512) for b_ in range(NBLK)]
        prev = None
        for c in blocks:
            if prev is not None:
                stage_Tev(prev)
            stage_A1(c)
            stage_Tmm(c)
            stage_A2(c)
            if prev is not None:
                stage_B(prev)
            prev = c
        stage_Tev(prev)
        stage_B(prev)

    pt4 = [[pb[6][:, :].bitcast(BF16), pb[4][:, :].bitcast(BF16)], [pb[7][:, :].bitcast(BF16), pb[5][:, :].bitcast(BF16)]]
    r_pt4 = [[r_pb[6], r_pb[4]], [r_pb[7], r_pb[5]]]
    pP = [[pb[0], pb[1]], [pb[2], pb[3]]]
    r_pP = [[r_pb[0], r_pb[1]], [r_pb[2], r_pb[3]]]

    def step_info(d, item, last):
        kind, c = item
        if kind == "lat":
            return dict(d=d, lat=True, c=c, ck=c, kcol=c * 128, bi=c // 4, upd=not last)
        return dict(d=d, lat=False, c=c, ck=NCH + c, kcol=nlat + c * 128, bi=NBLK, upd=True)

    def pre_a(st_, sl):
        if not st_["upd"]:
            return
        d, ck, kcol, bi = st_["d"], st_["ck"], st_["kcol"], st_["bi"]
        Kd, r_Kd = (KF, r_KF) if d == 0 else (KB, r_KB)
        i2_ = 1 if d == 0 else 2
        P.op("act", lambda e: e.activation(out=kh[d][sl][:], in_=Kd[:, kcol:kcol + 128], func=AF.Identity,
                                           scale=es[d][:, ck, i2_:i2_ + 1]),
             reads=[r_Kd[bi], r_es[d][bi]], writes=[r_kh[d][sl]])
        P.op("pe", lambda e: e.transpose(out=pt4[d][sl][:, 0:128], in_=kh[d][sl][:], identity=ident[:]),
             reads=[r_kh[d][sl], r_ident], writes=[r_pt4[d][sl]])

    def pre_b(st_, sl):
        if not st_["upd"]:
            return
        d, ck, bi = st_["d"], st_["ck"], st_["bi"]
        P.op("dve", lambda e: e.tensor_copy(out=kT[d][sl][:], in_=pt4[d][sl][:, 0:128]),
             reads=[r_pt4[d][sl]], writes=[r_kT[d][sl]])
        P.op("pe", lambda e: e.matmul(pP[d][sl][:, 0:128], lhsT=kT[d][sl][:], rhs=V[:, ck, :], start=True, stop=True),
             reads=[r_kT[d][sl], r_V[bi]], writes=[r_pP[d][sl]])

    def chain(st_, sl):
        d, ck, bi, c = st_["d"], st_["ck"], st_["bi"], st_["c"]
        if st_["lat"]:
            dst, r_dst, i3 = (SF, r_SF, 2) if d == 0 else (SB, r_SB, 1)
            if d == 0:
                P.op("act", lambda e: e.activation(out=dst[:, c, :], in_=Sd[d][:], func=AF.Identity,
                                                   scale=es[d][:, c, i3:i3 + 1]),
                     reads=[r_Sd[d], r_es[d][bi]], writes=[r_dst[c]])
            else:
                P.op("dve", lambda e: e.tensor_scalar(out=dst[:, c, :], in0=Sd[d][:], scalar1=es[d][:, c, i3:i3 + 1],
                                                      scalar2=None, op0=ALU.mult),
                     reads=[r_Sd[d], r_es[d][bi]], writes=[r_dst[c]])
        if st_["upd"]:
            P.op("dve", lambda e: e.scalar_tensor_tensor(out=Sd[d][:], in0=Sd[d][:], scalar=es[d][:, ck, 0:1],
                                                         in1=pP[d][sl][:, 0:128], op0=ALU.mult, op1=ALU.add),
                 reads=[r_pP[d][sl], r_es[d][bi], r_Sd[d]], writes=[r_Sd[d]])

    for h in range(nheads):
        load_w(h)
        pass1(h)
        if upto < 2:
            continue
        for d in range(2):
            P.op("dve", lambda e, d=d: e.memset(Sd[d][:], 0.0), writes=[r_Sd[d]])
        seq_f = [("ctx", c) for c in range(NCC)] + [("lat", n) for n in range(NCH)]
        seq_b = [("ctx", c) for c in range(NCC - 1, -1, -1)] + [("lat", n) for n in range(NCH - 1, -1, -1)]
        ns = len(seq_f)
        steps = [[step_info(0, seq_f[i], i == ns - 1) for i in range(ns)],
                 [step_info(1, seq_b[i], i == ns - 1) for i in range(ns)]]
        for d in range(2):
            pre_a(steps[d][0], 0)
        for d in range(2):
            pre_b(steps[d][0], 0)
        for i in range(ns):
            if i + 1 < ns:
                for d in range(2):
                    pre_a(steps[d][i + 1], (i + 1) % 2)
            for d in range(2):
                chain(steps[d][i], i % 2)
            if i + 1 < ns:
                for d in range(2):
                    pre_b(steps[d][i + 1], (i + 1) % 2)
        if upto < 3:
            continue
        def stage_X(n):
                bi = n // 4
                g, j = n // 4, n % 4
                gb = g % 2
                sl = n % 2
                csl = slice(n * 128, (n + 1) * 128)
                c0_, c1_, c2_ = n * 128, n * 128 + 64, (n + 1) * 128
                bf_, bb_ = sl * 2, sl * 2 + 1
                P.op("pe", lambda e, c0_=c0_, c1_=c1_, c2_=c2_, bf_=bf_: e.matmul(
                    pb[bf_][0:64, 0:128], lhsT=KF[:, c0_:c1_], rhs=QF[:, c0_:c2_], start=True, stop=True),
                     reads=[r_KF[bi], r_QF[bi]], writes=[r_pb[bf_]], inc=False)
                P.op("pe", lambda e, c0_=c0_, c1_=c1_, c2_=c2_, bf_=bf_: e.matmul(
                    pb[bf_][64:128, 64:128], lhsT=KF[:, c1_:c2_], rhs=QF[:, c1_:c2_], start=True, stop=True),
                     reads=[r_KF[bi], r_QF[bi]], writes=[r_pb[bf_]])
                P.op("pe", lambda e, c0_=c0_, c1_=c1_, c2_=c2_, bb_=bb_: e.matmul(
                    pb[bb_][64:128, 0:128], lhsT=KB[:, c1_:c2_], rhs=QB[:, c0_:c2_], start=True, stop=True),
                     reads=[r_KB[bi], r_QB[bi]], writes=[r_pb[bb_]], inc=False)
                P.op("pe", lambda e, c0_=c0_, c1_=c1_, c2_=c2_, bb_=bb_: e.matmul(
                    pb[bb_][0:64, 0:64], lhsT=KB[:, c0_:c1_], rhs=QB[:, c0_:c1_], start=True, stop=True),
                     reads=[r_KB[bi], r_QB[bi]], writes=[r_pb[bb_]])
                for d, bk in ((0, bf_), (1, bb_)):
                    P.op("dve", lambda e, d=d, sl=sl, bk=bk: e.tensor_tensor(out=Am[d][sl][:], in0=pb[bk][:, 0:128],
                                                                             in1=masks[:, d, :], op=ALU.mult),
                         reads=[r_pb[bk], r_masks], writes=[r_Am[d][sl]])

        def stage_Y(n):
                bi = n // 4
                g, j = n // 4, n % 4
                gb = g % 2
                sl = n % 2
                csl = slice(n * 128, (n + 1) * 128)
                c0_, c1_, c2_ = n * 128, n * 128 + 64, (n + 1) * 128
                bf_, bb_ = sl * 2, sl * 2 + 1
                ot, r_ot = o_tile(g, j)
                P.op("pe", lambda e, sl=sl, n=n, ot=ot: e.matmul(ot, lhsT=Am[0][sl][:], rhs=V[:, n, :], start=True, stop=False),
                     reads=[r_Am[0][sl], r_V[bi]], writes=[r_ot], inc=False)
                P.op("pe", lambda e, sl=sl, n=n, ot=ot: e.matmul(ot, lhsT=Am[1][sl][:], rhs=V[:, n, :], start=False, stop=False),
                     reads=[r_Am[1][sl]], writes=[r_ot], inc=False)
                P.op("pe", lambda e, csl=csl, n=n, ot=ot: e.matmul(ot, lhsT=QF[:, csl], rhs=SF[:, n, :], start=False, stop=False),
                     reads=[r_QF[bi], r_SF[n]], writes=[r_ot], inc=False)
                P.op("pe", lambda e, csl=csl, n=n, ot=ot: e.matmul(ot, lhsT=QB[:, csl], rhs=SB[:, n, :], start=False, stop=True),
                     reads=[r_QB[bi], r_SB[n]], writes=[r_ot])
                P.op("act", lambda e, gb=gb, j=j, ot=ot: e.activation(out=sq[:], in_=ot, func=AF.Square,
                                                                      accum_out=st[gb][:, 0, j:j + 1]),
                     reads=[r_ot], writes=[r_sq, r_st[gb]])
                if j == 3:
                    P.op("act", lambda e, gb=gb: e.activation(out=st[gb][:, 1, :], in_=st[gb][:, 0, :], func=AF.Ln,
                                                              bias=epst[:], scale=1.0 / 128),
                         reads=[r_st[gb], r_eps], writes=[r_st[gb]])
                    P.op("act", lambda e, gb=gb: e.activation(out=st[gb][:, 2, :], in_=st[gb][:, 1, :], func=AF.Exp,
                                                              scale=-0.5),
                         reads=[r_st[gb]], writes=[r_st[gb]])
                    for jj in range(4):
                        nn = g * 4 + jj
                        otj, r_otj = o_tile(g, jj)
                        P.op("dve", lambda e, gb=gb, jj=jj, nn=nn, otj=otj: e.scalar_tensor_tensor(
                            out=y4[gb][:, jj, :], in0=otj, scalar=st[gb][:, 2, jj:jj + 1],
                            in1=ZG[:, nn, :], op0=ALU.mult, op1=ALU.mult),
                             reads=[r_otj, r_st[gb], r_ZG[bi]], writes=[r_y4[gb]])
                    P.dma("pool", y_d[g * 512:(g + 1) * 512, h * 128:(h + 1) * 128].rearrange("(j p) v -> p j v", p=128),
                          y4[gb][:], reads=[r_y4[gb]], writes=[r_out])

        stage_X(0)
        for n in range(NCH):
            if n + 1 < NCH:
                stage_X(n + 1)
            stage_Y(n)
    P.wait("sp", [r_out])
    P.emit()
    return nc, P


def _consts():
    ident = np.eye(128, dtype=NPBF)
    s = np.arange(128)[:, None]
    t = np.arange(128)[None, :]
    masks = np.stack([(s <= t), (s >= t)], axis=1).astype(np.float32)
    return ident, np.ascontiguousarray(masks)


def hgrn_maps(inp, hl, hc):
    ident, masks = _consts()
    w_in = np.asarray(inp["hg_w_in"][0])
    lgt = np.asarray(inp["hg_lb_logits"])
    maps = []
    for core in range(NCORES):
        b, hg = core // 4, core % 4
        w5 = w_in.reshape(8, 128, 5, 16, 128)[:, :, :, hg * 4:(hg + 1) * 4, :]
        w = np.ascontiguousarray(w5.transpose(1, 0, 3, 2, 4).reshape(128, 8, 4, 640))
        lg = lgt.reshape(2, 2, 16, 128)[:, :, hg * 4:(hg + 1) * 4, :]
        lg = np.ascontiguousarray(lg.transpose(3, 0, 1, 2))
        hlT = np.ascontiguousarray(np.asarray(hl[b]).reshape(SEQ, 8, 128).transpose(1, 2, 0))
        hcT = np.ascontiguousarray(np.asarray(hc[b]).reshape(CTX, 8, 128).transpose(1, 2, 0))
        maps.append(dict(hl=hlT, hc=hcT, w=w, lg=lg, ident=ident, masks=masks))
    return maps


def build_fourier():
    nc = bass.Bass("TRN2", target_bir_lowering=False)
    P = Prog(nc)

    def din(name, shape, dt):
        return nc.dram_tensor(name, list(shape), dt, kind="ExternalInput").ap()

    hp_d = din("hp", [8, 128, 64, 128], BF16)
    wu_d = din("wu", [128, 8, 512], F32)
    wz_d = din("wz", [128, 8, 512], F32)
    cs_d = din("cs", [128, 2, 512], BF16)
    gt_d = din("gt", [64, 128, 512], BF16)
    fb_d = din("fb", [128, 2, 128], BF16)
    id_d = din("ident", [128, 128], BF16)
    y_d = nc.dram_tensor("yg", [SEQ, 512], BF16, kind="ExternalOutput").ap()

    ident = P.sb("ident", [128, 128], BF16); r_ident = Reg()
    cs = P.sb("cs", [128, 2, 512], BF16); r_cs = Reg()
    fb = P.sb("fb", [128, 2, 128], BF16); r_fb = Reg()
    stg = [P.sb("stg%d" % i, [128, 512], F32) for i in range(2)]; r_stg = [Reg(), Reg()]
    wubk = [P.sb("wubk%d" % i, [128, 512], BF16) for i in range(2)]; r_wubk = [Reg(), Reg()]
    WuT = P.sb("WuT", [128, 4, 1024], BF16); r_WuT = Reg()
    Wp = P.sb("Wp", [128, 8, 1024], BF16); r_Wp = Reg()
    wzb = P.sb("wzb", [128, 8, 512], BF16); r_wzb = Reg()
    hblk = [P.sb("hblk%d" % i, [128, 8, 2, 128], BF16) for i in range(2)]; r_hblk = [Reg(), Reg()]
    Zsb = [P.sb("Zsb%d" % i, [128, 1024], BF16) for i in range(2)]; r_Zsb = [Reg(), Reg()]
    gtab = [P.sb("gtab%d" % i, [128, 512], BF16) for i in range(2)]; r_gtab = [Reg(), Reg()]
    Abuf = P.sb("Abuf", [128, 4, 2, 128, 64], BF16)
    r_Ab = [Reg() for _ in range(64)]
    ATs = [P.sb("ATs%d" % i, [128, 2, 4, 128], BF16) for i in range(2)]; r_ATs = [Reg(), Reg()]
    zs = [P.sb("zs%d" % i, [128, 512], F32) for i in range(2)]; r_zs = [Reg(), Reg()]
    ygt = [P.sb("ygt%d" % i, [128, 512], BF16) for i in range(2)]; r_ygt = [Reg(), Reg()]
    r_out = Reg()

    pb = [P.ps("pb%d" % i, [128, 512], F32) for i in range(6)]; r_pb = [Reg() for _ in range(6)]
    pt = [P.ps("pt%d" % i, [128, 1024], BF16) for i in range(2)]; r_pt = [Reg(), Reg()]

    P.dma("sp", ident[:], id_d, writes=[r_ident])
    P.dma("sp", cs[:], cs_d, writes=[r_cs])
    P.dma("sp", fb[:], fb_d, writes=[r_fb])

    for kc in range(8):
        s = kc % 2
        P.dma("sp", stg[s][:], wu_d[:, kc, :], writes=[r_stg[s]])
        P.op("dve", lambda e, s=s: e.tensor_copy(out=wubk[s][:], in_=stg[s][:]), reads=[r_stg[s]], writes=[r_wubk[s]])
        for jb in range(4):
            P.op("pe", lambda e, s=s, jb=jb: e.transpose(out=pt[s][:, jb * 128:(jb + 1) * 128],
                                                         in_=wubk[s][:, jb * 128:(jb + 1) * 128], identity=ident[:]),
                 reads=[r_wubk[s], r_ident], writes=[r_pt[s]], inc=(jb == 3))
        P.op("act", lambda e, s=s, kc=kc: e.copy(out=WuT[:, :, kc * 128:(kc + 1) * 128],
                                                 in_=pt[s][:, 0:512].rearrange("p (j k) -> p j k", k=128)),
             reads=[r_pt[s]], writes=[r_WuT])
    for kc in range(8):
        s = kc % 2
        P.dma("sp", stg[s][:], wz_d[:, kc, :], writes=[r_stg[s]])
        P.op("act", lambda e, s=s, kc=kc: e.copy(out=wzb[:, kc, :], in_=stg[s][:]),
             reads=[r_stg[s]], writes=[r_wzb])
    i = 0
    for g in range(2):
        for kc in range(8):
            bk = i % 2
            i += 1
            for jc in range(2):
                P.op("pe", lambda e, g=g, kc=kc, jc=jc, bk=bk: e.matmul(
                    pb[bk][:], lhsT=WuT[:, g * 2 + jc, kc * 128:(kc + 1) * 128], rhs=cs[:, jc, :],
                    start=(jc == 0), stop=(jc == 1)), reads=[r_WuT, r_cs], writes=[r_pb[bk]], inc=(jc == 1))
            outv = Wp[:, kc, :].rearrange("p (c g m) -> p c g m", c=2, g=2)[:, :, g, :]
            inv = pb[bk][:, :].rearrange("p (c m) -> p c m", c=2)
            if bk == 0:
                P.op("act", lambda e, outv=outv, inv=inv: e.copy(out=outv, in_=inv), reads=[r_pb[bk]], writes=[r_Wp])
            else:
                P.op("dve", lambda e, outv=outv, inv=inv: e.tensor_copy(out=outv, in_=inv), reads=[r_pb[bk]], writes=[r_Wp])

    def s1a(b):
        bp, bj = b // 2, b % 2
        hb = bp % 2
        zb = b % 2
        zbank = (0, 1) if b % 2 == 0 else (4, 5)
        if bj == 0:
            P.dma("sp", hblk[hb][:], hp_d[:, :, 2 * bp:2 * bp + 2, :].rearrange("k p b a -> p k b a"), writes=[r_hblk[hb]])
        P.dma("sp", gtab[zb][:], gt_d[b], writes=[r_gtab[zb]])
        for half in range(2):
            bk = zbank[half]
            for kc in range(8):
                P.op("pe", lambda e, hb=hb, bj=bj, half=half, kc=kc, bk=bk: e.matmul(
                    pb[bk][:], lhsT=hblk[hb][:, kc, bj, :], rhs=Wp[:, kc, half * 512:(half + 1) * 512],
                    start=(kc == 0), stop=(kc == 7)), reads=[r_hblk[hb], r_Wp], writes=[r_pb[bk]], inc=(kc == 7))
        P.op("act", lambda e, zb=zb, bk=zbank[0]: e.copy(out=Zsb[zb][:, 0:512], in_=pb[bk][:]),
             reads=[r_pb[zbank[0]]], writes=[r_Zsb[zb]])
        P.op("dve", lambda e, zb=zb, bk=zbank[1]: e.tensor_copy(out=Zsb[zb][:, 512:1024], in_=pb[bk][:]),
             reads=[r_pb[zbank[1]]], writes=[r_Zsb[zb]])

    def s1b(b):
        zb = b % 2
        for mp in range(2):
            bank = 2 + mp
            for mj in range(2):
                mb = mp * 2 + mj
                P.op("pe", lambda e, zb=zb, mb=mb, mj=mj, bank=bank: e.matmul(
                    pb[bank][:, mj * 256:(mj + 1) * 256], lhsT=Zsb[zb][:, mb * 128:(mb + 1) * 128],
                    rhs=gtab[zb][:, 0:256], start=True, stop=False),
                     reads=[r_Zsb[zb], r_gtab[zb]], writes=[r_pb[bank]], inc=False)
                P.op("pe", lambda e, zb=zb, mb=mb, mj=mj, bank=bank: e.matmul(
                    pb[bank][:, mj * 256:(mj + 1) * 256], lhsT=Zsb[zb][:, 512 + mb * 128:512 + (mb + 1) * 128],
                    rhs=gtab[zb][:, 256:512], start=False, stop=True),
                     reads=[r_Zsb[zb], r_gtab[zb]], writes=[r_pb[bank]], inc=(mj == 1))
            outv = Abuf[:, mp * 2:mp * 2 + 2, :, :, b]
            inv = pb[bank][:, :].rearrange("p (mb ri k) -> p mb ri k", mb=2, ri=2)
            if mp == 0:
                P.op("act", lambda e, outv=outv, inv=inv: e.copy(out=outv, in_=inv), reads=[r_pb[bank]], writes=[r_Ab[b]])
            else:
                P.op("dve", lambda e, outv=outv, inv=inv: e.tensor_copy(out=outv, in_=inv),
                     reads=[r_pb[bank]], writes=[r_Ab[b]])

    s1a(0)
    for b in range(64):
        if b + 1 < 64:
            s1a(b + 1)
        s1b(b)

    yv = y_d.rearrange("(k2 r) c -> r k2 c", r=128)

    def s3a(pr):
        s_ = pr % 2
        prm, par = pr % 32, pr // 32
        zbk = s_
        P.dma("sp", hblk[s_][:], hp_d[:, :, 2 * prm:2 * prm + 2, :].rearrange("k p b a -> p k b a"), writes=[r_hblk[s_]])
        for kc in range(8):
            lhs = hblk[s_][:, kc, :, :].rearrange("p b (k2 two) -> p (b k2) two", two=2)[:, :, par]
            P.op("pe", lambda e, kc=kc, lhs=lhs, zbk=zbk: e.matmul(pb[zbk][:], lhsT=lhs, rhs=wzb[:, kc, :],
                                                                   start=(kc == 0), stop=(kc == 7)),
                 reads=[r_hblk[s_], r_wzb], writes=[r_pb[zbk]], inc=(kc == 7))
        P.op("act", lambda e, s_=s_, zbk=zbk: e.activation(out=zs[s_][:], in_=pb[zbk][:], func=AF.Silu),
             reads=[r_pb[zbk]], writes=[r_zs[s_]])
        for ri in range(2):
            for mb in range(4):
                src = Abuf[:, mb, ri, 2 * pr:2 * pr + 2, :].rearrange("p k b -> p (k b)")
                P.op("pe", lambda e, s_=s_, ri=ri, mb=mb, src=src: e.transpose(
                    out=pt[s_][:, (ri * 4 + mb) * 128:(ri * 4 + mb + 1) * 128], in_=src, identity=ident[:]),
                     reads=r_Ab + [r_ident] if (pr == 0 and ri == 0 and mb == 0) else [r_ident], writes=[r_pt[s_]],
                     inc=(ri == 1 and mb == 3))
        P.op("dve", lambda e, s_=s_: e.tensor_copy(out=ATs[s_][:].rearrange("p r m c -> p (r m c)"), in_=pt[s_][:]),
             reads=[r_pt[s_]], writes=[r_ATs[s_]])

    def s3b(pr):
        s_ = pr % 2
        ybk = 2 + s_
        for ri in range(2):
            P.op("pe", lambda e, s_=s_, ri=ri, ybk=ybk: e.matmul(pb[ybk][:], lhsT=fb[:, ri, :],
                                                                 rhs=ATs[s_][:, ri, :, :].rearrange("p m c -> p (m c)"),
                                                                 start=(ri == 0), stop=(ri == 1)),
                 reads=[r_fb, r_ATs[s_]], writes=[r_pb[ybk]], inc=(ri == 1))
        P.op("dve", lambda e, s_=s_, ybk=ybk: e.tensor_tensor(out=ygt[s_][:], in0=pb[ybk][:], in1=zs[s_][:], op=ALU.mult),
             reads=[r_pb[ybk], r_zs[s_]], writes=[r_ygt[s_]])
        for kap in range(2):
            P.dma("pool", yv[2 * pr + kap], ygt[s_][kap * 64:(kap + 1) * 64, :], reads=[r_ygt[s_]], writes=[r_out])

    s3a(0)
    for pr in range(64):
        if pr + 1 < 64:
            s3a(pr + 1)
        s3b(pr)
    P.wait("sp", [r_out])
    P.emit()
    return nc, P


def fourier_tables():
    N = SEQ
    j = np.arange(256)[:, None]; m = np.arange(256)[None, :]
    ang = 2 * np.pi * (j * m % 256) / 256
    C = np.cos(ang) / 16.0; S = np.sin(ang) / 16.0
    cs = np.concatenate([C, S], axis=1).reshape(2, 128, 512).transpose(1, 0, 2)
    a = np.arange(128)[None, :, None]; b = np.arange(64)[:, None, None]; k1 = np.arange(128)[None, None, :]
    th = 2 * np.pi * ((k1 * (64 * a + b)) % N) / N
    Gr = np.cos(th) / np.sqrt(128.0); Gi = -np.sin(th) / np.sqrt(128.0)
    gt = np.concatenate([Gr, Gi, Gi, -Gr], axis=2)
    bb = np.arange(64)[:, None]; k2 = np.arange(64)[None, :]
    ph = 2 * np.pi * ((bb * k2) % 64) / 64
    Fc = np.cos(ph) / 8.0; Fs = np.sin(ph) / 8.0
    fb = np.zeros((128, 2, 128))
    for kap in range(2):
        fb[kap * 64:(kap + 1) * 64, 0, kap * 64:(kap + 1) * 64] = Fc
        fb[kap * 64:(kap + 1) * 64, 1, kap * 64:(kap + 1) * 64] = Fs
    return (np.ascontiguousarray(cs).astype(NPBF), np.ascontiguousarray(gt).astype(NPBF),
            np.ascontiguousarray(fb).astype(NPBF))


def fourier_maps(inp, h1):
    ident, _ = _consts()
    cs, gt, fb = fourier_tables()
    w_in = np.asarray(inp["ft_w_in"][0])
    maps = []
    for core in range(NCORES):
        b, gp = core // 4, core % 4
        wu = np.ascontiguousarray(w_in[:, gp * 512:(gp + 1) * 512].reshape(8, 128, 512).transpose(1, 0, 2))
        wz = np.ascontiguousarray(w_in[:, E + gp * 512:E + (gp + 1) * 512].reshape(8, 128, 512).transpose(1, 0, 2))
        hp = np.ascontiguousarray(np.asarray(h1[b]).reshape(128, 64, 8, 128).transpose(2, 3, 1, 0))
        maps.append(dict(hp=hp, wu=wu, wz=wz, cs=cs, gt=gt, fb=fb, ident=ident))
    return maps


_CACHE = {}


def _prog(key, builder):
    if key not in _CACHE:
        _CACHE[key] = builder()[0]
    return _CACHE[key]


def _ada_maps(inp, layer):
    aw = np.asarray(inp["ada_w"][layer], np.float32)
    ab = np.asarray(inp["ada_b"][layer], np.float32)
    return aw, ab


def kernel(x, c, ctx, c_ctx, ada_w, ada_b, norm_g, hg_w_in, hg_lb_logits, hg_norm_g,
           hg_w_out, ft_w_in, ft_w_out, final_g):
    inp = dict(x=np.asarray(x, np.float32), c=np.asarray(c, np.float32), ctx=np.asarray(ctx, np.float32),
               c_ctx=np.asarray(c_ctx, np.float32), ada_w=np.asarray(ada_w, np.float32),
               ada_b=np.asarray(ada_b, np.float32), norm_g=np.asarray(norm_g, np.float32),
               hg_w_in=np.asarray(hg_w_in, np.float32), hg_lb_logits=np.asarray(hg_lb_logits, np.float32),
               hg_norm_g=np.asarray(hg_norm_g, np.float32), hg_w_out=np.asarray(hg_w_out, np.float32),
               ft_w_in=np.asarray(ft_w_in, np.float32), ft_w_out=np.asarray(ft_w_out, np.float32),
               final_g=np.asarray(final_g, np.float32))
    ident, _ = _consts()
    TS = SEQ // 4
    CS_ = CTX // 4

    def awm(layer):
        aw, ab = _ada_maps(inp, layer)
        return (np.ascontiguousarray(aw[:, :2 * D].reshape(8, 128, 2 * D).transpose(1, 0, 2)), _col(ab[:2 * D]))

    def awg(layer):
        aw, ab = _ada_maps(inp, layer)
        return (np.ascontiguousarray(aw[:, 2 * D:].reshape(8, 128, D).transpose(1, 0, 2)),
                np.ascontiguousarray(ab[None, 2 * D:]))

    nc = _prog("A1", lambda: build_tok(TS, CS_, False, True, False))
    aw_m0, ab_m0 = awm(0)
    maps = []
    for core in range(NCORES):
        b, seg = core // 4, core % 4
        cv = np.stack([_col(inp["c"][b]), _col(inp["c_ctx"])], axis=-1)
        maps.append(dict(x=np.ascontiguousarray(inp["x"][b, seg * TS:(seg + 1) * TS]),
                         xc=np.ascontiguousarray(inp["ctx"][b, seg * CS_:(seg + 1) * CS_]),
                         cvec=np.ascontiguousarray(cv), ident=ident, aw_m=aw_m0, ab_m=ab_m0,
                         ng=_col(inp["norm_g"][0])))
    res = _run(nc, maps)
    hT = [np.asarray(r["hT"]) for r in res]

    nc = _prog("A2", build_hgrn)
    _, masks = _consts()
    w_in = inp["hg_w_in"][0]
    lgt = inp["hg_lb_logits"]
    maps = []
    for core in range(NCORES):
        b, hg = core // 4, core % 4
        w5 = w_in.reshape(8, 128, 5, 16, 128)[:, :, :, hg * 4:(hg + 1) * 4, :]
        w = np.ascontiguousarray(w5.transpose(1, 0, 3, 2, 4).reshape(128, 8, 4, 640))
        lg = np.ascontiguousarray(lgt.reshape(2, 2, 16, 128)[:, :, hg * 4:(hg + 1) * 4, :].transpose(3, 0, 1, 2))
        hl = np.ascontiguousarray(np.concatenate([hT[b * 4 + s][:, :, :TS] for s in range(4)], axis=2))
        hc = np.ascontiguousarray(np.concatenate([hT[b * 4 + s][:, :, TS:] for s in range(4)], axis=2))
        maps.append(dict(hl=hl, hc=hc, w=w, lg=lg, ident=ident, masks=masks))
    res = _run(nc, maps)
    y0 = [np.asarray(r["y"]) for r in res]

    nc = _prog("B", lambda: build_tok(TS, 0, True, True, False))
    aw_g0, ab_g0 = awg(0)
    aw_m1, ab_m1 = awm(1)
    maps = []
    for core in range(NCORES):
        b, seg = core // 4, core % 4
        y = np.ascontiguousarray(np.concatenate([y0[b * 4 + g][seg * TS:(seg + 1) * TS] for g in range(4)], axis=1))
        maps.append(dict(x=np.ascontiguousarray(inp["x"][b, seg * TS:(seg + 1) * TS]), y=y,
                         w_out=np.ascontiguousarray(inp["hg_w_out"][0]),
                         wsc=np.ascontiguousarray(inp["hg_norm_g"][0].reshape(128, 1)),
                         cvec=np.ascontiguousarray(_col(inp["c"][b])[:, :, None]), ident=ident,
                         aw_g=aw_g0, ab_g=ab_g0, aw_m=aw_m1, ab_m=ab_m1, ng=_col(inp["norm_g"][1])))
    res = _run(nc, maps)
    x1 = [np.asarray(r["xo"]) for r in res]
    h1T = [np.asarray(r["hT"]) for r in res]

    nc = _prog("C", build_fourier)
    cs, gt, fb = fourier_tables()
    w_in = inp["ft_w_in"][0]
    maps = []
    for core in range(NCORES):
        b, gp = core // 4, core % 4
        wu = np.ascontiguousarray(w_in[:, gp * 512:(gp + 1) * 512].reshape(8, 128, 512).transpose(1, 0, 2))
        wz = np.ascontiguousarray(w_in[:, E + gp * 512:E + (gp + 1) * 512].reshape(8, 128, 512).transpose(1, 0, 2))
        hfull = np.concatenate([h1T[b * 4 + s] for s in range(4)], axis=2)
        hp = np.ascontiguousarray(hfull.reshape(8, 128, 128, 64).transpose(0, 1, 3, 2))
        maps.append(dict(hp=hp, wu=wu, wz=wz, cs=cs, gt=gt, fb=fb, ident=ident))
    res = _run(nc, maps)
    y1 = [np.asarray(r["yg"]) for r in res]

    nc = _prog("D", lambda: build_tok(TS, 0, True, False, True))
    aw_g1, ab_g1 = awg(1)
    maps = []
    for core in range(NCORES):
        b, seg = core // 4, core % 4
        y = np.ascontiguousarray(np.concatenate([y1[b * 4 + g][seg * TS:(seg + 1) * TS] for g in range(4)], axis=1))
        maps.append(dict(x=x1[core], y=y, w_out=np.ascontiguousarray(inp["ft_w_out"][0]),
                         wsc=np.ones((128, 1), np.float32),
                         cvec=np.ascontiguousarray(_col(inp["c"][b])[:, :, None]), ident=ident,
                         aw_g=aw_g1, ab_g=ab_g1, fg=np.ascontiguousarray(inp["final_g"][None, :])))
    res = _run(nc, maps)
    out = np.stack([np.concatenate([np.asarray(res[b * 4 + s]["xo"]) for s in range(4)], axis=0) for b in range(2)])
    return out.astype(np.float32)
```
